# Optimizing a Trainium2 kernel written in Bass

```python
import math, functools
import jax, jax.numpy as jnp
from jax import lax
import numpy as np

D_MODEL = 1024
BATCH = 2
SEQ = 8192
DEPTH = 2

GRID_W = 64
CTX_LEN = 256
DA_HEADS = 4
DA_HEAD_DIM = 64
DA_VDIM = 2 * DA_HEAD_DIM
DA_QK = DA_HEADS * 2 * DA_HEAD_DIM
MLA_HEADS = 8
MLA_NOPE = 64
MLA_ROPE = 32
MLA_V = 64
MLA_Q_RANK = 384
MLA_KV_RANK = 256
HGRN_HEADS = 4
HGRN_DK = 128
HGRN_DV = 128
HGRN_KW = HGRN_HEADS * HGRN_DK
HGRN_VW = HGRN_HEADS * HGRN_DV
SSD_HEADS = 8
SSD_HEAD_DIM = 64
SSD_GROUPS = 2
SSD_STATE = 128
SSD_CONV_W = 5
SSD_INNER = SSD_HEADS * SSD_HEAD_DIM
SSD_BC = SSD_GROUPS * SSD_STATE
SSD_CONV_CH = SSD_INNER + 2 * SSD_BC
MLP_HIDDEN = 4 * D_MODEL
Q_BLOCK = 128
SCAN_CHUNK = 64
ROPE_BASE = 10000.0
NORM_EPS = 1e-6

ATT_SPLITS = (DA_QK, DA_QK, DA_HEADS * DA_VDIM, MLA_Q_RANK, MLA_KV_RANK, MLA_ROPE)
ATT_IN = sum(ATT_SPLITS)
ATT_OUT = DA_HEADS * DA_VDIM + MLA_HEADS * MLA_V
REC_SPLITS = (HGRN_KW, HGRN_KW, HGRN_KW, HGRN_VW, HGRN_VW, SSD_INNER, SSD_CONV_CH, 2 * SSD_HEADS)
REC_IN = sum(REC_SPLITS)
REC_OUT = HGRN_VW + SSD_INNER

kernel_name = 'hybrid_diffattn_mla_hgrn2_ssd_block'


def split_sizes(x, sizes):
    return jnp.split(x, np.cumsum(sizes)[:-1].tolist(), axis=-1)


def rms_norm(x, g):
    xf = x.astype(jnp.float32)
    y = xf * lax.rsqrt(jnp.mean(xf * xf, axis=-1, keepdims=True) + NORM_EPS)
    return (y * g.astype(jnp.float32)).astype(x.dtype)


def modulate(x, g, shift, scale):
    return rms_norm(x, g) * (1 + scale) + shift


def sq_relu_mlp(u, w1, w2):
    return jnp.square(jax.nn.relu(u @ w1)) @ w2


def axial_rope(length, rot_dim):
    rows = length // GRID_W
    row = jnp.repeat(jnp.arange(rows, dtype=jnp.float32), GRID_W)
    col = jnp.tile(jnp.arange(GRID_W, dtype=jnp.float32), rows)
    n_freq = rot_dim // 4
    inv_freq = ROPE_BASE ** (-jnp.arange(n_freq, dtype=jnp.float32) / n_freq)
    ang = jnp.concatenate([row[:, None] * inv_freq, col[:, None] * inv_freq], axis=-1)
    return jnp.cos(ang), jnp.sin(ang)


def apply_rope(x, rope):
    cos, sin = (t.astype(x.dtype) for t in rope)
    half = x.shape[-1] // 2
    x1, x2 = x[..., :half], x[..., half:]
    return jnp.concatenate([x1 * cos - x2 * sin, x1 * sin + x2 * cos], axis=-1)


def block_attention(q, k, v, scale):
    b, m, h, s, dk = q.shape
    qb = q.reshape(b, m, h, s // Q_BLOCK, Q_BLOCK, dk).transpose(3, 0, 1, 2, 4, 5)

    def one_block(qi):
        sc = jnp.einsum('bmhqd,bmhkd->bmhqk', qi, k).astype(jnp.float32) * scale
        p = jax.nn.softmax(sc, axis=-1).astype(v.dtype)
        return jnp.einsum('bmhqk,bhkv->bmhqv', p, v)

    out = lax.map(one_block, qb)
    return out.transpose(1, 2, 3, 0, 4, 5).reshape(b, m, h, s, v.shape[-1])


def att_project(u, w_in, q_norm, w_uq, kv_norm, w_ukv, rope_da, rope_mla):
    b, s, _ = u.shape
    qa, ka, va, cq, ckv, kr = split_sizes(u @ w_in, ATT_SPLITS)
    qa = qa.reshape(b, s, DA_HEADS, 2, DA_HEAD_DIM).transpose(0, 3, 2, 1, 4)
    ka = ka.reshape(b, s, DA_HEADS, 2, DA_HEAD_DIM).transpose(0, 3, 2, 1, 4)
    va = va.reshape(b, s, DA_HEADS, DA_VDIM).transpose(0, 2, 1, 3)
    qm = (rms_norm(cq, q_norm) @ w_uq).reshape(b, s, MLA_HEADS, MLA_NOPE + MLA_ROPE).transpose(0, 2, 1, 3)
    kvm = (rms_norm(ckv, kv_norm) @ w_ukv).reshape(b, s, MLA_HEADS, MLA_NOPE + MLA_V).transpose(0, 2, 1, 3)
    qn, qr = qm[..., :MLA_NOPE], qm[..., MLA_NOPE:]
    kn, vm = kvm[..., :MLA_NOPE], kvm[..., MLA_NOPE:]
    kr = kr[:, None]
    if rope_da is not None:
        qa, ka = apply_rope(qa, rope_da), apply_rope(ka, rope_da)
        qr, kr = apply_rope(qr, rope_mla), apply_rope(kr, rope_mla)
    qm = jnp.concatenate([qn, qr], axis=-1)[:, None]
    km = jnp.concatenate([kn, jnp.broadcast_to(kr, kn.shape[:-1] + (MLA_ROPE,))], axis=-1)[:, None]
    return qa, ka, va, qm, km, vm


def attention_mixer(u_lat, u_ctx, layer, w_in, lam_p, subnorm, q_norm, w_uq, kv_norm, w_ukv, w_out,
                    rope_da, rope_mla, need_ctx):
    lam_init = 0.8 - 0.6 * math.exp(-0.3 * layer)
    lp = lam_p.astype(jnp.float32)
    lam = jnp.exp(jnp.sum(lp[0] * lp[1])) - jnp.exp(jnp.sum(lp[2] * lp[3])) + lam_init
    qa, ka, va, qm, km, vm = att_project(u_lat, w_in, q_norm, w_uq, kv_norm, w_ukv, rope_da, rope_mla)
    qa_c, ka_c, va_c, qm_c, km_c, vm_c = att_project(u_ctx, w_in, q_norm, w_uq, kv_norm, w_ukv, None, None)
    da_scale = DA_HEAD_DIM ** -0.5
    mla_scale = (MLA_NOPE + MLA_ROPE) ** -0.5

    def merge(oa, om):
        b, _, _, s, _ = oa.shape
        ya = rms_norm(oa[:, 0] - lam.astype(oa.dtype) * oa[:, 1], subnorm) * (1 - lam_init)
        ya = ya.transpose(0, 2, 1, 3).reshape(b, s, DA_HEADS * DA_VDIM)
        ym = om[:, 0].transpose(0, 2, 1, 3).reshape(b, s, MLA_HEADS * MLA_V)
        return jnp.concatenate([ya, ym], axis=-1) @ w_out

    oa = block_attention(qa, jnp.concatenate([ka_c, ka], axis=3), jnp.concatenate([va_c, va], axis=2), da_scale)
    om = block_attention(qm, jnp.concatenate([km_c, km], axis=3), jnp.concatenate([vm_c, vm], axis=2), mla_scale)
    y_lat = merge(oa, om)
    y_ctx = None
    if need_ctx:
        y_ctx = merge(block_attention(qa_c, ka_c, va_c, da_scale), block_attention(qm_c, km_c, vm_c, mla_scale))
    return y_lat, y_ctx


def hgrn2_scan(q, k, v, logf, s0):
    dtype = q.dtype
    q, k, v, logf, s0 = (t.astype(jnp.float32) for t in (q, k, v, logf, s0))
    b, L, H, K = q.shape
    V = v.shape[-1]
    nc = L // SCAN_CHUNK

    def chunks(t):
        return t.reshape(b, nc, SCAN_CHUNK, H, t.shape[-1]).transpose(1, 0, 3, 2, 4)

    mask = jnp.tril(jnp.ones((SCAN_CHUNK, SCAN_CHUNK), dtype=bool))[:, :, None]

    def step(S, inp):
        qi, ki, vi, gi = inp
        bc = jnp.cumsum(gi, axis=2)
        rel = bc[:, :, :, None, :] - bc[:, :, None, :, :]
        decay = jnp.exp(jnp.where(mask, rel, -jnp.inf))
        att = jnp.einsum('bhtk,bhtsk,bhsk->bhts', qi, decay, ki)
        o = jnp.einsum('bhts,bhsv->bhtv', att, vi) + jnp.einsum('bhtk,bhkv->bhtv', qi * jnp.exp(bc), S)
        btot = bc[:, :, -1]
        S = jnp.exp(btot)[..., None] * S + jnp.einsum('bhsk,bhsv->bhkv', ki * jnp.exp(btot[:, :, None] - bc), vi)
        return S, o

    S, o = lax.scan(step, s0, (chunks(q), chunks(k), chunks(v), chunks(logf)))
    o = o.transpose(1, 0, 3, 2, 4).reshape(b, L, H, V)
    return S.astype(dtype), o.astype(dtype)


def ssd_scan(x, dt, bm, cm, h0, a_neg):
    dtype = x.dtype
    x, dt, bm, cm, h0, a_neg = (t.astype(jnp.float32) for t in (x, dt, bm, cm, h0, a_neg))
    b, L, H, P = x.shape
    G, N = bm.shape[2], bm.shape[3]
    R = H // G
    nc = L // SCAN_CHUNK
    T = SCAN_CHUNK
    xdt = (x * dt[..., None]).reshape(b, nc, T, G, R, P)
    acum = jnp.cumsum((dt * a_neg).reshape(b, nc, T, G, R).transpose(0, 3, 4, 1, 2), axis=-1)
    bm = bm.reshape(b, nc, T, G, N)
    cm = cm.reshape(b, nc, T, G, N)
    mask = jnp.tril(jnp.ones((T, T), dtype=bool))
    decay = jnp.exp(jnp.where(mask, acum[..., :, None] - acum[..., None, :], -jnp.inf))
    cb = jnp.einsum('bctgn,bcsgn->bgcts', cm, bm)
    y_diag = jnp.einsum('bgrcts,bcsgrp->bctgrp', cb[:, :, None] * decay, xdt)
    w_end = jnp.exp(acum[..., -1:] - acum)
    st = jnp.einsum('bcsgn,bgrcs,bcsgrp->cbgrnp', bm, w_end, xdt)
    a_tot = jnp.moveaxis(jnp.exp(acum[..., -1]), -1, 0)

    def step(h, inp):
        s_c, a_c = inp
        return a_c[..., None, None] * h + s_c, h

    h_fin, h_in = lax.scan(step, h0, (st, a_tot))
    y_off = jnp.einsum('bctgn,cbgrnp,bgrct->bctgrp', cm, h_in, jnp.exp(acum))
    y = (y_diag + y_off).reshape(b, L, H, P)
    return h_fin.astype(dtype), y.astype(dtype)


def bidir(scan_f, scan_b, args_f, args_b, h0_f, h0_b):
    flip = lambda t: jnp.flip(t, axis=1)
    hf, yf = scan_f(*args_f, h0_f)
    hb, yb = scan_b(*[flip(t) for t in args_b], h0_b)
    return yf + flip(yb), hf, hb


def centred_dwconv(x, w, bias):
    pad = SSD_CONV_W // 2
    y = lax.conv_general_dilated(x, w[:, None, :].astype(x.dtype), window_strides=(1,), padding=[(pad, pad)],
                                 dimension_numbers=('NWC', 'WIO', 'NWC'), feature_group_count=x.shape[-1])
    return y + bias


def recurrent_mixer(u_lat, u_ctx, layer, w_in, bound_logits, out_norm, conv_w, conv_b, a_log, dt_bias, skip,
                    ssd_g, w_out, need_ctx):
    gamma = jax.nn.softmax(bound_logits.astype(jnp.float32), axis=0)
    lb = (jnp.cumsum(gamma, axis=0) - gamma[0])[layer]
    lb_f = lb[:HGRN_KW].reshape(HGRN_HEADS, HGRN_DK)
    lb_b = lb[HGRN_KW:].reshape(HGRN_HEADS, HGRN_DK)

    def project(u):
        b, s, _ = u.shape
        q, f_f, f_b, i, g, z, xbc, dt = split_sizes(u @ w_in, REC_SPLITS)
        xbc = jax.nn.silu(centred_dwconv(xbc, conv_w, conv_b))
        xs, bm, cm = split_sizes(xbc, (SSD_INNER, SSD_BC, SSD_BC))
        q = jax.nn.silu(q).reshape(b, s, HGRN_HEADS, HGRN_DK)
        i = i.reshape(b, s, HGRN_HEADS, HGRN_DV)

        def gate_args(f_raw, lower):
            f = lower + (1.0 - lower) * jax.nn.sigmoid(f_raw.astype(jnp.float32).reshape(b, s, HGRN_HEADS, HGRN_DK))
            return (q, 1.0 - f, i, jnp.log(f))

        xs = xs.reshape(b, s, SSD_HEADS, SSD_HEAD_DIM)
        bm = bm.reshape(b, s, SSD_GROUPS, SSD_STATE)
        cm = cm.reshape(b, s, SSD_GROUPS, SSD_STATE)
        ssd_f = (xs, jax.nn.softplus(dt[..., :SSD_HEADS] + dt_bias[0]), bm, cm)
        ssd_b = (xs, jax.nn.softplus(dt[..., SSD_HEADS:] + dt_bias[1]), bm, cm)
        return gate_args(f_f, lb_f), gate_args(f_b, lb_b), ssd_f, ssd_b, (g, z, xs)

    def merge(o, y, extras):
        g, z, xs = extras
        b, s = g.shape[0], g.shape[1]
        o = rms_norm(o, out_norm.reshape(HGRN_HEADS, HGRN_DV)).reshape(b, s, HGRN_VW) * jax.nn.silu(g)
        y = (y + skip[:, None] * xs).reshape(b, s, SSD_INNER) * jax.nn.silu(z)
        y = rms_norm(y.reshape(b, s, SSD_GROUPS, SSD_INNER // SSD_GROUPS),
                     ssd_g.reshape(SSD_GROUPS, SSD_INNER // SSD_GROUPS)).reshape(b, s, SSD_INNER)
        return jnp.concatenate([o, y], axis=-1) @ w_out

    ssd_fwd = functools.partial(ssd_scan, a_neg=-jnp.exp(a_log[0]))
    ssd_bwd = functools.partial(ssd_scan, a_neg=-jnp.exp(a_log[1]))
    hc_f, hc_b, sc_f, sc_b, ex_c = project(u_ctx)
    hl_f, hl_b, sl_f, sl_b, ex_l = project(u_lat)
    b = u_ctx.shape[0]
    zh = jnp.zeros((b, HGRN_HEADS, HGRN_DK, HGRN_DV), u_ctx.dtype)
    zs = jnp.zeros((b, SSD_GROUPS, SSD_HEADS // SSD_GROUPS, SSD_STATE, SSD_HEAD_DIM), u_ctx.dtype)
    oc, st_hf, st_hb = bidir(hgrn2_scan, hgrn2_scan, hc_f, hc_b, zh, zh)
    yc, st_sf, st_sb = bidir(ssd_fwd, ssd_bwd, sc_f, sc_b, zs, zs)
    ol, _, _ = bidir(hgrn2_scan, hgrn2_scan, hl_f, hl_b, st_hf, st_hb)
    yl, _, _ = bidir(ssd_fwd, ssd_bwd, sl_f, sl_b, st_sf, st_sb)
    y_lat = merge(ol, yl, ex_l)
    y_ctx = merge(oc, yc, ex_c) if need_ctx else None
    return y_lat, y_ctx


def setup_inputs(seed: int = 0) -> dict:
    key = jax.random.key(seed)
    ks = iter(jax.random.split(key, 48))
    D = D_MODEL
    n_even, n_odd = (DEPTH + 1) // 2, DEPTH // 2

    def nrm(shape, scale):
        return scale * jax.random.normal(next(ks), shape, jnp.float32)

    def gain(shape):
        return 1.0 + nrm(shape, 0.02)

    inp = {}
    inp['x'] = nrm((BATCH, SEQ, D), 1.0)
    inp['c'] = nrm((BATCH, D), 1.0)
    inp['ctx'] = nrm((BATCH, CTX_LEN, D), 1.0)
    inp['c_ctx'] = nrm((D,), 1.0)
    inp['w_mod'] = nrm((DEPTH, D, 6 * D), 0.5 * D ** -0.5)
    inp['b_mod'] = nrm((DEPTH, 6 * D), 0.01)
    inp['norm_mix'] = gain((DEPTH, D))
    inp['norm_mlp'] = gain((DEPTH, D))
    inp['w_mlp_in'] = nrm((DEPTH, D, MLP_HIDDEN), D ** -0.5)
    inp['w_mlp_out'] = nrm((DEPTH, MLP_HIDDEN, D), MLP_HIDDEN ** -0.5)
    inp['att_w_in'] = nrm((n_even, D, ATT_IN), D ** -0.5)
    inp['att_lambda'] = nrm((n_even, 4, DA_HEAD_DIM), 0.1)
    inp['att_subnorm'] = gain((n_even, DA_VDIM))
    inp['mla_q_norm'] = gain((n_even, MLA_Q_RANK))
    inp['mla_w_uq'] = nrm((n_even, MLA_Q_RANK, MLA_HEADS * (MLA_NOPE + MLA_ROPE)), MLA_Q_RANK ** -0.5)
    inp['mla_kv_norm'] = gain((n_even, MLA_KV_RANK))
    inp['mla_w_ukv'] = nrm((n_even, MLA_KV_RANK, MLA_HEADS * (MLA_NOPE + MLA_V)), MLA_KV_RANK ** -0.5)
    inp['att_w_out'] = nrm((n_even, ATT_OUT, D), ATT_OUT ** -0.5)
    inp['rec_w_in'] = nrm((n_odd, D, REC_IN), D ** -0.5)
    inp['hgrn_bound_logits'] = nrm((DEPTH, 2 * HGRN_KW), 0.5)
    inp['hgrn_out_norm'] = gain((n_odd, HGRN_VW))
    inp['ssd_conv_w'] = nrm((n_odd, SSD_CONV_W, SSD_CONV_CH), SSD_CONV_W ** -0.5)
    inp['ssd_conv_b'] = nrm((n_odd, SSD_CONV_CH), 0.01)
    inp['ssd_a_log'] = jnp.log(jax.random.uniform(next(ks), (n_odd, 2, SSD_HEADS), jnp.float32, 1.0, 16.0))
    dt0 = jnp.exp(jax.random.uniform(next(ks), (n_odd, 2, SSD_HEADS), jnp.float32, math.log(1e-3), math.log(1e-1)))
    inp['ssd_dt_bias'] = dt0 + jnp.log(-jnp.expm1(-dt0))
    inp['ssd_skip'] = 1.0 + nrm((n_odd, SSD_HEADS), 0.1)
    inp['ssd_norm'] = gain((n_odd, SSD_INNER))
    inp['rec_w_out'] = nrm((n_odd, REC_OUT, D), REC_OUT ** -0.5)
    inp['final_norm'] = gain((D,))
    return inp


def reference(x, c, ctx, c_ctx, w_mod, b_mod, norm_mix, norm_mlp, w_mlp_in, w_mlp_out,
              att_w_in, att_lambda, att_subnorm, mla_q_norm, mla_w_uq, mla_kv_norm, mla_w_ukv, att_w_out,
              rec_w_in, hgrn_bound_logits, hgrn_out_norm, ssd_conv_w, ssd_conv_b, ssd_a_log, ssd_dt_bias,
              ssd_skip, ssd_norm, rec_w_out, final_norm):
    L = x.shape[1]
    rope_da = axial_rope(L, DA_HEAD_DIM)
    rope_mla = axial_rope(L, MLA_ROPE)
    h_lat, h_ctx = x, ctx
    for l in range(DEPTH):
        last = l == DEPTH - 1
        j = l // 2
        mod_lat = [m[:, None, :] for m in jnp.split(jax.nn.silu(c) @ w_mod[l] + b_mod[l], 6, axis=-1)]
        mod_ctx = jnp.split(jax.nn.silu(c_ctx) @ w_mod[l] + b_mod[l], 6, axis=-1)
        u_lat = modulate(h_lat, norm_mix[l], mod_lat[0], mod_lat[1])
        u_ctx = modulate(h_ctx, norm_mix[l], mod_ctx[0], mod_ctx[1])
        if l % 2 == 0:
            y_lat, y_ctx = attention_mixer(u_lat, u_ctx, l, att_w_in[j], att_lambda[j], att_subnorm[j],
                                           mla_q_norm[j], mla_w_uq[j], mla_kv_norm[j], mla_w_ukv[j], att_w_out[j],
                                           rope_da, rope_mla, not last)
        else:
            y_lat, y_ctx = recurrent_mixer(u_lat, u_ctx, l, rec_w_in[j], hgrn_bound_logits, hgrn_out_norm[j],
                                           ssd_conv_w[j], ssd_conv_b[j], ssd_a_log[j], ssd_dt_bias[j],
                                           ssd_skip[j], ssd_norm[j], rec_w_out[j], not last)
        h_lat = h_lat + mod_lat[2] * y_lat
        h_lat = h_lat + mod_lat[5] * sq_relu_mlp(modulate(h_lat, norm_mlp[l], mod_lat[3], mod_lat[4]),
                                                 w_mlp_in[l], w_mlp_out[l])
        if not last:
            h_ctx = h_ctx + mod_ctx[2] * y_ctx
            h_ctx = h_ctx + mod_ctx[5] * sq_relu_mlp(modulate(h_ctx, norm_mlp[l], mod_ctx[3], mod_ctx[4]),
                                                     w_mlp_in[l], w_mlp_out[l])
    return rms_norm(h_lat, final_norm)
```

```python
from contextlib import ExitStack
import math
import os
import numpy as np
import concourse.bass as bass
import concourse.mybir as mybir
from concourse.bass_utils import run_bass_kernel_spmd

F32 = mybir.dt.float32
BF16 = mybir.dt.bfloat16
AF = mybir.ActivationFunctionType
ALU = mybir.AluOpType
AX = mybir.AxisListType


class Sched:
    def __init__(self, nc, es, n_dma_sems=64):
        self.nc = nc
        self.es = es
        self.eng = {"pe": nc.tensor, "act": nc.scalar, "dve": nc.vector, "pool": nc.gpsimd, "sp": nc.sync}
        self.sem = {}
        self.cnt = {}
        for k in ("pe", "act", "dve", "pool"):
            self.sem[k] = es.enter_context(nc.semaphore("sem_" + k))
            self.cnt[k] = 0
        self.seen = {k: {} for k in self.eng}
        self.semobj = {}
        for k in self.sem:
            self.semobj[id(self.sem[k])] = self.sem[k]
        self.w = {}
        self.r = {}
        self.dma_sem = {}
        self.sem_count = {}
        self.free_dma_sems = [es.enter_context(nc.semaphore("dsem%d" % i)) for i in range(n_dma_sems)]
        for s in self.free_dma_sems:
            self.semobj[id(s)] = s
        self.n_wait = 0
        self.n_inst = 0
        self.excl = set()

    def _deps(self, reads, writes):
        deps = {}
        for k in reads:
            ev = self.w.get(k)
            if ev is not None:
                deps[ev[0]] = max(deps.get(ev[0], 0), ev[1])
            if k in self.excl:
                for sid, v in self.r.get(k, {}).items():
                    deps[sid] = max(deps.get(sid, 0), v)
        for k in writes:
            ev = self.w.get(k)
            if ev is not None:
                deps[ev[0]] = max(deps.get(ev[0], 0), ev[1])
            for sid, v in self.r.get(k, {}).items():
                deps[sid] = max(deps.get(sid, 0), v)
        return deps

    def _wait(self, e, deps):
        for sid, v in deps.items():
            if e == "pe" and sid == id(self.sem["pe"]):
                continue
            if self.seen[e].get(sid, 0) < v:
                self.eng[e].wait_ge(self.semobj[sid], v)
                self.seen[e][sid] = v
                self.n_wait += 1

    def _record(self, ev, reads, writes):
        for k in writes:
            self.w[k] = ev
            self.r[k] = {}
        for k in reads:
            d = self.r.setdefault(k, {})
            d[ev[0]] = max(d.get(ev[0], 0), ev[1])

    def op(self, e, fn, reads=(), writes=()):
        self._wait(e, self._deps(reads, writes))
        inst = fn(self.eng[e])
        self.cnt[e] += 1
        inst.then_inc(self.sem[e], 1)
        self._record((id(self.sem[e]), self.cnt[e]), reads, writes)
        self.n_inst += 1
        return inst

    def group(self, e, fns, reads=(), writes=()):
        self._wait(e, self._deps(reads, writes))
        inst = None
        for fn in fns:
            inst = fn(self.eng[e])
            self.n_inst += 1
        self.cnt[e] += 1
        inst.then_inc(self.sem[e], 1)
        self._record((id(self.sem[e]), self.cnt[e]), reads, writes)
        return inst

    def dma(self, q, out, in_, reads=(), writes=(), semkey=None, **kw):
        if semkey is None:
            semkey = writes[0]
        sem = self.dma_sem.get(semkey)
        if sem is None:
            sem = self.free_dma_sems.pop()
            self.dma_sem[semkey] = sem
        self._wait(q, self._deps(reads, writes))
        inst = self.eng[q].dma_start(out=out, in_=in_, **kw)
        self.sem_count[id(sem)] = self.sem_count.get(id(sem), 0) + 16
        inst.then_inc(sem, 16)
        self._record((id(sem), self.sem_count[id(sem)]), reads, writes)
        self.n_inst += 1
        return inst

    def barrier(self):
        evs = {}
        for k in self.sem:
            if self.cnt[k] > 0:
                evs[id(self.sem[k])] = self.cnt[k]
        for sid, c in self.sem_count.items():
            evs[sid] = c
        for e in ("pe", "act", "dve", "pool", "sp"):
            for sid, v in evs.items():
                if self.seen[e].get(sid, 0) < v:
                    self.eng[e].wait_ge(self.semobj[sid], v)
                    self.seen[e][sid] = v
        for k, sem in self.dma_sem.items():
            self.free_dma_sems.append(sem)
        self.dma_sem = {}

    def release_dma_sem(self, semkey):
        pass

    def wait_all(self, e, keys):
        self._wait(e, self._deps(keys, ()))


D = 1024
NCTX = 256
NLAT = 8192
NKEY = NCTX + NLAT
NOWN = 2048
NQ = NCTX + NOWN
EPS = 1e-6
DA_SCALE = 64 ** -0.5
MLA_SCALE = 96 ** -0.5


class Blob:
    def __init__(self, nc, spec):
        self.spec = spec
        self.off = {}
        o = 0
        for name, shape in spec:
            self.off[name] = (o, shape)
            o += int(np.prod(shape))
        self.total = o
        self.ap = nc.dram_tensor("blob", [o], F32, kind="ExternalInput").ap()

    def get(self, name):
        o, shape = self.off[name]
        n = int(np.prod(shape))
        v = self.ap[o:o + n]
        if len(shape) == 1:
            return v
        if len(shape) == 2:
            return v.rearrange("(a b) -> a b", b=shape[1])
        if len(shape) == 3:
            return v.rearrange("(a b c) -> a b c", b=shape[1], c=shape[2])
        raise ValueError(shape)

    @staticmethod
    def pack(spec, m):
        parts = []
        for name, shape in spec:
            a = np.asarray(m[name], dtype=np.float32)
            assert tuple(a.shape) == tuple(shape), (name, a.shape, shape)
            parts.append(a.ravel())
        return np.concatenate(parts)


SPEC_A = [("xs", (NKEY, D)), ("c2", (128, 16)), ("w_mod", (D, 6 * D)), ("b_mod", (6 * D,)), ("norm_mix", (D,)),
          ("norm_mlp", (D,)), ("w_in", (D, 2208)), ("q_norm", (384,)), ("w_uq", (384, 768)), ("kv_norm", (256,)),
          ("w_ukv", (256, 1024)), ("w_out", (D, D)), ("lam", (256,)), ("subnorm", (128,)), ("w1", (D, 4 * D)),
          ("w2", (4 * D, D)), ("cs_da", (128, 2, NKEY)), ("cs_m", (96, 2, NKEY)), ("cs_kr", (32, 2, NKEY)),
          ("perm_da", (128, 128)), ("perm_m", (96, 96)), ("perm_kr", (32, 32)), ("ident", (128, 128))]


def spec_B(nlat=NLAT):
    return [("hs", (NCTX + nlat, D)), ("c2", (128, 16)), ("w_mod", (D, 2048)), ("b_mod", (2048,)), ("norm_mix", (D,)),
            ("w_rec", (D, NCOLB)), ("bl", (128, 4)), ("out_norm", (128,)), ("cw", (128, 15)), ("cb", (128, 3)),
            ("alog4", (4,)), ("dtb4", (4,)), ("skipb", (128,)), ("masks", (64, 4, 64)), ("mask01", (128, 512)),
            ("ident", (128, 128))]


class Ctx:
    def __init__(self, nc, es):
        self.nc = nc
        self.es = es
        self.S = Sched(nc, es)
        self.uid = 0

    def name(self, base):
        self.uid += 1
        return "%s_%d" % (base, self.uid)

    def sb(self, st, base, shape, dt=F32):
        n = self.name(base)
        t = st.enter_context(self.nc.sbuf_tensor(n, list(shape), dt))
        return t, n

    def ps(self, st, base, shape, dt=F32):
        n = self.name(base)
        t = st.enter_context(self.nc.psum_tensor(n, list(shape), dt))
        self.S.excl.add(n)
        return t, n

    def barrier(self):
        self.S.barrier()


class Ring:
    def __init__(self, items):
        self.items = items
        self.i = 0

    def next(self):
        it = self.items[self.i % len(self.items)]
        self.i += 1
        return it


def load_weight_bf16(cx, dst, dst_key, w_dram, kchunks, ncols, stage_ring, cast_ring, row_scale=None,
                     prows=128, col0=0):
    S = cx.S
    CH = 2048
    keys = []
    for kc in range(kchunks):
        dkey = "%s_k%d" % (dst_key, kc)
        keys.append(dkey)
        for c0 in range(0, ncols, CH):
            n = min(CH, ncols - c0)
            stg, skey = stage_ring.next()
            S.dma("sp", stg[:prows, 0:n], w_dram[kc * prows:(kc + 1) * prows, col0 + c0:col0 + c0 + n],
                  writes=[skey])
            e = cast_ring.next()
            if row_scale is None:
                if e == "act":
                    S.op("act", lambda E: E.copy(out=dst[:prows, kc, c0:c0 + n], in_=stg[:prows, 0:n]),
                         reads=[skey], writes=[dkey])
                else:
                    S.op(e, lambda E: E.tensor_copy(out=dst[:prows, kc, c0:c0 + n], in_=stg[:prows, 0:n]),
                         reads=[skey], writes=[dkey])
            else:
                sc, sckey = row_scale
                if e == "act":
                    S.op("act", lambda E: E.activation(out=dst[:prows, kc, c0:c0 + n], in_=stg[:prows, 0:n],
                                                       func=AF.Copy, scale=sc[:prows, kc:kc + 1]),
                         reads=[skey, sckey], writes=[dkey])
                else:
                    S.op(e, lambda E: E.tensor_scalar(out=dst[:prows, kc, c0:c0 + n], in0=stg[:prows, 0:n],
                                                      scalar1=sc[:prows, kc:kc + 1], scalar2=None, op0=ALU.mult),
                         reads=[skey, sckey], writes=[dkey])
    return keys


def emit_mod(cx, st, w_mod, b_mod, c2, need, gates, i0=0):
    S = cx.S
    nc = cx.nc
    modT, k_modT = cx.sb(st, "modT", [128, 48, 2])
    gate = {}
    for i in gates:
        for n in (0, 1):
            gate[(i, n)] = cx.sb(st, "gate%d%d" % (i, n), [128, 1024])
    with ExitStack() as tmp:
        csb, k_csb = cx.sb(tmp, "csb", [128, 8, 2])
        scT, k_scT = cx.sb(tmp, "scT", [128, 8, 2])
        scb, k_scb = cx.sb(tmp, "scb", [128, 2, 8, 128])
        ones, k_ones = cx.sb(tmp, "ones", [1, 128])
        wp = [cx.sb(tmp, "wpiece", [128, 8, 512]) for _ in range(2)]
        br = [cx.sb(tmp, "brow", [1, 512]) for _ in range(2)]
        pss = [cx.ps(tmp, "psm", [128, 512]) for _ in range(2)]
        psg = [cx.ps(tmp, "psg", [128, 512]) for _ in range(2)]
        S.dma("sp", csb[:].rearrange("p k n -> p (k n)"), c2[:, :], writes=[k_csb])
        S.op("act", lambda E: E.activation(out=scT[:], in_=csb[:], func=AF.Silu), reads=[k_csb], writes=[k_scT])
        for n in (0, 1):
            S.op("dve", lambda E: E.tensor_copy(out=scb[:, n], in_=scT[:, :, n:n + 1].broadcast_to([128, 8, 128])),
                 reads=[k_scT], writes=[k_scb])
        S.op("pool", lambda E: E.memset(ones[:], 1.0), writes=[k_ones])
        wv = w_mod.rearrange("(kc p) n -> p kc n", p=128)
        bv = b_mod.rearrange("(o n) -> o n", o=1)
        pi = 0
        ev = Ring(["act", "dve"])
        for i in need:
            for half in (0, 1):
                pc = 2 * i + half
                (wt, k_wt), (bt, k_bt) = wp[pi % 2], br[pi % 2]
                pi += 1
                pcs = pc - 2 * i0
                S.dma("sp", wt[:], wv[:, :, pcs * 512:(pcs + 1) * 512], writes=[k_wt])
                S.dma("sp", bt[:], bv[:, pcs * 512:(pcs + 1) * 512], writes=[k_bt])
                for sub in range(4):
                    cc = pc * 4 + sub
                    ps_, k_ps = pss[cc % 2]
                    fns = [lambda E, kc=kc: E.matmul(ps_[:, 0:2], lhsT=wt[:, kc, sub * 128:(sub + 1) * 128],
                                                      rhs=scT[:, kc, :], start=(kc == 0), stop=False)
                           for kc in range(8)]
                    fns.append(lambda E: E.matmul(ps_[:, 0:2], lhsT=bt[0:1, sub * 128:(sub + 1) * 128],
                                                  rhs=ones[0:1, 0:2], start=False, stop=True))
                    S.group("pe", fns, reads=[k_wt, k_bt, k_scT, k_ones], writes=[k_ps])
                    S.op("act", lambda E: E.copy(out=modT[:, cc, :], in_=ps_[:, 0:2]), reads=[k_ps], writes=[k_modT])
                if i in gates:
                    for n in (0, 1):
                        ps_, k_ps = psg[n]
                        fns = [lambda E, kc=kc: E.matmul(ps_[:, :], lhsT=scb[:, n, kc, :], rhs=wt[:, kc, :],
                                                          start=(kc == 0), stop=False) for kc in range(8)]
                        fns.append(lambda E: E.matmul(ps_[:, :], lhsT=ones[0:1, 0:128], rhs=bt[0:1, :],
                                                      start=False, stop=True))
                        S.group("pe", fns, reads=[k_wt, k_bt, k_scb, k_ones], writes=[k_ps])
                        g, k_g = gate[(i, n)]
                        S.op("dve", lambda E: E.tensor_copy(out=g[:, half * 512:(half + 1) * 512], in_=ps_[:, :]),
                             reads=[k_ps], writes=[k_g])
        cx.barrier()
    return (modT, k_modT), gate


def emit_gcols(cx, st, modT, k_modT, norm_dram, i_shift, i_scale, base):
    S = cx.S
    nT, k_nT = cx.sb(st, base + "_nT", [128, 8])
    G, k_G = cx.sb(st, base + "_G", [128, 8, 2])
    S.dma("sp", nT[:], norm_dram.rearrange("(c p) -> p c", p=128), writes=[k_nT], allow_slow_non_contiguous=True)
    S.op("dve", lambda E: E.tensor_scalar_add(out=G[:], in0=modT[:, i_scale * 8:(i_scale + 1) * 8, :], scalar1=1.0),
         reads=[k_modT], writes=[k_G])
    S.op("dve", lambda E: E.tensor_tensor(out=G[:], in0=G[:], in1=nT[:].unsqueeze(2).broadcast_to([128, 8, 2]),
                                          op=ALU.mult), reads=[k_G, k_nT], writes=[k_G])
    shift = modT[:, i_shift * 8:(i_shift + 1) * 8, :]
    return (G, k_G), shift


def emit_norm_T(cx, xt, k_xt, ntile, G, k_G, shift, k_shift, n, uT, k_uT, ident, k_id, wk):
    S = cx.S
    ss, k_ss = wk["ss"]
    rs, k_rs = wk["rs"]
    junk, k_junk = wk["junk"]
    xn, k_xn = wk["xn"]
    for j in range(ntile):
        S.op("act", lambda E: E.activation(out=junk[:], in_=xt[:, j, :], func=AF.Square, accum_out=ss[:, j:j + 1]),
             reads=[k_xt], writes=[k_junk, k_ss])
    S.op("act", lambda E: E.activation(out=rs[:, 0:ntile], in_=ss[:, 0:ntile], func=AF.Sqrt, scale=1.0 / D,
                                       bias=EPS),
         reads=[k_ss], writes=[k_rs])
    S.op("dve", lambda E: E.reciprocal(out=rs[:, 0:ntile], in_=rs[:, 0:ntile]), reads=[k_rs], writes=[k_rs])
    for j in range(ntile):
        S.op("dve", lambda E: E.tensor_scalar(out=xn[:, j, :], in0=xt[:, j, :], scalar1=rs[:, j:j + 1], scalar2=None,
                                              op0=ALU.mult), reads=[k_xt, k_rs], writes=[k_xn + "_%d" % j])
    for j in range(ntile):
        ptr, k_ptr = wk["ptr"].next()
        fns = [lambda E, kc=kc: E.transpose(out=ptr[:, kc, :], in_=xn[:, j, kc * 128:(kc + 1) * 128], identity=ident[:])
               for kc in range(8)]
        S.group("pe", fns, reads=[k_xn + "_%d" % j, k_id], writes=[k_ptr])
        tmp, k_tmp = wk["tmpT"].next()
        S.op("dve", lambda E: E.tensor_tensor(out=tmp[:], in0=ptr[:], in1=G[:, :, n:n + 1].broadcast_to([128, 8, 128]),
                                              op=ALU.mult), reads=[k_ptr, k_G], writes=[k_tmp])
        S.op("pool", lambda E: E.tensor_tensor(out=uT[:, :, j * 128:(j + 1) * 128], in0=tmp[:],
                                               in1=shift[:, :, n:n + 1].broadcast_to([128, 8, 128]), op=ALU.add),
             reads=[k_tmp, k_shift], writes=[k_uT + "_%d" % j])
    return [k_uT + "_%d" % j for j in range(ntile)]


def rope_tables(npos_order, dim):
    half = dim // 2
    nfreq = dim // 4
    inv = (10000.0 ** (-np.arange(nfreq, dtype=np.float32) / nfreq)).astype(np.float32)
    pos = np.asarray(npos_order)
    valid = pos >= 0
    p = np.where(valid, pos, 0)
    row = (p // 64).astype(np.float32)
    col = (p % 64).astype(np.float32)
    ang = np.concatenate([row[:, None] * inv, col[:, None] * inv], axis=-1).astype(np.float32)
    cos = np.cos(ang).astype(np.float32)
    sin = np.sin(ang).astype(np.float32)
    cos = np.where(valid[:, None], cos, 1.0).astype(np.float32)
    sin = np.where(valid[:, None], sin, 0.0).astype(np.float32)
    C = np.concatenate([cos, cos], axis=1).T
    Ssig = np.concatenate([-sin, sin], axis=1).T
    return np.ascontiguousarray(C), np.ascontiguousarray(Ssig)


def perm_matrix(nblocks, dim, pad_rows=0):
    n = pad_rows + nblocks * dim
    P = np.zeros((n, n), np.float32)
    half = dim // 2
    for b in range(nblocks):
        for j in range(dim):
            P[pad_rows + b * dim + (j + half) % dim, pad_rows + b * dim + j] = 1.0
    return P


def rope_emit(cx, ps_, k_ps, M, N, perm, k_perm, cs, k_cs, wk, out_ap, k_out, psw_ring):
    S = cx.S
    LV = int(os.environ.get("A1_D", "9"))
    if LV < 2:
        return
    kb, k_kb = wk["kb"].next()
    S.op("act", lambda E: E.copy(out=kb[:M, :N], in_=ps_[:M, :N]), reads=[k_ps], writes=[k_kb])
    psw, k_psw = psw_ring.next()
    S.op("pe", lambda E: E.matmul(psw[:M, :N], lhsT=perm[:M, :M], rhs=kb[:M, :N], start=True, stop=True),
         reads=[k_kb, k_perm], writes=[k_psw])
    if LV < 3:
        return
    t1, k_t1 = wk["t1"].next()
    t2, k_t2 = wk["t2"].next()
    S.op("dve", lambda E: E.tensor_tensor(out=t1[:M, :N], in0=ps_[:M, :N], in1=cs[:M, 0, :N], op=ALU.mult),
         reads=[k_ps, k_cs], writes=[k_t1])
    S.op("dve", lambda E: E.tensor_tensor(out=t2[:M, :N], in0=psw[:M, :N], in1=cs[:M, 1, :N], op=ALU.mult),
         reads=[k_psw, k_cs], writes=[k_t2])
    if LV < 4:
        return
    S.op("pool", lambda E: E.tensor_tensor(out=out_ap, in0=t1[:M, :N], in1=t2[:M, :N], op=ALU.add),
         reads=[k_t1, k_t2], writes=[k_out])


def build_A(debug=False, upto=99, nblk=17, nqb=5, nh_da=4, nh_m=8):
    nc = bass.Bass("TRN2", target_bir_lowering=False)
    dbg_kind = "ExternalOutput" if debug else "Internal"

    blob = Blob(nc, SPEC_A)

    def din(name, shape, dt=F32):
        return blob.get(name)

    def dscr(name, shape, dt=BF16):
        return nc.dram_tensor(name, list(shape), dt, kind=dbg_kind).ap()

    xs = din("xs", [NKEY, D])
    c2 = din("c2", [128, 16])
    w_mod = din("w_mod", [D, 6 * D])
    b_mod = din("b_mod", [6 * D])
    norm_mix = din("norm_mix", [D])
    norm_mlp = din("norm_mlp", [D])
    w_in = din("w_in", [D, 2208])
    q_norm = din("q_norm", [384])
    w_uq = din("w_uq", [384, 768])
    kv_norm = din("kv_norm", [256])
    w_ukv = din("w_ukv", [256, 1024])
    w_out = din("w_out", [D, D])
    lam = din("lam", [256])
    subnorm = din("subnorm", [128])
    w1 = din("w1", [D, 4 * D])
    w2 = din("w2", [4 * D, D])
    cs_da_d = din("cs_da", [128, 2, NKEY])
    cs_m_d = din("cs_m", [96, 2, NKEY])
    cs_kr_d = din("cs_kr", [32, 2, NKEY])
    perm_da_d = din("perm_da", [128, 128])
    perm_m_d = din("perm_m", [96, 96])
    perm_kr_d = din("perm_kr", [32, 32])
    ident_d = din("ident", [128, 128])

    kt_da = dscr("kt_da", [4, 128, NKEY])
    v_da = dscr("v_da", [NKEY, 512])
    kt_m = dscr("kt_m", [8, 96, NKEY])
    v_m = dscr("v_m", [NKEY, 8 * 65])
    qt_da = dscr("qt_da", [4, 128, NQ])
    qt_m = dscr("qt_m", [8, 96, NQ])
    yt_da = dscr("yt_da", [4, 128, NQ])
    yt_m = dscr("yt_m", [8, 64, NQ])
    h_out = nc.dram_tensor("h_out", [NQ, D], F32, kind="ExternalOutput").ap()
    modT_dbg = nc.dram_tensor("modT_dbg", [128, 96], F32, kind="ExternalOutput").ap() if debug else None

    with ExitStack() as es:
        cx = Ctx(nc, es)
        S = cx.S
        (modT, k_modT), gate = emit_mod(cx, es, w_mod, b_mod, c2, need=[0, 1, 2, 3, 4, 5], gates=[2, 5])
        (Gmix, k_Gmix), sh_mix = emit_gcols(cx, es, modT, k_modT, norm_mix, 0, 1, "mix")
        (Gmlp, k_Gmlp), sh_mlp = emit_gcols(cx, es, modT, k_modT, norm_mlp, 3, 4, "mlp")
        if debug:
            S.dma("sp", modT_dbg[:, :], modT[:].rearrange("p c n -> p (c n)"), reads=[k_modT], writes=["modT_dbg"])
        identf, k_identf = cx.sb(es, "identf", [128, 128])
        ident, k_id = cx.sb(es, "ident", [128, 128], BF16)
        S.dma("sp", identf[:], ident_d[:, :], writes=[k_identf])
        S.op("dve", lambda E: E.tensor_copy(out=ident[:], in_=identf[:]), reads=[k_identf], writes=[k_id])
        cx.barrier()
        if upto < 1:
            return nc

        with ExitStack() as st:
            stage_ring = Ring([cx.sb(st, "wstage", [128, 2048]) for _ in range(2)])
            cast_ring = Ring(["act", "dve", "pool"])
            w_in_sb, k_w_in = cx.sb(st, "w_in_sb", [128, 8, 2208], BF16)
            wk_in = load_weight_bf16(cx, w_in_sb, k_w_in, w_in, 8, 2208, stage_ring, cast_ring)
            qnT, k_qnT = cx.sb(st, "qnT", [128, 3])
            kvnT, k_kvnT = cx.sb(st, "kvnT", [128, 2])
            S.dma("sp", qnT[:], q_norm.rearrange("(c p) -> p c", p=128), writes=[k_qnT], allow_slow_non_contiguous=True)
            S.dma("sp", kvnT[:], kv_norm.rearrange("(c p) -> p c", p=128), writes=[k_kvnT], allow_slow_non_contiguous=True)
            w_uq_sb, k_w_uq = cx.sb(st, "w_uq_sb", [128, 3, 768], BF16)
            wk_uq = load_weight_bf16(cx, w_uq_sb, k_w_uq, w_uq, 3, 768, stage_ring, cast_ring, row_scale=(qnT, k_qnT))
            w_ukv_sb, k_w_ukv = cx.sb(st, "w_ukv_sb", [128, 2, 1024], BF16)
            wk_ukv = load_weight_bf16(cx, w_ukv_sb, k_w_ukv, w_ukv, 2, 1024, stage_ring, cast_ring,
                                      row_scale=(kvnT, k_kvnT))
            perms = {}
            for nm, dap, m in (("da", perm_da_d, 128), ("m", perm_m_d, 96), ("kr", perm_kr_d, 32)):
                pf, k_pf = cx.sb(st, "permf_" + nm, [m, m])
                pb, k_pb = cx.sb(st, "perm_" + nm, [m, m], BF16)
                S.dma("sp", pf[:], dap[:, :], writes=[k_pf])
                S.op("dve", lambda E: E.tensor_copy(out=pb[:], in_=pf[:]), reads=[k_pf], writes=[k_pb])
                perms[nm] = (pb, k_pb)

            xt_r = Ring([cx.sb(st, "xt", [128, 4, D]) for _ in range(2)])
            csda_r = Ring([cx.sb(st, "csda", [128, 2, 512]) for _ in range(2)])
            csm_r = Ring([cx.sb(st, "csm", [96, 2, 512]) for _ in range(2)])
            cskr_r = Ring([cx.sb(st, "cskr", [32, 2, 512]) for _ in range(2)])
            uT_r = Ring([cx.sb(st, "uT", [128, 8, 512], BF16) for _ in range(2)])
            wk = {
                "ss": cx.sb(st, "ss", [128, 4]), "rs": cx.sb(st, "rs", [128, 4]),
                "junk": cx.sb(st, "junk", [128, D], BF16), "xn": cx.sb(st, "xn", [128, 4, D], BF16),
                "ptr": Ring([cx.ps(st, "ptr", [128, 8, 128], BF16) for _ in range(2)]),
                "tmpT": Ring([cx.sb(st, "tmpT", [128, 8, 128]) for _ in range(2)]),
                "kb": Ring([cx.sb(st, "kb", [128, 512], BF16) for _ in range(2)]),
                "t1": Ring([cx.sb(st, "t1", [128, 512]) for _ in range(2)]),
                "t2": Ring([cx.sb(st, "t2", [128, 512]) for _ in range(2)]),
            }
            pg = Ring([cx.ps(st, "pg", [128, 512]) for _ in range(4)])
            psw_r = Ring([cx.ps(st, "psw", [128, 512]) for _ in range(2)])
            ob_r = Ring([cx.sb(st, "ob", [128, 512], BF16) for _ in range(6)])
            vt_r = Ring([cx.sb(st, "vt", [128, 512], BF16) for _ in range(4)])
            vaug_r = Ring([cx.sb(st, "vaug", [128, 8, 65], BF16) for _ in range(4)])
            for _ in range(4):
                va_, k_va = vaug_r.next()
                S.op("pool", lambda E: E.memset(va_[:], 1.0), writes=[k_va])
            cq_r = Ring([cx.sb(st, "cq", [128, 384]) for _ in range(2)])
            cqn_r = Ring([cx.sb(st, "cqn", [128, 384], BF16) for _ in range(2)])
            cqnT, k_cqnT = cx.sb(st, "cqnT", [128, 3, 512], BF16)
            ckvnT, k_ckvnT = cx.sb(st, "ckvnT", [128, 2, 512], BF16)
            krb, k_krb = cx.sb(st, "krb", [32, 512], BF16)
            s2, k_s2 = cx.sb(st, "s2", [128, 2])
            ev = Ring(["act", "dve"])

            def evac(out_ap, in_ap, reads, writes):
                e = ev.next()
                if e == "act":
                    S.op("act", lambda E: E.copy(out=out_ap, in_=in_ap), reads=reads, writes=writes)
                else:
                    S.op("dve", lambda E: E.tensor_copy(out=out_ap, in_=in_ap), reads=reads, writes=writes)

            blocks = [(0, 256, 1, True)] + [(NCTX + i * 512, 512, 0, i < 4) for i in range(16)]

            def issue_loads(bi):
                t0, NT, n, isq = blocks[bi]
                nt = NT // 128
                xt, k_xt = xt_r.next()
                S.dma("sp", xt[:, 0:nt, :], xs[t0:t0 + NT, :].rearrange("(j p) d -> p j d", p=128), writes=[k_xt])
                a, k_a = csda_r.next()
                S.dma("sp", a[:, :, 0:NT], cs_da_d[:, :, t0:t0 + NT], writes=[k_a])
                b_, k_b = csm_r.next()
                S.dma("sp", b_[:, :, 0:NT], cs_m_d[:, :, t0:t0 + NT], writes=[k_b])
                c_, k_c = cskr_r.next()
                S.dma("sp", c_[:, :, 0:NT], cs_kr_d[:, :, t0:t0 + NT], writes=[k_c])
                return (xt, k_xt), (a, k_a), (b_, k_b), (c_, k_c)

            blocks = blocks[:nblk]
            loaded = {0: issue_loads(0)} if blocks else {}
            for bi, (t0, NT, n, isq) in enumerate(blocks):
                nt = NT // 128
                if bi + 1 < len(blocks):
                    loaded[bi + 1] = issue_loads(bi + 1)
                (xt, k_xt), (csda, k_csda), (csm, k_csm), (cskr, k_cskr) = loaded.pop(bi)
                uT, k_uT = uT_r.next()
                uk = emit_norm_T(cx, xt, k_xt, nt, Gmix, k_Gmix, sh_mix, k_modT, n, uT, k_uT, ident, k_id, wk)
                q0 = 0 if bi == 0 else NCTX + (bi - 1) * 512

                def fm(M, col0, wtile, wkeys, rhs, rkeys, kcs):
                    ps_, k_ps = pg.next()
                    fns = [lambda E, kc=kc: E.matmul(ps_[:M, :NT], lhsT=wtile[:, kc, col0:col0 + M], rhs=rhs[:, kc, :NT],
                                                      start=(kc == 0), stop=(kc == kcs - 1)) for kc in range(kcs)]
                    S.group("pe", fns, reads=wkeys + rkeys, writes=[k_ps])
                    return ps_, k_ps

                SKIP = os.environ.get('A1_SKIP', '')
                for h in range(4 if 'd' not in SKIP else 0):
                    ps_, k_ps = fm(128, 512 + h * 128, w_in_sb, wk_in, uT, uk, 8)
                    ob, k_ob = ob_r.next()
                    rope_emit(cx, ps_, k_ps, 128, NT, perms["da"][0], perms["da"][1], csda, k_csda, wk, ob[:, :NT], k_ob, psw_r)
                    if int(os.environ.get("A1_D", "9")) >= 5:
                        S.dma("sp", kt_da[h, :, t0:t0 + NT], ob[:, :NT], reads=[k_ob], writes=[], semkey="st_" + k_ob)
                    if isq:
                        ps_, k_ps = fm(128, h * 128, w_in_sb, wk_in, uT, uk, 8)
                        ob, k_ob = ob_r.next()
                        rope_emit(cx, ps_, k_ps, 128, NT, perms["da"][0], perms["da"][1], csda, k_csda, wk, ob[:, :NT], k_ob, psw_r)
                        if int(os.environ.get("A1_D", "9")) >= 5:
                            S.dma("sp", qt_da[h, :, q0:q0 + NT], ob[:, :NT], reads=[k_ob], writes=[], semkey="st_" + k_ob)
                if 'k' not in SKIP:
                  ps_, k_ps = fm(32, 2176, w_in_sb, wk_in, uT, uk, 8)
                  rope_emit(cx, ps_, k_ps, 32, NT, perms["kr"][0], perms["kr"][1], cskr, k_cskr, wk, krb[:, :NT], k_krb, psw_r)
                for j in range(nt if 'v' not in SKIP else 0):
                    ps_, k_ps = pg.next()
                    fns = [lambda E, kc=kc: E.matmul(ps_[:, :512], lhsT=uT[:, kc, j * 128:(j + 1) * 128],
                                                      rhs=w_in_sb[:, kc, 1024:1536], start=(kc == 0), stop=(kc == 7))
                           for kc in range(8)]
                    S.group("pe", fns, reads=wk_in + [uk[j]], writes=[k_ps])
                    vt, k_vt = vt_r.next()
                    evac(vt[:, :], ps_[:, :512], [k_ps], [k_vt])
                    S.dma("sp", v_da[t0 + j * 128:t0 + (j + 1) * 128, :], vt[:, :], reads=[k_vt], writes=[], semkey="st_" + k_vt)
                    for which in (("ckv", 1920, 256, ckvnT, k_ckvnT, 2),) + ((("cq", 1536, 384, cqnT, k_cqnT, 3),) if isq else ()):
                        nm, c0, W, dstT, k_dstT, nch = which
                        ps_, k_ps = pg.next()
                        fns = [lambda E, kc=kc: E.matmul(ps_[:, :W], lhsT=uT[:, kc, j * 128:(j + 1) * 128],
                                                          rhs=w_in_sb[:, kc, c0:c0 + W], start=(kc == 0), stop=(kc == 7))
                               for kc in range(8)]
                        S.group("pe", fns, reads=wk_in + [uk[j]], writes=[k_ps])
                        cq, k_cq = cq_r.next()
                        S.op("act", lambda E: E.activation(out=cq[:, :W], in_=ps_[:, :W], func=AF.Square,
                                                           accum_out=s2[:, 0:1]), reads=[k_ps], writes=[k_cq, k_s2])
                        S.op("act", lambda E: E.activation(out=s2[:, 1:2], in_=s2[:, 0:1], func=AF.Sqrt, scale=1.0 / W,
                                                           bias=EPS), reads=[k_s2], writes=[k_s2])
                        S.op("dve", lambda E: E.reciprocal(out=s2[:, 1:2], in_=s2[:, 1:2]), reads=[k_s2], writes=[k_s2])
                        cqn, k_cqn = cqn_r.next()
                        S.op("dve", lambda E: E.tensor_scalar(out=cqn[:, :W], in0=ps_[:, :W], scalar1=s2[:, 1:2],
                                                              scalar2=None, op0=ALU.mult),
                             reads=[k_ps, k_s2], writes=[k_cqn])
                        ptr, k_ptr = wk["ptr"].next()
                        fns = [lambda E, kc=kc: E.transpose(out=ptr[:, kc, :], in_=cqn[:, kc * 128:(kc + 1) * 128],
                                                             identity=ident[:]) for kc in range(nch)]
                        S.group("pe", fns, reads=[k_cqn, k_id], writes=[k_ptr])
                        evac(dstT[:, 0:nch, j * 128:(j + 1) * 128], ptr[:, 0:nch, :], [k_ptr], [k_dstT + "_%d" % j])
                ckv_keys = [k_ckvnT + "_%d" % j for j in range(nt)]
                cq_keys = [k_cqnT + "_%d" % j for j in range(nt)]
                for j in range(nt if 'm' not in SKIP else 0):
                    ps_, k_ps = pg.next()
                    fns = [lambda E, kc=kc: E.matmul(ps_[:, :512], lhsT=ckvnT[:, kc, j * 128:(j + 1) * 128],
                                                      rhs=w_ukv_sb[:, kc, :].rearrange("p (h t d) -> p h t d", h=8, t=2)[:, :, 1, :],
                                                      start=(kc == 0), stop=(kc == 1)) for kc in range(2)]
                    S.group("pe", fns, reads=wk_ukv + [ckv_keys[j]], writes=[k_ps])
                    va_, k_va = vaug_r.next()
                    evac(va_[:, :, 0:64], ps_[:, :512].rearrange("p (h d) -> p h d", h=8), [k_ps], [k_va])
                    S.dma("sp", v_m[t0 + j * 128:t0 + (j + 1) * 128, :], va_[:].rearrange("p h d -> p (h d)"),
                          reads=[k_va], writes=[], semkey="st_" + k_va)
                for h in range(8 if 'h' not in SKIP else 0):
                    ps_, k_ps = fm(64, h * 128, w_ukv_sb, wk_ukv, ckvnT, ckv_keys, 2)
                    ob, k_ob = ob_r.next()
                    evac(ob[0:64, :NT], ps_[0:64, :NT], [k_ps], [k_ob])
                    S.op("act", lambda E: E.copy(out=ob[64:96, :NT], in_=krb[0:32, :NT]), reads=[k_krb], writes=[k_ob])
                    S.dma("sp", kt_m[h, :, t0:t0 + NT], ob[0:96, :NT], reads=[k_ob], writes=[], semkey="st_" + k_ob)
                    if isq:
                        ps_, k_ps = fm(96, h * 96, w_uq_sb, wk_uq, cqnT, cq_keys, 3)
                        ob, k_ob = ob_r.next()
                        rope_emit(cx, ps_, k_ps, 96, NT, perms["m"][0], perms["m"][1], csm, k_csm, wk, ob[0:96, :NT], k_ob, psw_r)
                        S.dma("sp", qt_m[h, :, q0:q0 + NT], ob[0:96, :NT], reads=[k_ob], writes=[], semkey="st_" + k_ob)
            cx.barrier()
        if upto >= 2:
            build_A2(cx, locals())
    return nc


def build_A2(cx, L):
    nc, S = cx.nc, cx.S
    kt_da, v_da, qt_da, yt_da = L["kt_da"], L["v_da"], L["qt_da"], L["yt_da"]
    kt_m, v_m, qt_m, yt_m = L["kt_m"], L["v_m"], L["qt_m"], L["yt_m"]
    nqb = L.get("nqb", 5)
    qblocks = ([(0, 256, 2)] + [(NCTX + i * 512, 512, NKEY // 128) for i in range(4)])[:nqb]
    lam_init = 0.8 - 0.6 * math.exp(-0.3 * 0)

    with ExitStack() as st:
        onesb, k_onesb = cx.sb(st, "onesb", [128, 128], BF16)
        onesm, k_onesm = cx.sb(st, "onesm", [128, 128], BF16)
        S.op("pool", lambda E: E.memset(onesb[:], 1.0), writes=[k_onesb])
        S.op("pool", lambda E: E.memset(onesm[:], 1.0 / 128), writes=[k_onesm])
        lamb, k_lamb = cx.sb(st, "lamb", [128, 256])
        lw, k_lw = cx.sb(st, "lw", [128, 8])
        S.dma("sp", lamb[:], L["lam"].partition_broadcast(128), writes=[k_lamb])
        prod, k_prod = cx.sb(st, "lprod", [128, 2, 64])
        lam4 = lamb[:].rearrange("p (a d) -> p a d", a=4)
        S.op("dve", lambda E: E.tensor_tensor(out=prod[:, 0, :], in0=lam4[:, 0, :], in1=lam4[:, 1, :], op=ALU.mult),
             reads=[k_lamb], writes=[k_prod])
        S.op("dve", lambda E: E.tensor_tensor(out=prod[:, 1, :], in0=lam4[:, 2, :], in1=lam4[:, 3, :], op=ALU.mult),
             reads=[k_lamb], writes=[k_prod])
        S.op("dve", lambda E: E.reduce_sum(out=lw[:, 0:2], in_=prod[:], axis=AX.X), reads=[k_prod], writes=[k_lw])
        S.op("act", lambda E: E.activation(out=lw[:, 2:4], in_=lw[:, 0:2], func=AF.Exp), reads=[k_lw], writes=[k_lw])
        S.op("dve", lambda E: E.tensor_tensor(out=lw[:, 4:5], in0=lw[:, 3:4], in1=lw[:, 2:3], op=ALU.subtract),
             reads=[k_lw], writes=[k_lw])
        S.op("dve", lambda E: E.tensor_scalar_add(out=lw[:, 5:6], in0=lw[:, 4:5], scalar1=-lam_init),
             reads=[k_lw], writes=[k_lw])
        neglam = lw[:, 5:6]
        sn, k_sn = cx.sb(st, "sn", [128, 2])
        S.dma("sp", sn[:, 0:1], L["subnorm"].rearrange("(p o) -> p o", o=1), writes=[k_sn])
        S.op("dve", lambda E: E.tensor_scalar(out=sn[:, 1:2], in0=sn[:, 0:1], scalar1=1.0 - lam_init, scalar2=None,
                                              op0=ALU.mult), reads=[k_sn], writes=[k_sn])
        kt_r = Ring([cx.sb(st, "ktda", [128, NKEY], BF16) for _ in range(2)])
        v_r = Ring([cx.sb(st, "vda", [128, NKEY // 128, 128], BF16) for _ in range(2)])
        q_r = Ring([cx.sb(st, "qda", [128, NQ], BF16) for _ in range(2)])
        pt_r = Ring([cx.sb(st, "pt", [128, 512], BF16) for _ in range(6)])
        acc_o = [cx.ps(st, "acc_o", [128, 512]) for _ in range(2)]
        acc_s = [cx.ps(st, "acc_s", [128, 512]) for _ in range(2)]
        pss_r = Ring([cx.ps(st, "pss", [128, 512]) for _ in range(4)])
        rr = [cx.sb(st, "rr", [128, 512]) for _ in range(2)]
        tt = [cx.sb(st, "tt", [128, 512]) for _ in range(2)]
        dd, k_dd = cx.sb(st, "dd", [128, 512])
        d2, k_d2 = cx.sb(st, "d2", [128, 512], BF16)
        rstd, k_rstd = cx.sb(st, "rstd", [128, 512])
        ya, k_ya = cx.sb(st, "ya", [128, 512])
        yab_r = Ring([cx.sb(st, "yab", [128, 512], BF16) for _ in range(2)])

        def load_head(h):
            kt, k_kt = kt_r.next()
            S.dma("sp", kt[:], kt_da[h, :, :], writes=[k_kt])
            v, k_v = v_r.next()
            S.dma("sp", v[:], v_da[:, h * 128:(h + 1) * 128].rearrange("(t p) d -> p t d", p=128), writes=[k_v])
            q, k_q = q_r.next()
            S.dma("sp", q[:], qt_da[h, :, :], writes=[k_q])
            return (kt, k_kt), (v, k_v), (q, k_q)

        nh_da = L.get("nh_da", 4)
        nxt = load_head(0) if nh_da > 0 else None
        for h in range(nh_da):
            (kt, k_kt), (v, k_v), (q, k_q) = nxt
            if h + 1 < nh_da:
                nxt = load_head(h + 1)
            for (q0, Nq, nkt) in qblocks:
                pend = None
                for step in range(nkt + 1):
                    cur = None
                    if step < nkt:
                        cur = []
                        for m in (0, 1):
                            ps_, k_ps = pss_r.next()
                            S.op("pe", lambda E: E.matmul(ps_[:, :Nq], lhsT=kt[m * 64:(m + 1) * 64, step * 128:(step + 1) * 128],
                                                          rhs=q[m * 64:(m + 1) * 64, q0:q0 + Nq], start=True, stop=True),
                                 reads=[k_kt, k_q], writes=[k_ps])
                            pt, k_pt = pt_r.next()
                            S.op("act", lambda E: E.activation(out=pt[:, :Nq], in_=ps_[:, :Nq], func=AF.Exp, scale=DA_SCALE),
                                 reads=[k_ps], writes=[k_pt])
                            cur.append((pt, k_pt))
                    if pend is not None:
                        kk = step - 1
                        for m in (0, 1):
                            pt, k_pt = pend[m]
                            S.op("pe", lambda E: E.matmul(acc_o[m][0][:, :Nq], lhsT=v[:, kk, :], rhs=pt[:, :Nq],
                                                          start=(kk == 0), stop=(kk == nkt - 1)),
                                 reads=[k_v, k_pt], writes=[acc_o[m][1]])
                            S.op("pe", lambda E: E.matmul(acc_s[m][0][:, :Nq], lhsT=onesb[:, :], rhs=pt[:, :Nq],
                                                          start=(kk == 0), stop=(kk == nkt - 1)),
                                 reads=[k_onesb, k_pt], writes=[acc_s[m][1]])
                    pend = cur
                for m in (0, 1):
                    S.op("dve", lambda E: E.reciprocal(out=rr[m][0][:, :Nq], in_=acc_s[m][0][:, :Nq]),
                         reads=[acc_s[m][1]], writes=[rr[m][1]])
                    S.op("dve", lambda E: E.tensor_tensor(out=tt[m][0][:, :Nq], in0=acc_o[m][0][:, :Nq], in1=rr[m][0][:, :Nq],
                                                          op=ALU.mult), reads=[acc_o[m][1], rr[m][1]], writes=[tt[m][1]])
                S.op("dve", lambda E: E.scalar_tensor_tensor(out=dd[:, :Nq], in0=tt[1][0][:, :Nq], scalar=neglam,
                                                             in1=tt[0][0][:, :Nq], op0=ALU.mult, op1=ALU.add),
                     reads=[tt[0][1], tt[1][1], k_lw], writes=[k_dd])
                S.op("pool", lambda E: E.tensor_tensor(out=d2[:, :Nq], in0=dd[:, :Nq], in1=dd[:, :Nq], op=ALU.mult),
                     reads=[k_dd], writes=[k_d2])
                ps_, k_ps = pss_r.next()
                S.op("pe", lambda E: E.matmul(ps_[:, :Nq], lhsT=onesm[:, :], rhs=d2[:, :Nq], start=True, stop=True),
                     reads=[k_onesm, k_d2], writes=[k_ps])
                S.op("act", lambda E: E.activation(out=rstd[:, :Nq], in_=ps_[:, :Nq], func=AF.Sqrt, bias=EPS),
                     reads=[k_ps], writes=[k_rstd])
                S.op("dve", lambda E: E.reciprocal(out=rstd[:, :Nq], in_=rstd[:, :Nq]), reads=[k_rstd], writes=[k_rstd])
                S.op("dve", lambda E: E.tensor_tensor(out=ya[:, :Nq], in0=dd[:, :Nq], in1=rstd[:, :Nq], op=ALU.mult),
                     reads=[k_dd, k_rstd], writes=[k_ya])
                yab, k_yab = yab_r.next()
                S.op("act", lambda E: E.activation(out=yab[:, :Nq], in_=ya[:, :Nq], func=AF.Copy, scale=sn[:, 1:2]),
                     reads=[k_ya, k_sn], writes=[k_yab])
                S.dma("sp", yt_da[h, :, q0:q0 + Nq], yab[:, :Nq], reads=[k_yab], writes=[], semkey="st_" + k_yab)
        cx.barrier()

    with ExitStack() as st:
        sel, k_sel = cx.sb(st, "sel65", [65, 64])
        S.op("pool", lambda E: E.memset(sel[:], 0.0), writes=[k_sel])
        S.op("pool", lambda E: E.memset(sel[64:65, :], 1.0), writes=[k_sel])
        kt_r = Ring([cx.sb(st, "ktm", [96, NKEY], BF16) for _ in range(2)])
        v_r = Ring([cx.sb(st, "vm", [128, NKEY // 128, 65], BF16) for _ in range(2)])
        q_r = Ring([cx.sb(st, "qm", [96, NQ], BF16) for _ in range(2)])
        pt_r = Ring([cx.sb(st, "ptm", [128, 512], BF16) for _ in range(4)])
        acc_r = Ring([cx.ps(st, "accm", [128, 512]) for _ in range(2)])
        pss_r = Ring([cx.ps(st, "pssm", [128, 512]) for _ in range(4)])
        psb_r = Ring([cx.ps(st, "psb", [128, 512]) for _ in range(2)])
        accs_r = Ring([cx.sb(st, "accs", [65, 512]) for _ in range(2)])
        rrm, k_rrm = cx.sb(st, "rrm", [64, 512])
        ymb_r = Ring([cx.sb(st, "ymb", [64, 512], BF16) for _ in range(2)])

        def load_head_m(h):
            kt, k_kt = kt_r.next()
            S.dma("sp", kt[:], kt_m[h, :, :], writes=[k_kt])
            v, k_v = v_r.next()
            S.dma("sp", v[:], v_m[:, h * 65:(h + 1) * 65].rearrange("(t p) d -> p t d", p=128), writes=[k_v])
            q, k_q = q_r.next()
            S.dma("sp", q[:], qt_m[h, :, :], writes=[k_q])
            return (kt, k_kt), (v, k_v), (q, k_q)

        nh_m = L.get("nh_m", 8)
        nxt = load_head_m(0) if nh_m > 0 else None
        for h in range(nh_m):
            (kt, k_kt), (v, k_v), (q, k_q) = nxt
            if h + 1 < nh_m:
                nxt = load_head_m(h + 1)
            for (q0, Nq, nkt) in qblocks:
                acc, k_acc = acc_r.next()
                pend = None
                for step in range(nkt + 1):
                    cur = None
                    if step < nkt:
                        ps_, k_ps = pss_r.next()
                        S.op("pe", lambda E: E.matmul(ps_[:, :Nq], lhsT=kt[:, step * 128:(step + 1) * 128],
                                                      rhs=q[:, q0:q0 + Nq], start=True, stop=True),
                             reads=[k_kt, k_q], writes=[k_ps])
                        pt, k_pt = pt_r.next()
                        S.op("act", lambda E: E.activation(out=pt[:, :Nq], in_=ps_[:, :Nq], func=AF.Exp, scale=MLA_SCALE),
                             reads=[k_ps], writes=[k_pt])
                        cur = (pt, k_pt)
                    if pend is not None:
                        kk = step - 1
                        pt, k_pt = pend
                        S.op("pe", lambda E: E.matmul(acc[0:65, :Nq], lhsT=v[:, kk, :], rhs=pt[:, :Nq],
                                                      start=(kk == 0), stop=(kk == nkt - 1)),
                             reads=[k_v, k_pt], writes=[k_acc])
                    pend = cur
                accs, k_accs = accs_r.next()
                S.op("act", lambda E: E.copy(out=accs[0:65, :Nq], in_=acc[0:65, :Nq]), reads=[k_acc], writes=[k_accs])
                psb, k_psb = psb_r.next()
                S.op("pe", lambda E: E.matmul(psb[0:64, :Nq], lhsT=sel[0:65, :], rhs=accs[0:65, :Nq], start=True, stop=True),
                     reads=[k_sel, k_accs], writes=[k_psb])
                S.op("dve", lambda E: E.reciprocal(out=rrm[:, :Nq], in_=psb[0:64, :Nq]), reads=[k_psb], writes=[k_rrm])
                ymb, k_ymb = ymb_r.next()
                S.op("dve", lambda E: E.tensor_tensor(out=ymb[:, :Nq], in0=accs[0:64, :Nq], in1=rrm[:, :Nq], op=ALU.mult),
                     reads=[k_accs, k_rrm], writes=[k_ymb])
                S.dma("sp", yt_m[h, :, q0:q0 + Nq], ymb[:, :Nq], reads=[k_ymb], writes=[], semkey="st_" + k_ymb)
        cx.barrier()
    if L.get("upto", 99) >= 3:
        ysrc = [(yt_da, 4, 128, 0), (yt_m, 8, 64, 512)]
        emit_outproj_mlp(cx, ysrc, L["w_out"], L["xs"], L["gate"], L["Gmlp"], L["k_Gmlp"], L["sh_mlp"], L["k_modT"],
                         L["w1"], L["w2"], L["h_out"], L["ident"], L["k_id"], final_norm=None)


def emit_outproj_mlp(cx, ysrc, w_out, hsrc, gate, Gmlp, k_Gmlp, sh_mlp, k_shmlp, w1, w2, h_out, ident, k_id,
                     final_norm=None, gate_mix=2, gate_mlp=5, ytok=None):
    nc, S = cx.nc, cx.S
    NT_ALL = NQ // 128
    with ExitStack() as st:
        h1, k_h1 = cx.sb(st, "h1", [128, NT_ALL, D])
        uT, k_uT = cx.sb(st, "uTmlp", [128, 8, NQ], BF16)
        stage_ring = Ring([cx.sb(st, "wstage2", [128, 2048]) for _ in range(2)])
        cast_ring = Ring(["act", "pool", "dve"])
        tmp_r = Ring([cx.sb(st, "gtmp", [128, 512]) for _ in range(3)])
        with ExitStack() as st2:
            wo = []
            for (ydram, nch, prows, row0) in ysrc:
                wt, k_wt = cx.sb(st2, "wo", [prows, nch, D], BF16)
                keys = load_weight_bf16(cx, wt, k_wt, w_out[row0:row0 + nch * prows, :], nch, D, stage_ring, cast_ring,
                                        prows=prows)
                wo.append((wt, keys))
            yt_r = [Ring([cx.sb(st2, "ytile", [prows, nch, 128], BF16) for _ in range(2)]) for (_, nch, prows, _) in ysrc]
            if ytok is not None:
                Ydram, ssdg = ytok
                sgb, k_sgb = cx.sb(st2, "sgb", [128, 512])
                S.dma("sp", sgb[:], ssdg.partition_broadcast(128), writes=[k_sgb])
                yf_r = Ring([cx.sb(st2, "yf", [128, D]) for _ in range(2)])
                ynb_r = Ring([cx.sb(st2, "ynb", [128, D], BF16) for _ in range(2)])
                ysq, k_ysq = cx.sb(st2, "ysq", [128, 256], BF16)
                ys2, k_ys2 = cx.sb(st2, "ys2", [128, 4])
            xt_r = Ring([cx.sb(st2, "xres", [128, 1, D]) for _ in range(2)])
            pso_r = Ring([cx.ps(st2, "pso", [128, 512]) for _ in range(4)])
            wk = {
                "ss": cx.sb(st2, "ss", [128, 4]), "rs": cx.sb(st2, "rs", [128, 4]),
                "junk": cx.sb(st2, "junk", [128, D], BF16), "xn": cx.sb(st2, "xn", [128, 1, D], BF16),
                "ptr": Ring([cx.ps(st2, "ptr", [128, 8, 128], BF16) for _ in range(2)]),
                "tmpT": Ring([cx.sb(st2, "tmpT", [128, 8, 128]) for _ in range(2)]),
            }

            def loads(ti):
                ys = []
                if ytok is not None:
                    yf, k_yf = yf_r.next()
                    S.dma("sp", yf[:], Ydram[ti * 128:(ti + 1) * 128, :], writes=[k_yf])
                    ynb, k_ynb = ynb_r.next()
                    for gi in (0, 1):
                        S.op("act", lambda E: E.activation(out=ysq[:], in_=yf[:, 512 + gi * 256:768 + gi * 256], func=AF.Square,
                                                           accum_out=ys2[:, gi:gi + 1]), reads=[k_yf], writes=[k_ysq, k_ys2])
                    S.op("act", lambda E: E.activation(out=ys2[:, 2:4], in_=ys2[:, 0:2], func=AF.Sqrt, scale=1.0 / 256, bias=EPS),
                         reads=[k_ys2], writes=[k_ys2])
                    S.op("dve", lambda E: E.reciprocal(out=ys2[:, 2:4], in_=ys2[:, 2:4]), reads=[k_ys2], writes=[k_ys2])
                    S.op("pool", lambda E: E.tensor_copy(out=ynb[:, 0:512], in_=yf[:, 0:512]), reads=[k_yf], writes=[k_ynb + "a"])
                    for gi in (0, 1):
                        S.op("dve", lambda E: E.scalar_tensor_tensor(out=ynb[:, 512 + gi * 256:768 + gi * 256],
                                                                     in0=yf[:, 512 + gi * 256:768 + gi * 256], scalar=ys2[:, 2 + gi:3 + gi],
                                                                     in1=sgb[:, gi * 256:(gi + 1) * 256], op0=ALU.mult, op1=ALU.mult),
                             reads=[k_yf, k_ys2, k_sgb], writes=[k_ynb + "b%d" % gi])
                    ptr, k_ptr = wk["ptr"].next()
                    fns = [lambda E, kc=kc: E.transpose(out=ptr[:, kc, :], in_=ynb[:, kc * 128:(kc + 1) * 128], identity=ident[:])
                           for kc in range(8)]
                    S.group("pe", fns, reads=[k_ynb + "a", k_ynb + "b0", k_ynb + "b1", k_id], writes=[k_ptr])
                    yt, k_yt = yt_r[0].next()
                    S.op("act", lambda E: E.copy(out=yt[:], in_=ptr[:]), reads=[k_ptr], writes=[k_yt])
                    ys.append((yt, k_yt))
                for si, (ydram, nch, prows, row0) in enumerate(ysrc if ytok is None else []):
                    yt, k_yt = yt_r[si].next()
                    S.dma("sp", yt[:], ydram[:, :, ti * 128:(ti + 1) * 128].rearrange("c p t -> p c t"), writes=[k_yt])
                    ys.append((yt, k_yt))
                xt, k_xt = xt_r.next()
                S.dma("sp", xt[:, 0, :], hsrc[ti * 128:(ti + 1) * 128, :], writes=[k_xt])
                return ys, (xt, k_xt)

            nxt = loads(0)
            for ti in range(NT_ALL):
                n = 1 if ti < 2 else 0
                ys, (xt, k_xt) = nxt
                if ti + 1 < NT_ALL:
                    nxt = loads(ti + 1)
                for half in (0, 1):
                    ps_, k_ps = pso_r.next()
                    fns = []
                    rk = []
                    tot = sum(nch for (_, nch, _, _) in ysrc)
                    idx = 0
                    for si, (ydram, nch, prows, row0) in enumerate(ysrc):
                        yt, k_yt = ys[si]
                        wt, wkeys = wo[si]
                        rk += [k_yt] + wkeys
                        for c in range(nch):
                            fns.append(lambda E, yt=yt, wt=wt, c=c, prows=prows, idx=idx: E.matmul(
                                ps_[:, :], lhsT=yt[0:prows, c, :], rhs=wt[0:prows, c, half * 512:(half + 1) * 512],
                                start=(idx == 0), stop=(idx == tot - 1)))
                            idx += 1
                    S.group("pe", fns, reads=rk, writes=[k_ps])
                    g, k_g = gate[(gate_mix, n)]
                    tmp, k_tmp = tmp_r.next()
                    S.op("dve", lambda E: E.tensor_tensor(out=tmp[:], in0=ps_[:, :], in1=g[:, half * 512:(half + 1) * 512],
                                                          op=ALU.mult), reads=[k_ps, k_g], writes=[k_tmp])
                    S.op("pool", lambda E: E.tensor_tensor(out=h1[:, ti, half * 512:(half + 1) * 512], in0=tmp[:],
                                                           in1=xt[:, 0, half * 512:(half + 1) * 512], op=ALU.add),
                         reads=[k_tmp, k_xt], writes=["%s_%d_%d" % (k_h1, ti, half)])
                hk = ["%s_%d_%d" % (k_h1, ti, hh) for hh in (0, 1)]
                S.op("act", lambda E: E.activation(out=wk["junk"][0][:], in_=h1[:, ti, :], func=AF.Square,
                                                   accum_out=wk["ss"][0][:, 0:1]), reads=hk, writes=[wk["junk"][1], wk["ss"][1]])
                S.op("act", lambda E: E.activation(out=wk["rs"][0][:, 0:1], in_=wk["ss"][0][:, 0:1], func=AF.Sqrt,
                                                   scale=1.0 / D, bias=EPS), reads=[wk["ss"][1]], writes=[wk["rs"][1]])
                S.op("dve", lambda E: E.reciprocal(out=wk["rs"][0][:, 0:1], in_=wk["rs"][0][:, 0:1]),
                     reads=[wk["rs"][1]], writes=[wk["rs"][1]])
                xn, k_xn = wk["xn"]
                S.op("dve", lambda E: E.tensor_scalar(out=xn[:, 0, :], in0=h1[:, ti, :], scalar1=wk["rs"][0][:, 0:1],
                                                      scalar2=None, op0=ALU.mult), reads=hk + [wk["rs"][1]], writes=[k_xn])
                ptr, k_ptr = wk["ptr"].next()
                fns = [lambda E, kc=kc: E.transpose(out=ptr[:, kc, :], in_=xn[:, 0, kc * 128:(kc + 1) * 128], identity=ident[:])
                       for kc in range(8)]
                S.group("pe", fns, reads=[k_xn, k_id], writes=[k_ptr])
                tmpT, k_tmpT = wk["tmpT"].next()
                S.op("dve", lambda E: E.tensor_tensor(out=tmpT[:], in0=ptr[:], in1=Gmlp[:, :, n:n + 1].broadcast_to([128, 8, 128]),
                                                      op=ALU.mult), reads=[k_ptr, k_Gmlp], writes=[k_tmpT])
                S.op("pool", lambda E: E.tensor_tensor(out=uT[:, :, ti * 128:(ti + 1) * 128], in0=tmpT[:],
                                                       in1=sh_mlp[:, :, n:n + 1].broadcast_to([128, 8, 128]), op=ALU.add),
                     reads=[k_tmpT, k_shmlp], writes=["%s_%d" % (k_uT, ti)])
            cx.barrier()
        with ExitStack() as st2:
            w1_r = Ring([cx.sb(st2, "w1c", [128, 8, 512], BF16) for _ in range(2)])
            w2_r = Ring([cx.sb(st2, "w2c", [128, 4, D], BF16) for _ in range(2)])
            psu_r = Ring([cx.ps(st2, "psu", [128, 512]) for _ in range(4)])
            psd_r = Ring([cx.ps(st2, "psd", [128, 512]) for _ in range(4)])
            rl_r = Ring([cx.sb(st2, "rl", [128, 512]) for _ in range(2)])
            hT_r = Ring([cx.sb(st2, "hT", [128, 4, 512], BF16) for _ in range(2)])
            tblocks = [(0, 256)] + [(NCTX + i * 512, 512) for i in range(4)]
            w2v = w2.rearrange("(c p) n -> p c n", p=128)

            def load_w(hc):
                w1c, k_w1c = w1_r.next()
                k1 = load_weight_bf16(cx, w1c, k_w1c, w1, 8, 512, stage_ring, cast_ring, col0=hc * 512)
                w2c, k_w2c = w2_r.next()
                k2 = load_weight_bf16(cx, w2c, k_w2c, w2[hc * 512:(hc + 1) * 512, :], 4, D, stage_ring, cast_ring)
                return (w1c, k1), (w2c, k2)

            nxt = load_w(0)
            for hc in range(8):
                (w1c, k1), (w2c, k2) = nxt
                if hc + 1 < 8:
                    nxt = load_w(hc + 1)
                for (t0, NT) in tblocks:
                    n = 1 if t0 == 0 else 0
                    hT, k_hT = hT_r.next()
                    tiles = list(range(t0 // 128, (t0 + NT) // 128))
                    for sub in range(4):
                        ps_, k_ps = psu_r.next()
                        fns = [lambda E, kc=kc: E.matmul(ps_[:, :NT], lhsT=w1c[:, kc, sub * 128:(sub + 1) * 128],
                                                          rhs=uT[:, kc, t0:t0 + NT], start=(kc == 0), stop=(kc == 7))
                               for kc in range(8)]
                        S.group("pe", fns, reads=k1 + ["%s_%d" % (k_uT, ti) for ti in tiles], writes=[k_ps])
                        rl, k_rl = rl_r.next()
                        S.op("act", lambda E: E.activation(out=rl[:, :NT], in_=ps_[:, :NT], func=AF.Relu),
                             reads=[k_ps], writes=[k_rl])
                        S.op("pool", lambda E: E.tensor_tensor(out=hT[:, sub, :NT], in0=rl[:, :NT], in1=rl[:, :NT], op=ALU.mult),
                             reads=[k_rl], writes=["%s_%d" % (k_hT, sub)])
                    for j, ti in enumerate(tiles):
                        for half in (0, 1):
                            ps_, k_ps = psd_r.next()
                            fns = [lambda E, sub=sub: E.matmul(ps_[:, :], lhsT=hT[:, sub, j * 128:(j + 1) * 128],
                                                                rhs=w2c[:, sub, half * 512:(half + 1) * 512],
                                                                start=(sub == 0), stop=(sub == 3)) for sub in range(4)]
                            S.group("pe", fns, reads=k2 + ["%s_%d" % (k_hT, sub) for sub in range(4)], writes=[k_ps])
                            g, k_g = gate[(gate_mlp, n)]
                            tmp, k_tmp = tmp_r.next()
                            S.op("dve", lambda E: E.tensor_tensor(out=tmp[:], in0=ps_[:, :], in1=g[:, half * 512:(half + 1) * 512],
                                                                  op=ALU.mult), reads=[k_ps, k_g], writes=[k_tmp])
                            hk = "%s_%d_%d" % (k_h1, ti, half)
                            S.op("pool", lambda E: E.tensor_tensor(out=h1[:, ti, half * 512:(half + 1) * 512],
                                                                   in0=h1[:, ti, half * 512:(half + 1) * 512], in1=tmp[:], op=ALU.add),
                                 reads=[k_tmp, hk], writes=[hk])
            cx.barrier()
        if final_norm is not None:
            with ExitStack() as st2:
                fn_b, k_fnb = cx.sb(st2, "fnb", [128, D])
                S.dma("sp", fn_b[:], final_norm.partition_broadcast(128), writes=[k_fnb])
                ss, k_ss = cx.sb(st2, "fss", [128, 2])
                junk, k_junk = cx.sb(st2, "fjunk", [128, D], BF16)
                for ti in range(NT_ALL):
                    hk = ["%s_%d_%d" % (k_h1, ti, hh) for hh in (0, 1)]
                    S.op("act", lambda E: E.activation(out=junk[:], in_=h1[:, ti, :], func=AF.Square, accum_out=ss[:, 0:1]),
                         reads=hk, writes=[k_junk, k_ss])
                    S.op("act", lambda E: E.activation(out=ss[:, 1:2], in_=ss[:, 0:1], func=AF.Sqrt, scale=1.0 / D, bias=EPS),
                         reads=[k_ss], writes=[k_ss])
                    S.op("dve", lambda E: E.reciprocal(out=ss[:, 1:2], in_=ss[:, 1:2]), reads=[k_ss], writes=[k_ss])
                    S.op("dve", lambda E: E.scalar_tensor_tensor(out=h1[:, ti, :], in0=h1[:, ti, :], scalar=ss[:, 1:2],
                                                                 in1=fn_b[:], op0=ALU.mult, op1=ALU.mult),
                         reads=hk + [k_ss, k_fnb], writes=hk)
                cx.barrier()
        for ti in range(NT_ALL):
            hk = ["%s_%d_%d" % (k_h1, ti, hh) for hh in (0, 1)]
            S.dma("sp", h_out[ti * 128:(ti + 1) * 128, :], h1[:, ti, :], reads=hk, writes=[], semkey="st_h1_%d" % (ti % 4))
        cx.barrier()


def host_consts(qtr):
    pos = np.concatenate([-np.ones(NCTX, np.int64), (np.arange(NLAT) + qtr * NOWN) % NLAT])
    c64, s64 = rope_tables(pos, 64)
    c32, s32 = rope_tables(pos, 32)
    cs_da = np.stack([np.concatenate([c64, c64], 0), np.concatenate([s64, s64], 0)], axis=1)
    one = np.ones((64, NKEY), np.float32)
    zero = np.zeros((64, NKEY), np.float32)
    cs_m = np.stack([np.concatenate([one, c32], 0), np.concatenate([zero, s32], 0)], axis=1)
    cs_kr = np.stack([c32, s32], axis=1)
    return {
        "cs_da": np.ascontiguousarray(cs_da, np.float32), "cs_m": np.ascontiguousarray(cs_m, np.float32),
        "cs_kr": np.ascontiguousarray(cs_kr, np.float32),
        "perm_da": perm_matrix(2, 64), "perm_m": perm_matrix(1, 32, pad_rows=64), "perm_kr": perm_matrix(1, 32),
        "ident": np.eye(128, dtype=np.float32),
    }


def prep_A(inp, core):
    b, qtr = core // 4, core % 4
    f = lambda a: np.ascontiguousarray(a, dtype=np.float32)
    xs = np.concatenate([inp["ctx"][b], np.roll(inp["x"][b], -qtr * NOWN, axis=0)], axis=0)
    c2 = np.stack([inp["c"][b].reshape(8, 128).T, inp["c_ctx"].reshape(8, 128).T], axis=-1).reshape(128, 16)
    m = {
        "xs": f(xs), "c2": f(c2), "w_mod": f(inp["w_mod"][0]), "b_mod": f(inp["b_mod"][0]),
        "norm_mix": f(inp["norm_mix"][0]), "norm_mlp": f(inp["norm_mlp"][0]), "w_in": f(inp["att_w_in"][0]),
        "q_norm": f(inp["mla_q_norm"][0]), "w_uq": f(inp["mla_w_uq"][0]), "kv_norm": f(inp["mla_kv_norm"][0]),
        "w_ukv": f(inp["mla_w_ukv"][0]), "w_out": f(inp["att_w_out"][0]), "lam": f(inp["att_lambda"][0].reshape(256)),
        "subnorm": f(inp["att_subnorm"][0]), "w1": f(inp["w_mlp_in"][0]), "w2": f(inp["w_mlp_out"][0]),
    }
    m.update(host_consts(qtr))
    return m


NCOLB = 1156
C_Q, C_FF, C_FB, C_I, C_G, C_Z, C_X, C_B, C_C, C_DT = 0, 128, 256, 384, 512, 640, 768, 896, 1024, 1152


def build_B(nlat=NLAT, debug=False):
    nc = bass.Bass("TRN2", target_bir_lowering=False)
    NK = NCTX + nlat
    NPAD = NK + 8
    OFF_CTX, OFF_LAT = 2, NCTX + 6

    blob = Blob(nc, spec_B(nlat))

    def din(name, shape, dt=F32):
        return blob.get(name)

    def dscr(name, shape, dt=BF16):
        return nc.dram_tensor(name, list(shape), dt, kind="Internal").ap()

    hs = din("hs", [NK, D])
    c2 = din("c2", [128, 16])
    w_mod = din("w_mod", [D, 2048])
    b_mod = din("b_mod", [2048])
    norm_mix = din("norm_mix", [D])
    w_rec = din("w_rec", [D, NCOLB])
    bl = din("bl", [128, 4])
    out_norm = din("out_norm", [128])
    cw = din("cw", [128, 15])
    cb = din("cb", [128, 3])
    alog4 = din("alog4", [4])
    dtb4 = din("dtb4", [4])
    skipb = din("skipb", [128])
    masks = din("masks", [64, 4, 64])
    mask01 = din("mask01", [128, 512])
    ident_d = din("ident", [128, 128])
    yb = nc.dram_tensor("yb", [NK, 256], F32, kind="ExternalOutput").ap()

    xr = dscr("xr", [3, 128, NPAD], F32)
    xc = dscr("xc", [3, 128, NK], BF16)
    o_d = [dscr("o_d%d" % d, [NK, 128], F32) for d in (0, 1)]
    y_d = [dscr("y_d%d" % d, [NK, 128], F32) for d in (0, 1)]
    gz_d = dscr("gz_d", [NK, 256], BF16)

    nlb = nlat // 512
    blocks = [(0, 256, 1)] + [(NCTX + i * 512, 512, 0) for i in range(nlb)]

    with ExitStack() as es:
        cx = Ctx(nc, es)
        S = cx.S
        (modT, k_modT), _ = emit_mod(cx, es, w_mod, b_mod, c2, need=[0, 1], gates=[])
        (Gmix, k_Gmix), sh_mix = emit_gcols(cx, es, modT, k_modT, norm_mix, 0, 1, "mixb")
        identf, k_identf = cx.sb(es, "identf", [128, 128])
        ident, k_id = cx.sb(es, "ident", [128, 128], BF16)
        S.dma("sp", identf[:], ident_d[:, :], writes=[k_identf])
        S.op("dve", lambda E: E.tensor_copy(out=ident[:], in_=identf[:]), reads=[k_identf], writes=[k_id])
        cast_ring = Ring(["act", "dve", "pool"])
        w_sb, k_w = cx.sb(es, "w_rec_sb", [128, 8, NCOLB], BF16)
        with ExitStack() as tmpst:
            stage_ring = Ring([cx.sb(tmpst, "wstageb", [128, 2048]) for _ in range(2)])
            wk_rec = load_weight_bf16(cx, w_sb, k_w, w_rec, 8, NCOLB, stage_ring, cast_ring)
            cx.barrier()
        wdt, k_wdt = cx.sb(es, "wdt", [128, 8, 4, 128], BF16)
        for kc in range(8):
            S.op("dve", lambda E: E.tensor_copy(out=wdt[:, kc], in_=w_sb[:, kc, C_DT:C_DT + 4].unsqueeze(2).broadcast_to([128, 4, 128])),
                 reads=wk_rec, writes=[k_wdt + "_%d" % kc])
        wk_dt = [k_wdt + "_%d" % kc for kc in range(8)]
        blt, k_bl = cx.sb(es, "blt", [128, 8])
        S.dma("sp", blt[:, 0:4], bl[:, :], writes=[k_bl])
        for d in (0, 1):
            S.op("dve", lambda E: E.tensor_tensor(out=blt[:, 4 + 2 * d:5 + 2 * d], in0=blt[:, 2 * d + 1:2 * d + 2],
                                                  in1=blt[:, 2 * d:2 * d + 1], op=ALU.subtract), reads=[k_bl], writes=[k_bl])
            S.op("act", lambda E: E.activation(out=blt[:, 4 + 2 * d:5 + 2 * d], in_=blt[:, 4 + 2 * d:5 + 2 * d], func=AF.Sigmoid),
                 reads=[k_bl], writes=[k_bl])
            S.op("dve", lambda E: E.tensor_scalar(out=blt[:, 5 + 2 * d:6 + 2 * d], in0=blt[:, 4 + 2 * d:5 + 2 * d], scalar1=-1.0,
                                                  scalar2=1.0, op0=ALU.mult, op1=ALU.add), reads=[k_bl], writes=[k_bl])
        onb, k_onb = cx.sb(es, "onb", [64, 128])
        S.dma("sp", onb[:], out_norm.partition_broadcast(64), writes=[k_onb])
        onb128 = cx.sb(es, "onb128", [128, 128])
        S.dma("sp", onb128[0][:], out_norm.partition_broadcast(128), writes=[onb128[1]])
        skb, k_skb = cx.sb(es, "skb", [64, 128])
        S.dma("sp", skb[:], skipb.partition_broadcast(64), writes=[k_skb])
        cwt, k_cw = cx.sb(es, "cwt", [128, 15])
        cbt, k_cb = cx.sb(es, "cbt", [128, 3])
        S.dma("sp", cwt[:], cw[:, :], writes=[k_cw])
        S.dma("sp", cbt[:], cb[:, :], writes=[k_cb])
        ad, k_ad = cx.sb(es, "ad", [128, 12])
        S.dma("sp", ad[:, 0:4], alog4.partition_broadcast(128), writes=[k_ad])
        S.dma("sp", ad[:, 4:8], dtb4.partition_broadcast(128), writes=[k_ad + "b"])
        S.op("act", lambda E: E.activation(out=ad[:, 8:12], in_=ad[:, 0:4], func=AF.Exp), reads=[k_ad], writes=[k_ad + "c"])
        S.op("dve", lambda E: E.tensor_scalar(out=ad[:, 8:12], in0=ad[:, 8:12], scalar1=-1.0, scalar2=None, op0=ALU.mult),
             reads=[k_ad + "c"], writes=[k_ad + "c"])
        k_aneg, k_dtb = k_ad + "c", k_ad + "b"
        mk, k_mk = cx.sb(es, "mk", [64, 4, 64])
        S.dma("sp", mk[:], masks[:, :, :], writes=[k_mk])
        m01, k_m01 = cx.sb(es, "m01", [128, 512])
        S.dma("sp", m01[:], mask01[:, :], writes=[k_m01])
        cx.barrier()

        wkn = {
            "ss": cx.sb(es, "ss", [128, 4]), "rs": cx.sb(es, "rs", [128, 4]),
            "junk": cx.sb(es, "junk", [128, D], BF16), "xn": cx.sb(es, "xn", [128, 4, D], BF16),
            "ptr": Ring([cx.ps(es, "ptr", [128, 8, 128], BF16) for _ in range(2)]),
            "tmpT": Ring([cx.sb(es, "tmpT", [128, 8, 128]) for _ in range(1)]),
        }
        xt_r = Ring([cx.sb(es, "xt", [128, 4, D]) for _ in range(2)])
        uT_r = Ring([cx.sb(es, "uT", [128, 8, 512], BF16) for _ in range(2)])

        def load_x(bi):
            t0, NT, n = blocks[bi]
            xt, k_xt = xt_r.next()
            S.dma("sp", xt[:, 0:NT // 128, :], hs[t0:t0 + NT, :].rearrange("(j p) d -> p j d", p=128), writes=[k_xt])
            return xt, k_xt

        with ExitStack() as st:
            pg = Ring([cx.ps(st, "pg0", [128, 512]) for _ in range(3)])
            xo_r = Ring([cx.sb(st, "xo", [128, 512]) for _ in range(3)])
            zt, k_zt = cx.sb(st, "zt", [128, 4])
            S.op("pool", lambda E: E.memset(zt[:], 0.0), writes=[k_zt])
            for grp in range(3):
                for off in (0, OFF_CTX + NCTX, OFF_LAT + nlat):
                    w_ = 2 if off != OFF_CTX + NCTX else 4
                    S.dma("sp", xr[grp, :, off:off + w_], zt[:, 0:w_], reads=[k_zt], writes=[], semkey="st_zt")
            nxt = load_x(0)
            for bi, (t0, NT, n) in enumerate(blocks):
                xt, k_xt = nxt
                if bi + 1 < len(blocks):
                    nxt = load_x(bi + 1)
                uT, k_uT = uT_r.next()
                uk = emit_norm_T(cx, xt, k_xt, NT // 128, Gmix, k_Gmix, sh_mix, k_modT, n, uT, k_uT, ident, k_id, wkn)
                po = (OFF_CTX + t0) if n == 1 else (OFF_LAT + t0 - NCTX)
                for grp, c0 in enumerate((C_X, C_B, C_C)):
                    ps_, k_ps = pg.next()
                    fns = [lambda E, kc=kc: E.matmul(ps_[:, :NT], lhsT=w_sb[:, kc, c0:c0 + 128], rhs=uT[:, kc, :NT],
                                                      start=(kc == 0), stop=(kc == 7)) for kc in range(8)]
                    S.group("pe", fns, reads=wk_rec + uk, writes=[k_ps])
                    xo, k_xo = xo_r.next()
                    S.op("act", lambda E: E.copy(out=xo[:, :NT], in_=ps_[:, :NT]), reads=[k_ps], writes=[k_xo])
                    S.dma("sp", xr[grp, :, po:po + NT], xo[:, :NT], reads=[k_xo], writes=[], semkey="st_" + k_xo)
            cx.barrier()
            win_r = Ring([cx.sb(st, "win", [128, 516]) for _ in range(3)])
            acc_r = Ring([cx.sb(st, "cacc", [128, 512]) for _ in range(2)])
            co_r = Ring([cx.sb(st, "cout", [128, 512], BF16) for _ in range(3)])
            for bi, (t0, NT, n) in enumerate(blocks):
                po = (OFF_CTX + t0) if n == 1 else (OFF_LAT + t0 - NCTX)
                for grp in range(3):
                    win, k_win = win_r.next()
                    S.dma("sp", win[:, 0:NT + 4], xr[grp, :, po - 2:po + NT + 2], writes=[k_win])
                    acc, k_acc = acc_r.next()
                    S.op("dve", lambda E: E.tensor_scalar(out=acc[:, :NT], in0=win[:, 0:NT], scalar1=cwt[:, grp * 5:grp * 5 + 1],
                                                          scalar2=cbt[:, grp:grp + 1], op0=ALU.mult, op1=ALU.add),
                         reads=[k_win, k_cw, k_cb], writes=[k_acc])
                    for j in range(1, 5):
                        S.op("dve", lambda E: E.scalar_tensor_tensor(out=acc[:, :NT], in0=win[:, j:j + NT],
                                                                     scalar=cwt[:, grp * 5 + j:grp * 5 + j + 1], in1=acc[:, :NT],
                                                                     op0=ALU.mult, op1=ALU.add),
                             reads=[k_win, k_cw, k_acc], writes=[k_acc])
                    co, k_co = co_r.next()
                    S.op("act", lambda E: E.activation(out=co[:, :NT], in_=acc[:, :NT], func=AF.Silu), reads=[k_acc], writes=[k_co])
                    S.dma("sp", xc[grp, :, t0:t0 + NT], co[:, :NT], reads=[k_co], writes=[], semkey="st_" + k_co)
            cx.barrier()
        build_B_sweeps(cx, locals())
    return nc


def build_B_sweeps(cx, L):
    nc, S = cx.nc, cx.S
    g = lambda n: L[n]
    blocks, hs, xc, o_d, y_d, gz_d, yb = g("blocks"), g("hs"), g("xc"), g("o_d"), g("y_d"), g("gz_d"), g("yb")
    w_sb, wk_rec, wdt, wk_dt = g("w_sb"), g("wk_rec"), g("wdt"), g("wk_dt")
    blt, k_bl, ad, k_aneg, k_dtb = g("blt"), g("k_bl"), g("ad"), g("k_aneg"), g("k_dtb")
    mk, k_mk, m01, k_m01 = g("mk"), g("k_mk"), g("m01"), g("k_m01")
    skb, k_skb, onb, k_onb = g("skb"), g("k_skb"), g("onb"), g("k_onb")
    ident, k_id, identf, k_identf = g("ident"), g("k_id"), g("identf"), g("k_identf")
    Gmix, k_Gmix, sh_mix, k_modT, wkn, uT_r, load_x = g("Gmix"), g("k_Gmix"), g("sh_mix"), g("k_modT"), g("wkn"), g("uT_r"), g("load_x")
    NK = g("NK")

    with ExitStack() as st:
        def T(name, shape, dt=F32, n=1):
            return Ring([cx.sb(st, name, shape, dt) for _ in range(n)])
        pbig = Ring([cx.ps(st, "pbig", [128, 512]) for _ in range(1)])
        psm = Ring([cx.ps(st, "psm", [128, 512]) for _ in range(2)])
        psb = Ring([cx.ps(st, "psb", [128, 512]) for _ in range(2)])
        ptq = cx.ps(st, "ptq", [128, 512])

        def sweep(d):
            xT_r, BT_r, CT_r = T("xT", [128, 512], BF16, 1), T("BT", [128, 512], BF16, 1), T("CT", [128, 512], BF16, 1)
            f_t, lf_t, k_t, bc_t, A_t, dA_t, e_t, q_t = (T(nm, [128, 512]) for nm in ("f_t", "lf_t", "k_t", "bc_t", "A_t", "dA_t", "e_t", "q_t"))
            qtl_t, ktl_t, qh_t = T("qtl", [128, 512], BF16), T("ktl", [128, 512], BF16), T("qh", [128, 512], BF16)
            ksc_t, dtot_t = T("ksc", [128, 8]), T("dtot", [128, 8])
            dt_t = [T("dt%d" % h, [128, 512]) for h in (0, 1)]
            a_t = [T("a%d" % h, [128, 512]) for h in (0, 1)]
            As_t = [T("As%d" % h, [128, 512]) for h in (0, 1)]
            eA_t = [T("eA%d" % h, [128, 512]) for h in (0, 1)]
            wd_t = [T("wd%d" % h, [128, 512]) for h in (0, 1)]
            et_t = [T("et%d" % h, [128, 8]) for h in (0, 1)]
            v_r, x_r, B_r, kk_r = (T(nm, [64, 128], BF16, 3) for nm in ("v_c", "x_c", "B_c", "k_c"))
            xs_r = T("xs_c", [64, 128], F32, 2)
            gz_r = T("gz_c", [64, 256], BF16, 2)
            cols_r = T("cols", [64, 8], F32, 3)
            attm_r = T("attm", [64, 64], BF16, 3)
            oc_r, yc_r = T("o_c", [64, 128], F32, 2), T("y_c", [64, 128], F32, 2)
            cb_r = T("cb_c", [64, 64], F32, 2)
            dm_r, dec_r = T("dm", [64, 64], F32, 2), T("dec", [64, 64], F32, 2)
            wt_r = T("wt", [64, 64], BF16, 6)
            ytmp_r = T("ytmp", [64, 64], F32, 2)
            xw_r = T("xw", [64, 64], BF16, 6)
            Sst, k_S = cx.sb(st, "Sst", [128, 128])
            Sbf, k_Sbf = cx.sb(st, "Sbf", [128, 128], BF16)
            tS_r = T("tS", [128, 128], F32, 2)
            hst = [cx.sb(st, "hst%d" % h, [128, 64]) for h in (0, 1)]
            hbf = [cx.sb(st, "hbf%d" % h, [128, 64], BF16) for h in (0, 1)]


            S.op("pool", lambda E: E.memset(Sst[:], 0.0), writes=[k_S])
            S.op("pool", lambda E: E.memset(Sbf[:], 0.0), writes=[k_Sbf])
            for h in (0, 1):
                S.op("pool", lambda E: E.memset(hst[h][0][:], 0.0), writes=[hst[h][1]])
                S.op("pool", lambda E: E.memset(hbf[h][0][:], 0.0), writes=[hbf[h][1]])
            order = list(range(len(blocks))) if d == 0 else [0] + list(range(len(blocks) - 1, 0, -1))
            for oi, bi in enumerate(order):
                t0, NT, n = blocks[bi]
                nch = NT // 64
                xt, k_xt = load_x(bi)
                uT, k_uT = uT_r.next()
                uk = emit_norm_T(cx, xt, k_xt, NT // 128, Gmix, k_Gmix, sh_mix, k_modT, n, uT, k_uT, ident, k_id, wkn)
                (xT, k_xT), (BT, k_BT), (CT, k_CT) = xT_r.next(), BT_r.next(), CT_r.next()
                S.dma("sp", xT[:, :NT], xc[0, :, t0:t0 + NT], writes=[k_xT])
                S.dma("sp", BT[:, :NT], xc[1, :, t0:t0 + NT], writes=[k_BT])
                S.dma("sp", CT[:, :NT], xc[2, :, t0:t0 + NT], writes=[k_CT])

                def fm(c0, wt_=None, wkeys=None):
                    ps_, k_ps = pbig.next()
                    if wt_ is None:
                        fns = [lambda E, kc=kc: E.matmul(ps_[:, :NT], lhsT=w_sb[:, kc, c0:c0 + 128], rhs=uT[:, kc, :NT],
                                                          start=(kc == 0), stop=(kc == 7)) for kc in range(8)]
                        S.group("pe", fns, reads=wk_rec + uk, writes=[k_ps])
                    else:
                        fns = [lambda E, kc=kc: E.matmul(ps_[:, :NT], lhsT=wdt[:, kc, c0, :], rhs=uT[:, kc, :NT],
                                                          start=(kc == 0), stop=(kc == 7)) for kc in range(8)]
                        S.group("pe", fns, reads=wk_dt + uk, writes=[k_ps])
                    return ps_, k_ps

                v3 = lambda ap: ap[:, :NT].rearrange("p (c t) -> p c t", t=64)
                (f_, k_f), (lf, k_lf), (k_, k_k), (bc, k_bc), (A_, k_A), (dA, k_dA), (e_, k_e), (q_, k_q) = (
                    r.next() for r in (f_t, lf_t, k_t, bc_t, A_t, dA_t, e_t, q_t))
                (qtl, k_qtl), (ktl, k_ktl), (qh, k_qh) = qtl_t.next(), ktl_t.next(), qh_t.next()
                (ksc, k_ksc), (dtot, k_dtot) = ksc_t.next(), dtot_t.next()
                ps_, k_ps = fm(C_FF if d == 0 else C_FB)
                S.op("act", lambda E: E.activation(out=f_[:, :NT], in_=ps_[:, :NT], func=AF.Sigmoid), reads=[k_ps], writes=[k_f])
                S.op("dve", lambda E: E.tensor_scalar(out=f_[:, :NT], in0=f_[:, :NT], scalar1=blt[:, 5 + 2 * d:6 + 2 * d],
                                                      scalar2=blt[:, 4 + 2 * d:5 + 2 * d], op0=ALU.mult, op1=ALU.add),
                     reads=[k_f, k_bl], writes=[k_f])
                S.op("act", lambda E: E.activation(out=lf[:, :NT], in_=f_[:, :NT], func=AF.Ln), reads=[k_f], writes=[k_lf])
                S.op("dve", lambda E: E.tensor_scalar(out=k_[:, :NT], in0=f_[:, :NT], scalar1=-1.0, scalar2=1.0, op0=ALU.mult,
                                                      op1=ALU.add), reads=[k_f], writes=[k_k])
                S.op("dve", lambda E: E.tensor_tensor_scan(out=bc[:, :NT], data0=m01[:, :NT], data1=lf[:, :NT], initial=0.0,
                                                           op0=ALU.mult, op1=ALU.add), reads=[k_m01, k_lf], writes=[k_bc])
                btot = v3(bc)[:, :, 63:64]
                if d == 0:
                    A3, k_Ax = v3(bc), k_bc
                    Af = bc
                else:
                    S.op("dve", lambda E: E.tensor_tensor(out=A_[:, :NT], in0=lf[:, :NT], in1=bc[:, :NT], op=ALU.subtract),
                         reads=[k_lf, k_bc], writes=[k_A])
                    S.op("dve", lambda E: E.tensor_tensor(out=v3(A_), in0=v3(A_), in1=btot.broadcast_to([128, nch, 64]), op=ALU.add),
                         reads=[k_A, k_bc], writes=[k_A])
                    A3, k_Ax = v3(A_), k_A
                    Af = A_
                S.op("dve", lambda E: E.tensor_tensor(out=v3(dA), in0=A3, in1=A3[:, :, 32:33].broadcast_to([128, nch, 64]),
                                                      op=ALU.subtract), reads=[k_Ax], writes=[k_dA])
                ps_, k_ps = fm(C_Q)
                S.op("act", lambda E: E.activation(out=q_[:, :NT], in_=ps_[:, :NT], func=AF.Silu), reads=[k_ps], writes=[k_q])
                S.op("act", lambda E: E.activation(out=e_[:, :NT], in_=dA[:, :NT], func=AF.Exp), reads=[k_dA], writes=[k_e])
                S.op("pool", lambda E: E.tensor_tensor(out=qtl[:, :NT], in0=q_[:, :NT], in1=e_[:, :NT], op=ALU.mult),
                     reads=[k_q, k_e], writes=[k_qtl])
                S.op("act", lambda E: E.activation(out=e_[:, :NT], in_=dA[:, :NT], func=AF.Exp, scale=-1.0), reads=[k_dA], writes=[k_e])
                S.op("pool", lambda E: E.tensor_tensor(out=ktl[:, :NT], in0=k_[:, :NT], in1=e_[:, :NT], op=ALU.mult),
                     reads=[k_k, k_e], writes=[k_ktl])
                S.op("act", lambda E: E.activation(out=e_[:, :NT], in_=Af[:, :NT], func=AF.Exp), reads=[k_Ax], writes=[k_e])
                S.op("pool", lambda E: E.tensor_tensor(out=qh[:, :NT], in0=q_[:, :NT], in1=e_[:, :NT], op=ALU.mult),
                     reads=[k_q, k_e], writes=[k_qh])
                S.op("dve", lambda E: E.tensor_tensor(out=ksc[:, 0:nch].unsqueeze(2), in0=btot, in1=A3[:, :, 32:33], op=ALU.subtract),
                     reads=[k_bc, k_Ax], writes=[k_ksc])
                S.op("act", lambda E: E.activation(out=ksc[:, 0:nch], in_=ksc[:, 0:nch], func=AF.Exp), reads=[k_ksc], writes=[k_ksc])
                S.op("act", lambda E: E.activation(out=dtot[:, 0:nch].unsqueeze(2), in_=btot, func=AF.Exp), reads=[k_bc], writes=[k_dtot])
                hq = []
                for h in (0, 1):
                    j = 2 * d + h
                    (dt, k_dt), (a_, k_a), (As, k_As), (eA, k_eA), (wd, k_wd), (et, k_et) = (
                        r.next() for r in (dt_t[h], a_t[h], As_t[h], eA_t[h], wd_t[h], et_t[h]))
                    ps_, k_ps = fm(j, wdt, wk_dt)
                    S.op("act", lambda E: E.activation(out=dt[:, :NT], in_=ps_[:, :NT], func=AF.Exp, bias=ad[:, 4 + j:5 + j]),
                         reads=[k_ps, k_dtb], writes=[k_dt])
                    S.op("act", lambda E: E.activation(out=dt[:, :NT], in_=dt[:, :NT], func=AF.Ln, bias=1.0), reads=[k_dt], writes=[k_dt])
                    S.op("dve", lambda E: E.tensor_scalar(out=a_[:, :NT], in0=dt[:, :NT], scalar1=ad[:, 8 + j:9 + j], scalar2=None,
                                                          op0=ALU.mult), reads=[k_dt, k_aneg], writes=[k_a])
                    S.op("dve", lambda E: E.tensor_tensor_scan(out=wd[:, :NT], data0=m01[:, :NT], data1=a_[:, :NT], initial=0.0,
                                                               op0=ALU.mult, op1=ALU.add), reads=[k_m01, k_a], writes=[k_wd])
                    atot = v3(wd)[:, :, 63:64]
                    if d == 0:
                        S.op("dve", lambda E: E.tensor_copy(out=As[:, :NT], in_=wd[:, :NT]), reads=[k_wd], writes=[k_As])
                    else:
                        S.op("dve", lambda E: E.tensor_tensor(out=As[:, :NT], in0=a_[:, :NT], in1=wd[:, :NT], op=ALU.subtract),
                             reads=[k_a, k_wd], writes=[k_As])
                        S.op("dve", lambda E: E.tensor_tensor(out=v3(As), in0=v3(As), in1=atot.broadcast_to([128, nch, 64]), op=ALU.add),
                             reads=[k_As, k_wd], writes=[k_As])
                    S.op("act", lambda E: E.activation(out=et[:, 0:nch].unsqueeze(2), in_=atot, func=AF.Exp), reads=[k_wd], writes=[k_et])
                    S.op("act", lambda E: E.activation(out=eA[:, :NT], in_=As[:, :NT], func=AF.Exp), reads=[k_As], writes=[k_eA])
                    S.op("dve", lambda E: E.tensor_tensor(out=v3(wd), in0=v3(As), in1=atot.broadcast_to([128, nch, 64]), op=ALU.subtract),
                         reads=[k_As, k_wd], writes=[k_wd])
                    S.op("act", lambda E: E.activation(out=wd[:, :NT], in_=wd[:, :NT], func=AF.Exp, scale=-1.0), reads=[k_wd], writes=[k_wd])
                    S.op("pool", lambda E: E.tensor_tensor(out=wd[:, :NT], in0=wd[:, :NT], in1=dt[:, :NT], op=ALU.mult),
                         reads=[k_wd, k_dt], writes=[k_wd])
                    hq.append(((dt, k_dt), (As, k_As), (eA, k_eA), (wd, k_wd), (et, k_et)))
                corder = list(range(nch)) if d == 0 else list(range(nch - 1, -1, -1))

                def front(c):
                    cs = slice(c * 64, (c + 1) * 64)
                    tok0 = t0 + c * 64
                    F = {"cs": cs, "tok0": tok0, "c": c}
                    (v_c, k_v), (x_c, k_x), (B_c, k_B), (k_c, k_kc) = v_r.next(), x_r.next(), B_r.next(), kk_r.next()
                    F.update(v=(v_c, k_v), x=(x_c, k_x), B=(B_c, k_B), kc=(k_c, k_kc))
                    ps_, k_ps = psm.next()
                    fns = [lambda E, kc=kc: E.matmul(ps_[0:64, 0:128], lhsT=uT[:, kc, cs], rhs=w_sb[:, kc, C_I:C_I + 128],
                                                      start=(kc == 0), stop=(kc == 7)) for kc in range(8)]
                    S.group("pe", fns, reads=wk_rec + uk, writes=[k_ps])
                    S.op("act", lambda E: E.copy(out=v_c[:], in_=ps_[0:64, 0:128]), reads=[k_ps], writes=[k_v])
                    if d == 0:
                        gz_c, k_gz = gz_r.next()
                        ps_, k_ps = psm.next()
                        fns = [lambda E, kc=kc: E.matmul(ps_[0:64, 0:256], lhsT=uT[:, kc, cs], rhs=w_sb[:, kc, C_G:C_G + 256],
                                                          start=(kc == 0), stop=(kc == 7)) for kc in range(8)]
                        S.group("pe", fns, reads=wk_rec + uk, writes=[k_ps])
                        S.op("act", lambda E: E.activation(out=gz_c[:], in_=ps_[0:64, 0:256], func=AF.Silu), reads=[k_ps], writes=[k_gz])
                        S.dma("sp", gz_d[tok0:tok0 + 64, :], gz_c[:], reads=[k_gz], writes=[], semkey="st_" + k_gz)
                    for (src, k_src, dst, k_dst) in ((xT, k_xT, x_c, k_x), (BT, k_BT, B_c, k_B), (ktl, k_ktl, k_c, k_kc)):
                        ptr, k_ptr = wkn["ptr"].next()
                        S.op("pe", lambda E: E.transpose(out=ptr[0:64, 0, :], in_=src[:, cs], identity=ident[:]),
                             reads=[k_src, k_id], writes=[k_ptr])
                        S.op("dve", lambda E: E.tensor_copy(out=dst[:], in_=ptr[0:64, 0, :]), reads=[k_ptr], writes=[k_dst])
                    cols, k_cols = cols_r.next()
                    F["cols"] = (cols, k_cols)
                    fns = []
                    rk = []
                    for h in (0, 1):
                        (dt, k_dt), (As, k_As), (eA, k_eA), (wd, k_wd), (et, k_et) = hq[h]
                        for qi, (src, k_src) in enumerate(((As, k_As), (dt, k_dt), (wd, k_wd), (eA, k_eA))):
                            fns.append(lambda E, src=src, qi=qi, h=h: E.transpose(out=ptq[0][0:64, (h * 4 + qi) * 64:(h * 4 + qi) * 64 + 64],
                                                                                   in_=src[0:64, cs], identity=identf[0:64, 0:64]))
                            rk.append(k_src)
                    S.group("pe", fns, reads=rk + [k_identf], writes=[ptq[1]])
                    S.op("dve", lambda E: E.tensor_copy(out=cols[:].unsqueeze(2),
                                                        in_=ptq[0][0:64, :].rearrange("p (q t) -> p q t", t=64)[:, :, 0:1]),
                         reads=[ptq[1]], writes=[k_cols])
                    ps_a, k_psa = psm.next()
                    S.op("pe", lambda E: E.matmul(ps_a[0:64, 0:64], lhsT=ktl[:, cs], rhs=qtl[:, cs], start=True, stop=True),
                         reads=[k_ktl, k_qtl], writes=[k_psa])
                    attm, k_attm = attm_r.next()
                    F["attm"] = (attm, k_attm)
                    S.op("dve", lambda E: E.tensor_tensor(out=attm[:], in0=ps_a[0:64, 0:64], in1=mk[:, d, :], op=ALU.mult),
                         reads=[k_psa, k_mk], writes=[k_attm])
                    ps_cb, k_pscb = psm.next()
                    S.op("pe", lambda E: E.matmul(ps_cb[0:64, 0:64], lhsT=BT[:, cs], rhs=CT[:, cs], start=True, stop=True),
                         reads=[k_BT, k_CT], writes=[k_pscb])
                    cb_c, k_cbc = cb_r.next()
                    S.op("act", lambda E: E.copy(out=cb_c[:], in_=ps_cb[0:64, 0:64]), reads=[k_pscb], writes=[k_cbc])
                    F["wt"], F["xw"] = [], []
                    for h in (0, 1):
                        (dt, k_dt), (As, k_As), (eA, k_eA), (wd, k_wd), (et, k_et) = hq[h]
                        cA, cdt, cwd, ceA = (cols[:, h * 4 + i:h * 4 + i + 1] for i in range(4))
                        dm, k_dm = dm_r.next()
                        S.op("dve", lambda E: E.scalar_tensor_tensor(out=dm[:], in0=As[0:64, cs], scalar=cA, in1=mk[:, 2 + d, :],
                                                                     op0=ALU.subtract, op1=ALU.add),
                             reads=[k_As, k_cols, k_mk], writes=[k_dm])
                        dec, k_dec = dec_r.next()
                        S.op("act", lambda E: E.activation(out=dec[:], in_=dm[:], func=AF.Exp), reads=[k_dm], writes=[k_dec])
                        wt, k_wt = wt_r.next()
                        S.op("dve", lambda E: E.scalar_tensor_tensor(out=wt[:], in0=dec[:], scalar=cdt, in1=cb_c[:],
                                                                     op0=ALU.mult, op1=ALU.mult),
                             reads=[k_dec, k_cols, k_cbc], writes=[k_wt])
                        xw, k_xw = xw_r.next()
                        S.op("pool", lambda E: E.tensor_scalar(out=xw[:], in0=x_c[:, h * 64:(h + 1) * 64], scalar1=cwd, scalar2=None,
                                                               op0=ALU.mult), reads=[k_x, k_cols], writes=[k_xw])
                        F["wt"].append((wt, k_wt))
                        F["xw"].append((xw, k_xw))
                    return F

                def back(F):
                    cs, tok0, c = F["cs"], F["tok0"], F["c"]
                    (v_c, k_v), (x_c, k_x), (B_c, k_B), (k_c, k_kc) = F["v"], F["x"], F["B"], F["kc"]
                    cols, k_cols = F["cols"]
                    attm, k_attm = F["attm"]
                    ps_o, k_pso = psb.next()
                    S.group("pe", [lambda E: E.matmul(ps_o[0:64, 0:128], lhsT=attm[:], rhs=v_c[:], start=True, stop=False),
                                   lambda E: E.matmul(ps_o[0:64, 0:128], lhsT=qh[:, cs], rhs=Sbf[:], start=False, stop=True)],
                            reads=[k_attm, k_v, k_qh, k_Sbf], writes=[k_pso])
                    o_c, k_oc = oc_r.next()
                    S.op("act", lambda E: E.copy(out=o_c[:], in_=ps_o[0:64, 0:128]), reads=[k_pso], writes=[k_oc])
                    S.dma("sp", o_d[d][tok0:tok0 + 64, :], o_c[:], reads=[k_oc], writes=[], semkey="st_" + k_oc)
                    ps_s, k_pss = psb.next()
                    S.op("pe", lambda E: E.matmul(ps_s[:, 0:128], lhsT=k_c[:], rhs=v_c[:], start=True, stop=True),
                         reads=[k_kc, k_v], writes=[k_pss])
                    tS, k_tS = tS_r.next()
                    S.op("dve", lambda E: E.tensor_scalar(out=tS[:], in0=ps_s[:, 0:128], scalar1=ksc[:, c:c + 1], scalar2=None,
                                                          op0=ALU.mult), reads=[k_pss, k_ksc], writes=[k_tS])
                    S.op("dve", lambda E: E.scalar_tensor_tensor(out=Sst[:], in0=Sst[:], scalar=dtot[:, c:c + 1], in1=tS[:],
                                                                 op0=ALU.mult, op1=ALU.add), reads=[k_S, k_dtot, k_tS], writes=[k_S])
                    S.op("act", lambda E: E.copy(out=Sbf[:], in_=Sst[:]), reads=[k_S], writes=[k_Sbf])
                    y_c, k_yc = yc_r.next()
                    for h in (0, 1):
                        (dt, k_dt), (As, k_As), (eA, k_eA), (wd, k_wd), (et, k_et) = hq[h]
                        ceA = cols[:, h * 4 + 3:h * 4 + 4]
                        wt, k_wt = F["wt"][h]
                        xw, k_xw = F["xw"][h]
                        ps_y, k_psy = psb.next()
                        S.op("pe", lambda E: E.matmul(ps_y[0:64, 0:64], lhsT=wt[:], rhs=x_c[:, h * 64:(h + 1) * 64], start=True, stop=True),
                             reads=[k_wt, k_x], writes=[k_psy])
                        ytmp, k_ytmp = ytmp_r.next()
                        S.op("act", lambda E: E.copy(out=ytmp[:], in_=ps_y[0:64, 0:64]), reads=[k_psy], writes=[k_ytmp])
                        ps_yo, k_psyo = psb.next()
                        S.op("pe", lambda E: E.matmul(ps_yo[0:64, 0:64], lhsT=CT[:, cs], rhs=hbf[h][0][:], start=True, stop=True),
                             reads=[k_CT, hbf[h][1]], writes=[k_psyo])
                        S.op("dve", lambda E: E.scalar_tensor_tensor(out=y_c[:, h * 64:(h + 1) * 64], in0=ps_yo[0:64, 0:64], scalar=ceA,
                                                                     in1=ytmp[:], op0=ALU.mult, op1=ALU.add),
                             reads=[k_psyo, k_cols, k_ytmp], writes=[k_yc + "_%d" % h])
                        ps_st, k_psst = psb.next()
                        S.op("pe", lambda E: E.matmul(ps_st[:, 0:64], lhsT=B_c[:], rhs=xw[:], start=True, stop=True),
                             reads=[k_B, k_xw], writes=[k_psst])
                        S.op("dve", lambda E: E.scalar_tensor_tensor(out=hst[h][0][:], in0=hst[h][0][:], scalar=et[:, c:c + 1],
                                                                     in1=ps_st[:, 0:64], op0=ALU.mult, op1=ALU.add),
                             reads=[hst[h][1], k_et, k_psst], writes=[hst[h][1]])
                        S.op("act", lambda E: E.copy(out=hbf[h][0][:], in_=hst[h][0][:]), reads=[hst[h][1]], writes=[hbf[h][1]])
                    yks = [k_yc + "_0", k_yc + "_1"]
                    if d == 0:
                        xs_, k_xs = xs_r.next()
                        S.op("pool", lambda E: E.tensor_tensor(out=xs_[:], in0=x_c[:], in1=skb[:], op=ALU.mult),
                             reads=[k_x, k_skb], writes=[k_xs])
                        S.op("pool", lambda E: E.tensor_tensor(out=y_c[:], in0=y_c[:], in1=xs_[:], op=ALU.add),
                             reads=yks + [k_xs], writes=yks)
                    S.dma("sp", y_d[d][tok0:tok0 + 64, :], y_c[:], reads=yks, writes=[], semkey="st_" + k_yc)

                Fq = front(corder[0])
                for ci, c in enumerate(corder):
                    Fn = front(corder[ci + 1]) if ci + 1 < len(corder) else None
                    back(Fq)
                    Fq = Fn
                    yield

        gens = [sweep(0), sweep(1)]
        while gens:
            for g_ in list(gens):
                try:
                    next(g_)
                except StopIteration:
                    gens.remove(g_)
        cx.barrier()

        st.close()
        ld = [T("m_of", [128, 128]), T("m_ob", [128, 128]), T("m_yf", [128, 128]), T("m_yb", [128, 128])]
        gzl = T("m_gz", [128, 256], BF16)
        osum, ysum = T("m_os", [128, 128]), T("m_ys", [128, 128])
        sq, st2 = T("m_sq", [128, 128]), T("m_st", [128, 2])
        outt = T("m_out", [128, 256], F32, 2)
        for ti in range(NK // 128):
            rows = slice(ti * 128, (ti + 1) * 128)
            tl = [r.next() for r in ld]
            for (t_, k_t), src in zip(tl, (o_d[0], o_d[1], y_d[0], y_d[1])):
                S.dma("sp", t_[:], src[rows, :], writes=[k_t])
            gzt, k_gzt = gzl.next()
            S.dma("sp", gzt[:], gz_d[rows, :], writes=[k_gzt])
            (os_, k_os), (ys_, k_ys), (sq_, k_sq), (s2, k_s2) = osum.next(), ysum.next(), sq.next(), st2.next()
            ot, k_ot = outt.next()
            S.op("dve", lambda E: E.tensor_tensor(out=os_[:], in0=tl[0][0][:], in1=tl[1][0][:], op=ALU.add),
                 reads=[tl[0][1], tl[1][1]], writes=[k_os])
            S.op("act", lambda E: E.activation(out=sq_[:], in_=os_[:], func=AF.Square, accum_out=s2[:, 0:1]), reads=[k_os], writes=[k_sq, k_s2])
            S.op("act", lambda E: E.activation(out=s2[:, 1:2], in_=s2[:, 0:1], func=AF.Sqrt, scale=1.0 / 128, bias=EPS), reads=[k_s2], writes=[k_s2])
            S.op("dve", lambda E: E.reciprocal(out=s2[:, 1:2], in_=s2[:, 1:2]), reads=[k_s2], writes=[k_s2])
            S.op("dve", lambda E: E.scalar_tensor_tensor(out=os_[:], in0=os_[:], scalar=s2[:, 1:2], in1=gzt[:, 0:128], op0=ALU.mult,
                                                         op1=ALU.mult), reads=[k_os, k_s2, k_gzt], writes=[k_os])
            S.op("pool", lambda E: E.tensor_tensor(out=ot[:, 0:128], in0=os_[:], in1=L["onb128"][0][:], op=ALU.mult),
                 reads=[k_os, L["onb128"][1]], writes=[k_ot + "a"])
            S.op("dve", lambda E: E.tensor_tensor(out=ys_[:], in0=tl[2][0][:], in1=tl[3][0][:], op=ALU.add),
                 reads=[tl[2][1], tl[3][1]], writes=[k_ys])
            S.op("pool", lambda E: E.tensor_tensor(out=ot[:, 128:256], in0=ys_[:], in1=gzt[:, 128:256], op=ALU.mult),
                 reads=[k_ys, k_gzt], writes=[k_ot + "b"])
            S.dma("sp", yb[rows, :], ot[:], reads=[k_ot + "a", k_ot + "b"], writes=[], semkey="st_" + k_ot)
        cx.barrier()


def host_consts_B():
    s = np.arange(64)[:, None]
    t = np.arange(64)[None, :]
    mul_f = (s <= t).astype(np.float32)
    mul_b = (s >= t).astype(np.float32)
    masks = np.stack([mul_f, mul_b, (mul_f - 1.0) * 30000.0, (mul_b - 1.0) * 30000.0], axis=1)
    m01 = np.ones((128, 512), np.float32)
    m01[:, ::64] = 0.0
    return {"masks": np.ascontiguousarray(masks, np.float32), "mask01": m01, "ident": np.eye(128, dtype=np.float32)}


def prep_B(inp, core, h_lat, h_ctx):
    b, g = core // 4, core % 4
    f = lambda a: np.ascontiguousarray(a, dtype=np.float32)
    W = inp["rec_w_in"][0]
    s0, s1, gg = 2 * g, 2 * g + 1, g // 2
    cols = np.concatenate([
        np.arange(0 + g * 128, 0 + (g + 1) * 128), np.arange(512 + g * 128, 512 + (g + 1) * 128),
        np.arange(1024 + g * 128, 1024 + (g + 1) * 128), np.arange(1536 + g * 128, 1536 + (g + 1) * 128),
        np.arange(2048 + g * 128, 2048 + (g + 1) * 128), np.arange(2560 + s0 * 64, 2560 + s0 * 64 + 128),
        np.arange(3072 + s0 * 64, 3072 + s0 * 64 + 128), np.arange(3584 + gg * 128, 3584 + (gg + 1) * 128),
        np.arange(3840 + gg * 128, 3840 + (gg + 1) * 128), np.array([4096 + s0, 4096 + s1, 4096 + 8 + s0, 4096 + 8 + s1])])
    bl_all = inp["hgrn_bound_logits"]
    bl = np.stack([bl_all[0, g * 128:(g + 1) * 128], bl_all[1, g * 128:(g + 1) * 128],
                   bl_all[0, 512 + g * 128:512 + (g + 1) * 128], bl_all[1, 512 + g * 128:512 + (g + 1) * 128]], axis=1)
    xch = [np.arange(s0 * 64, s0 * 64 + 128), 512 + np.arange(gg * 128, (gg + 1) * 128), 768 + np.arange(gg * 128, (gg + 1) * 128)]
    cwm = inp["ssd_conv_w"][0]
    cw = np.concatenate([cwm[:, ch].T for ch in xch], axis=1)
    cb = np.stack([inp["ssd_conv_b"][0][ch] for ch in xch], axis=1)
    al, db = inp["ssd_a_log"][0], inp["ssd_dt_bias"][0]
    m = {
        "hs": f(np.concatenate([h_ctx, h_lat], 0)),
        "c2": f(np.stack([inp["c"][b].reshape(8, 128).T, inp["c_ctx"].reshape(8, 128).T], axis=-1).reshape(128, 16)),
        "w_mod": f(inp["w_mod"][1][:, 0:2048]), "b_mod": f(inp["b_mod"][1][0:2048]), "norm_mix": f(inp["norm_mix"][1]),
        "w_rec": f(W[:, cols]), "bl": f(bl), "out_norm": f(inp["hgrn_out_norm"][0][g * 128:(g + 1) * 128]),
        "cw": f(cw), "cb": f(cb),
        "alog4": f(np.array([al[0, s0], al[0, s1], al[1, s0], al[1, s1]])),
        "dtb4": f(np.array([db[0, s0], db[0, s1], db[1, s0], db[1, s1]])),
        "skipb": f(np.repeat(inp["ssd_skip"][0][[s0, s1]], 64)),
    }
    m.update(host_consts_B())
    return m


SPEC_C = [("hs", (NQ, D)), ("Y", (NQ, D)), ("c2", (128, 16)), ("w_mod", (D, 4096)), ("b_mod", (4096,)),
          ("norm_mlp", (D,)), ("w_out", (D, D)), ("ssd_g", (512,)), ("w1", (D, 4 * D)), ("w2", (4 * D, D)),
          ("final_norm", (D,)), ("ident", (128, 128))]


def build_C():
    nc = bass.Bass("TRN2", target_bir_lowering=False)
    blob = Blob(nc, SPEC_C)
    out = nc.dram_tensor("out", [NQ, D], F32, kind="ExternalOutput").ap()
    with ExitStack() as es:
        cx = Ctx(nc, es)
        S = cx.S
        (modT, k_modT), gate = emit_mod(cx, es, blob.get("w_mod"), blob.get("b_mod"), blob.get("c2"),
                                        need=[2, 3, 4, 5], gates=[2, 5], i0=2)
        (Gmlp, k_Gmlp), sh_mlp = emit_gcols(cx, es, modT, k_modT, blob.get("norm_mlp"), 3, 4, "mlpc")
        identf, k_identf = cx.sb(es, "identf", [128, 128])
        ident, k_id = cx.sb(es, "ident", [128, 128], BF16)
        S.dma("sp", identf[:], blob.get("ident"), writes=[k_identf])
        S.op("dve", lambda E: E.tensor_copy(out=ident[:], in_=identf[:]), reads=[k_identf], writes=[k_id])
        cx.barrier()
        emit_outproj_mlp(cx, [(None, 8, 128, 0)], blob.get("w_out"), blob.get("hs"), gate, Gmlp, k_Gmlp, sh_mlp, k_modT,
                         blob.get("w1"), blob.get("w2"), out, ident, k_id, final_norm=blob.get("final_norm"),
                         ytok=(blob.get("Y"), blob.get("ssd_g")))
    return nc


def prep_C(inp, core, h_own, h_ctx, Yrows):
    b = core // 4
    f = lambda a: np.ascontiguousarray(a, dtype=np.float32)
    return {
        "hs": f(np.concatenate([h_ctx, h_own], 0)), "Y": f(Yrows),
        "c2": f(np.stack([inp["c"][b].reshape(8, 128).T, inp["c_ctx"].reshape(8, 128).T], axis=-1).reshape(128, 16)),
        "w_mod": f(inp["w_mod"][1][:, 2048:]), "b_mod": f(inp["b_mod"][1][2048:]), "norm_mlp": f(inp["norm_mlp"][1]),
        "w_out": f(inp["rec_w_out"][0]), "ssd_g": f(inp["ssd_norm"][0]), "w1": f(inp["w_mlp_in"][1]),
        "w2": f(inp["w_mlp_out"][1]), "final_norm": f(inp["final_norm"]), "ident": np.eye(128, dtype=np.float32),
    }


_NC_CACHE = {}


def _get_nc(name):
    if name not in _NC_CACHE:
        _NC_CACHE[name] = {"A": build_A, "B": build_B, "C": build_C}[name]()
    return _NC_CACHE[name]


def kernel(**inp):
    inp = {k: np.asarray(v) for k, v in inp.items()}
    cores = list(range(8))
    mapsA = [{"blob": Blob.pack(SPEC_A, prep_A(inp, c))} for c in cores]
    resA = run_bass_kernel_spmd(_get_nc("A"), mapsA, core_ids=cores).results
    hA = [np.asarray(r["h_out"], dtype=np.float32) for r in resA]
    del mapsA
    h_lat = [np.concatenate([hA[b * 4 + q][NCTX:] for q in range(4)], 0) for b in range(2)]
    h_ctx = [hA[b * 4][:NCTX] for b in range(2)]
    sB = spec_B()
    mapsB = [{"blob": Blob.pack(sB, prep_B(inp, c, h_lat[c // 4], h_ctx[c // 4]))} for c in cores]
    resB = run_bass_kernel_spmd(_get_nc("B"), mapsB, core_ids=cores).results
    del mapsB
    Y = []
    for b in range(2):
        Yb = np.empty((NKEY, D), np.float32)
        for g in range(4):
            yb = np.asarray(resB[b * 4 + g]["yb"], dtype=np.float32)
            Yb[:, g * 128:(g + 1) * 128] = yb[:, 0:128]
            Yb[:, 512 + g * 128:512 + (g + 1) * 128] = yb[:, 128:256]
        Y.append(Yb)
    mapsC = []
    for c in cores:
        b, q = c // 4, c % 4
        rows = np.concatenate([Y[b][:NCTX], Y[b][NCTX + q * NOWN:NCTX + (q + 1) * NOWN]], 0)
        mapsC.append({"blob": Blob.pack(SPEC_C, prep_C(inp, c, hA[c][NCTX:], h_ctx[b], rows))})
    resC = run_bass_kernel_spmd(_get_nc("C"), mapsC, core_ids=cores).results
    out = np.empty((2, NLAT, D), np.float32)
    for c in cores:
        b, q = c // 4, c % 4
        out[b, q * NOWN:(q + 1) * NOWN] = np.asarray(resC[c]["out"], dtype=np.float32)[NCTX:]
    return out
```

```python
from contextlib import ExitStack
import math
import os
import numpy as np
import concourse.bass as bass
import concourse.mybir as mybir
from concourse.bass_utils import run_bass_kernel_spmd

F32 = mybir.dt.float32
BF16 = mybir.dt.bfloat16
AF = mybir.ActivationFunctionType
ALU = mybir.AluOpType
AX = mybir.AxisListType


class Sched:
    def __init__(self, nc, es, n_dma_sems=64):
        self.nc = nc
        self.es = es
        self.eng = {"pe": nc.tensor, "act": nc.scalar, "dve": nc.vector, "pool": nc.gpsimd, "sp": nc.sync}
        self.sem = {}
        self.cnt = {}
        for k in ("pe", "act", "dve", "pool"):
            self.sem[k] = es.enter_context(nc.semaphore("sem_" + k))
            self.cnt[k] = 0
        self.seen = {k: {} for k in self.eng}
        self.semobj = {}
        for k in self.sem:
            self.semobj[id(self.sem[k])] = self.sem[k]
        self.w = {}
        self.r = {}
        self.dma_sem = {}
        self.sem_count = {}
        self.free_dma_sems = [es.enter_context(nc.semaphore("dsem%d" % i)) for i in range(n_dma_sems)]
        for s in self.free_dma_sems:
            self.semobj[id(s)] = s
        self.n_wait = 0
        self.n_inst = 0
        self.excl = set()

    def _deps(self, reads, writes):
        deps = {}
        for k in reads:
            ev = self.w.get(k)
            if ev is not None:
                deps[ev[0]] = max(deps.get(ev[0], 0), ev[1])
            if k in self.excl:
                for sid, v in self.r.get(k, {}).items():
                    deps[sid] = max(deps.get(sid, 0), v)
        for k in writes:
            ev = self.w.get(k)
            if ev is not None:
                deps[ev[0]] = max(deps.get(ev[0], 0), ev[1])
            for sid, v in self.r.get(k, {}).items():
                deps[sid] = max(deps.get(sid, 0), v)
        return deps

    def _wait(self, e, deps):
        for sid, v in deps.items():
            if e == "pe" and sid == id(self.sem["pe"]):
                continue
            if self.seen[e].get(sid, 0) < v:
                self.eng[e].wait_ge(self.semobj[sid], v)
                self.seen[e][sid] = v
                self.n_wait += 1

    def _record(self, ev, reads, writes):
        for k in writes:
            self.w[k] = ev
            self.r[k] = {}
        for k in reads:
            d = self.r.setdefault(k, {})
            d[ev[0]] = max(d.get(ev[0], 0), ev[1])

    def op(self, e, fn, reads=(), writes=()):
        self._wait(e, self._deps(reads, writes))
        inst = fn(self.eng[e])
        self.cnt[e] += 1
        inst.then_inc(self.sem[e], 1)
        self._record((id(self.sem[e]), self.cnt[e]), reads, writes)
        self.n_inst += 1
        return inst

    def group(self, e, fns, reads=(), writes=()):
        self._wait(e, self._deps(reads, writes))
        inst = None
        for fn in fns:
            inst = fn(self.eng[e])
            self.n_inst += 1
        self.cnt[e] += 1
        inst.then_inc(self.sem[e], 1)
        self._record((id(self.sem[e]), self.cnt[e]), reads, writes)
        return inst

    def dma(self, q, out, in_, reads=(), writes=(), semkey=None, **kw):
        if semkey is None:
            semkey = writes[0]
        sem = self.dma_sem.get(semkey)
        if sem is None:
            sem = self.free_dma_sems.pop()
            self.dma_sem[semkey] = sem
        self._wait(q, self._deps(reads, writes))
        inst = self.eng[q].dma_start(out=out, in_=in_, **kw)
        self.sem_count[id(sem)] = self.sem_count.get(id(sem), 0) + 16
        inst.then_inc(sem, 16)
        self._record((id(sem), self.sem_count[id(sem)]), reads, writes)
        self.n_inst += 1
        return inst

    def barrier(self):
        evs = {}
        for k in self.sem:
            if self.cnt[k] > 0:
                evs[id(self.sem[k])] = self.cnt[k]
        for sid, c in self.sem_count.items():
            evs[sid] = c
        for e in ("pe", "act", "dve", "pool", "sp"):
            for sid, v in evs.items():
                if self.seen[e].get(sid, 0) < v:
                    self.eng[e].wait_ge(self.semobj[sid], v)
                    self.seen[e][sid] = v
        for k, sem in self.dma_sem.items():
            self.free_dma_sems.append(sem)
        self.dma_sem = {}

    def release_dma_sem(self, semkey):
        pass

    def wait_all(self, e, keys):
        self._wait(e, self._deps(keys, ()))


D = 1024
NCTX = 256
NLAT = 8192
NKEY = NCTX + NLAT
NOWN = 2048
NQ = NCTX + NOWN
EPS = 1e-6
DA_SCALE = 64 ** -0.5
MLA_SCALE = 96 ** -0.5


class Blob:
    def __init__(self, nc, spec):
        self.spec = spec
        self.off = {}
        o = 0
        for name, shape in spec:
            self.off[name] = (o, shape)
            o += int(np.prod(shape))
        self.total = o
        self.ap = nc.dram_tensor("blob", [o], F32, kind="ExternalInput").ap()

    def get(self, name):
        o, shape = self.off[name]
        n = int(np.prod(shape))
        v = self.ap[o:o + n]
        if len(shape) == 1:
            return v
        if len(shape) == 2:
            return v.rearrange("(a b) -> a b", b=shape[1])
        if len(shape) == 3:
            return v.rearrange("(a b c) -> a b c", b=shape[1], c=shape[2])
        raise ValueError(shape)

    @staticmethod
    def pack(spec, m):
        parts = []
        for name, shape in spec:
            a = np.asarray(m[name], dtype=np.float32)
            assert tuple(a.shape) == tuple(shape), (name, a.shape, shape)
            parts.append(a.ravel())
        return np.concatenate(parts)


SPEC_A = [("xs", (NKEY, D)), ("c2", (128, 16)), ("w_mod", (D, 6 * D)), ("b_mod", (6 * D,)), ("norm_mix", (D,)),
          ("norm_mlp", (D,)), ("w_in", (D, 2208)), ("q_norm", (384,)), ("w_uq", (384, 768)), ("kv_norm", (256,)),
          ("w_ukv", (256, 1024)), ("w_out", (D, D)), ("lam", (256,)), ("subnorm", (128,)), ("w1", (D, 4 * D)),
          ("w2", (4 * D, D)), ("cs_da", (128, 2, NKEY)), ("cs_m", (96, 2, NKEY)), ("cs_kr", (32, 2, NKEY)),
          ("perm_da", (128, 128)), ("perm_m", (96, 96)), ("perm_kr", (32, 32)), ("ident", (128, 128))]


def spec_B(nlat=NLAT):
    return [("hs", (NCTX + nlat, D)), ("c2", (128, 16)), ("w_mod", (D, 2048)), ("b_mod", (2048,)), ("norm_mix", (D,)),
            ("w_rec", (D, NCOLB)), ("bl", (128, 4)), ("out_norm", (128,)), ("cw", (128, 15)), ("cb", (128, 3)),
            ("alog4", (4,)), ("dtb4", (4,)), ("skipb", (128,)), ("masks", (64, 4, 64)), ("mask01", (128, 512)),
            ("ident", (128, 128))]


class Ctx:
    def __init__(self, nc, es):
        self.nc = nc
        self.es = es
        self.S = Sched(nc, es)
        self.uid = 0

    def name(self, base):
        self.uid += 1
        return "%s_%d" % (base, self.uid)

    def sb(self, st, base, shape, dt=F32):
        n = self.name(base)
        t = st.enter_context(self.nc.sbuf_tensor(n, list(shape), dt))
        return t, n

    def ps(self, st, base, shape, dt=F32):
        n = self.name(base)
        t = st.enter_context(self.nc.psum_tensor(n, list(shape), dt))
        self.S.excl.add(n)
        return t, n

    def barrier(self):
        self.S.barrier()


class Ring:
    def __init__(self, items):
        self.items = items
        self.i = 0

    def next(self):
        it = self.items[self.i % len(self.items)]
        self.i += 1
        return it


def load_weight_bf16(cx, dst, dst_key, w_dram, kchunks, ncols, stage_ring, cast_ring, row_scale=None,
                     prows=128, col0=0):
    S = cx.S
    CH = 2048
    keys = []
    for kc in range(kchunks):
        dkey = "%s_k%d" % (dst_key, kc)
        keys.append(dkey)
        for c0 in range(0, ncols, CH):
            n = min(CH, ncols - c0)
            stg, skey = stage_ring.next()
            S.dma("sp", stg[:prows, 0:n], w_dram[kc * prows:(kc + 1) * prows, col0 + c0:col0 + c0 + n],
                  writes=[skey])
            e = cast_ring.next()
            if row_scale is None:
                if e == "act":
                    S.op("act", lambda E: E.copy(out=dst[:prows, kc, c0:c0 + n], in_=stg[:prows, 0:n]),
                         reads=[skey], writes=[dkey])
                else:
                    S.op(e, lambda E: E.tensor_copy(out=dst[:prows, kc, c0:c0 + n], in_=stg[:prows, 0:n]),
                         reads=[skey], writes=[dkey])
            else:
                sc, sckey = row_scale
                if e == "act":
                    S.op("act", lambda E: E.activation(out=dst[:prows, kc, c0:c0 + n], in_=stg[:prows, 0:n],
                                                       func=AF.Copy, scale=sc[:prows, kc:kc + 1]),
                         reads=[skey, sckey], writes=[dkey])
                else:
                    S.op(e, lambda E: E.tensor_scalar(out=dst[:prows, kc, c0:c0 + n], in0=stg[:prows, 0:n],
                                                      scalar1=sc[:prows, kc:kc + 1], scalar2=None, op0=ALU.mult),
                         reads=[skey, sckey], writes=[dkey])
    return keys


def emit_mod(cx, st, w_mod, b_mod, c2, need, gates, i0=0):
    S = cx.S
    nc = cx.nc
    modT, k_modT = cx.sb(st, "modT", [128, 48, 2])
    gate = {}
    for i in gates:
        for n in (0, 1):
            gate[(i, n)] = cx.sb(st, "gate%d%d" % (i, n), [128, 1024])
    with ExitStack() as tmp:
        csb, k_csb = cx.sb(tmp, "csb", [128, 8, 2])
        scT, k_scT = cx.sb(tmp, "scT", [128, 8, 2])
        scb, k_scb = cx.sb(tmp, "scb", [128, 2, 8, 128])
        ones, k_ones = cx.sb(tmp, "ones", [1, 128])
        wp = [cx.sb(tmp, "wpiece", [128, 8, 512]) for _ in range(2)]
        br = [cx.sb(tmp, "brow", [1, 512]) for _ in range(2)]
        pss = [cx.ps(tmp, "psm", [128, 512]) for _ in range(2)]
        psg = [cx.ps(tmp, "psg", [128, 512]) for _ in range(2)]
        S.dma("sp", csb[:].rearrange("p k n -> p (k n)"), c2[:, :], writes=[k_csb])
        S.op("act", lambda E: E.activation(out=scT[:], in_=csb[:], func=AF.Silu), reads=[k_csb], writes=[k_scT])
        for n in (0, 1):
            S.op("dve", lambda E: E.tensor_copy(out=scb[:, n], in_=scT[:, :, n:n + 1].broadcast_to([128, 8, 128])),
                 reads=[k_scT], writes=[k_scb])
        S.op("pool", lambda E: E.memset(ones[:], 1.0), writes=[k_ones])
        wv = w_mod.rearrange("(kc p) n -> p kc n", p=128)
        bv = b_mod.rearrange("(o n) -> o n", o=1)
        pi = 0
        ev = Ring(["act", "dve"])
        for i in need:
            for half in (0, 1):
                pc = 2 * i + half
                (wt, k_wt), (bt, k_bt) = wp[pi % 2], br[pi % 2]
                pi += 1
                pcs = pc - 2 * i0
                S.dma("sp", wt[:], wv[:, :, pcs * 512:(pcs + 1) * 512], writes=[k_wt])
                S.dma("sp", bt[:], bv[:, pcs * 512:(pcs + 1) * 512], writes=[k_bt])
                for sub in range(4):
                    cc = pc * 4 + sub
                    ps_, k_ps = pss[cc % 2]
                    fns = [lambda E, kc=kc: E.matmul(ps_[:, 0:2], lhsT=wt[:, kc, sub * 128:(sub + 1) * 128],
                                                      rhs=scT[:, kc, :], start=(kc == 0), stop=False)
                           for kc in range(8)]
                    fns.append(lambda E: E.matmul(ps_[:, 0:2], lhsT=bt[0:1, sub * 128:(sub + 1) * 128],
                                                  rhs=ones[0:1, 0:2], start=False, stop=True))
                    S.group("pe", fns, reads=[k_wt, k_bt, k_scT, k_ones], writes=[k_ps])
                    S.op("act", lambda E: E.copy(out=modT[:, cc, :], in_=ps_[:, 0:2]), reads=[k_ps], writes=[k_modT])
                if i in gates:
                    for n in (0, 1):
                        ps_, k_ps = psg[n]
                        fns = [lambda E, kc=kc: E.matmul(ps_[:, :], lhsT=scb[:, n, kc, :], rhs=wt[:, kc, :],
                                                          start=(kc == 0), stop=False) for kc in range(8)]
                        fns.append(lambda E: E.matmul(ps_[:, :], lhsT=ones[0:1, 0:128], rhs=bt[0:1, :],
                                                      start=False, stop=True))
                        S.group("pe", fns, reads=[k_wt, k_bt, k_scb, k_ones], writes=[k_ps])
                        g, k_g = gate[(i, n)]
                        S.op("dve", lambda E: E.tensor_copy(out=g[:, half * 512:(half + 1) * 512], in_=ps_[:, :]),
                             reads=[k_ps], writes=[k_g])
        cx.barrier()
    return (modT, k_modT), gate


def emit_gcols(cx, st, modT, k_modT, norm_dram, i_shift, i_scale, base):
    S = cx.S
    nT, k_nT = cx.sb(st, base + "_nT", [128, 8])
    G, k_G = cx.sb(st, base + "_G", [128, 8, 2])
    S.dma("sp", nT[:], norm_dram.rearrange("(c p) -> p c", p=128), writes=[k_nT], allow_slow_non_contiguous=True)
    S.op("dve", lambda E: E.tensor_scalar_add(out=G[:], in0=modT[:, i_scale * 8:(i_scale + 1) * 8, :], scalar1=1.0),
         reads=[k_modT], writes=[k_G])
    S.op("dve", lambda E: E.tensor_tensor(out=G[:], in0=G[:], in1=nT[:].unsqueeze(2).broadcast_to([128, 8, 2]),
                                          op=ALU.mult), reads=[k_G, k_nT], writes=[k_G])
    shift = modT[:, i_shift * 8:(i_shift + 1) * 8, :]
    return (G, k_G), shift


def emit_norm_T(cx, xt, k_xt, ntile, G, k_G, shift, k_shift, n, uT, k_uT, ident, k_id, wk):
    S = cx.S
    ss, k_ss = wk["ss"]
    rs, k_rs = wk["rs"]
    junk, k_junk = wk["junk"]
    xn, k_xn = wk["xn"]
    for j in range(ntile):
        S.op("act", lambda E: E.activation(out=junk[:], in_=xt[:, j, :], func=AF.Square, accum_out=ss[:, j:j + 1]),
             reads=[k_xt], writes=[k_junk, k_ss])
    S.op("act", lambda E: E.activation(out=rs[:, 0:ntile], in_=ss[:, 0:ntile], func=AF.Sqrt, scale=1.0 / D,
                                       bias=EPS),
         reads=[k_ss], writes=[k_rs])
    S.op("dve", lambda E: E.reciprocal(out=rs[:, 0:ntile], in_=rs[:, 0:ntile]), reads=[k_rs], writes=[k_rs])
    for j in range(ntile):
        S.op("dve", lambda E: E.tensor_scalar(out=xn[:, j, :], in0=xt[:, j, :], scalar1=rs[:, j:j + 1], scalar2=None,
                                              op0=ALU.mult), reads=[k_xt, k_rs], writes=[k_xn + "_%d" % j])
    for j in range(ntile):
        ptr, k_ptr = wk["ptr"].next()
        fns = [lambda E, kc=kc: E.transpose(out=ptr[:, kc, :], in_=xn[:, j, kc * 128:(kc + 1) * 128], identity=ident[:])
               for kc in range(8)]
        S.group("pe", fns, reads=[k_xn + "_%d" % j, k_id], writes=[k_ptr])
        tmp, k_tmp = wk["tmpT"].next()
        S.op("dve", lambda E: E.tensor_tensor(out=tmp[:], in0=ptr[:], in1=G[:, :, n:n + 1].broadcast_to([128, 8, 128]),
                                              op=ALU.mult), reads=[k_ptr, k_G], writes=[k_tmp])
        S.op("pool", lambda E: E.tensor_tensor(out=uT[:, :, j * 128:(j + 1) * 128], in0=tmp[:],
                                               in1=shift[:, :, n:n + 1].broadcast_to([128, 8, 128]), op=ALU.add),
             reads=[k_tmp, k_shift], writes=[k_uT + "_%d" % j])
    return [k_uT + "_%d" % j for j in range(ntile)]


def rope_tables(npos_order, dim):
    half = dim // 2
    nfreq = dim // 4
    inv = (10000.0 ** (-np.arange(nfreq, dtype=np.float32) / nfreq)).astype(np.float32)
    pos = np.asarray(npos_order)
    valid = pos >= 0
    p = np.where(valid, pos, 0)
    row = (p // 64).astype(np.float32)
    col = (p % 64).astype(np.float32)
    ang = np.concatenate([row[:, None] * inv, col[:, None] * inv], axis=-1).astype(np.float32)
    cos = np.cos(ang).astype(np.float32)
    sin = np.sin(ang).astype(np.float32)
    cos = np.where(valid[:, None], cos, 1.0).astype(np.float32)
    sin = np.where(valid[:, None], sin, 0.0).astype(np.float32)
    C = np.concatenate([cos, cos], axis=1).T
    Ssig = np.concatenate([-sin, sin], axis=1).T
    return np.ascontiguousarray(C), np.ascontiguousarray(Ssig)


def perm_matrix(nblocks, dim, pad_rows=0):
    n = pad_rows + nblocks * dim
    P = np.zeros((n, n), np.float32)
    half = dim // 2
    for b in range(nblocks):
        for j in range(dim):
            P[pad_rows + b * dim + (j + half) % dim, pad_rows + b * dim + j] = 1.0
    return P


def rope_emit(cx, ps_, k_ps, M, N, perm, k_perm, cs, k_cs, wk, out_ap, k_out, psw_ring):
    S = cx.S
    LV = int(os.environ.get("A1_D", "9"))
    if LV < 2:
        return
    kb, k_kb = wk["kb"].next()
    S.op("act", lambda E: E.copy(out=kb[:M, :N], in_=ps_[:M, :N]), reads=[k_ps], writes=[k_kb])
    psw, k_psw = psw_ring.next()
    S.op("pe", lambda E: E.matmul(psw[:M, :N], lhsT=perm[:M, :M], rhs=kb[:M, :N], start=True, stop=True),
         reads=[k_kb, k_perm], writes=[k_psw])
    if LV < 3:
        return
    t1, k_t1 = wk["t1"].next()
    t2, k_t2 = wk["t2"].next()
    S.op("dve", lambda E: E.tensor_tensor(out=t1[:M, :N], in0=ps_[:M, :N], in1=cs[:M, 0, :N], op=ALU.mult),
         reads=[k_ps, k_cs], writes=[k_t1])
    S.op("dve", lambda E: E.tensor_tensor(out=t2[:M, :N], in0=psw[:M, :N], in1=cs[:M, 1, :N], op=ALU.mult),
         reads=[k_psw, k_cs], writes=[k_t2])
    if LV < 4:
        return
    S.op("pool", lambda E: E.tensor_tensor(out=out_ap, in0=t1[:M, :N], in1=t2[:M, :N], op=ALU.add),
         reads=[k_t1, k_t2], writes=[k_out])


def build_A(debug=False, upto=99, nblk=17, nqb=5, nh_da=4, nh_m=8):
    nc = bass.Bass("TRN2", target_bir_lowering=False)
    dbg_kind = "ExternalOutput" if debug else "Internal"

    blob = Blob(nc, SPEC_A)

    def din(name, shape, dt=F32):
        return blob.get(name)

    def dscr(name, shape, dt=BF16):
        return nc.dram_tensor(name, list(shape), dt, kind=dbg_kind).ap()

    xs = din("xs", [NKEY, D])
    c2 = din("c2", [128, 16])
    w_mod = din("w_mod", [D, 6 * D])
    b_mod = din("b_mod", [6 * D])
    norm_mix = din("norm_mix", [D])
    norm_mlp = din("norm_mlp", [D])
    w_in = din("w_in", [D, 2208])
    q_norm = din("q_norm", [384])
    w_uq = din("w_uq", [384, 768])
    kv_norm = din("kv_norm", [256])
    w_ukv = din("w_ukv", [256, 1024])
    w_out = din("w_out", [D, D])
    lam = din("lam", [256])
    subnorm = din("subnorm", [128])
    w1 = din("w1", [D, 4 * D])
    w2 = din("w2", [4 * D, D])
    cs_da_d = din("cs_da", [128, 2, NKEY])
    cs_m_d = din("cs_m", [96, 2, NKEY])
    cs_kr_d = din("cs_kr", [32, 2, NKEY])
    perm_da_d = din("perm_da", [128, 128])
    perm_m_d = din("perm_m", [96, 96])
    perm_kr_d = din("perm_kr", [32, 32])
    ident_d = din("ident", [128, 128])

    kt_da = dscr("kt_da", [4, 128, NKEY])
    v_da = dscr("v_da", [NKEY, 512])
    kt_m = dscr("kt_m", [8, 96, NKEY])
    v_m = dscr("v_m", [NKEY, 8 * 65])
    qt_da = dscr("qt_da", [4, 128, NQ])
    qt_m = dscr("qt_m", [8, 96, NQ])
    yt_da = dscr("yt_da", [4, 128, NQ])
    yt_m = dscr("yt_m", [8, 64, NQ])
    h_out = nc.dram_tensor("h_out", [NQ, D], F32, kind="ExternalOutput").ap()
    modT_dbg = nc.dram_tensor("modT_dbg", [128, 96], F32, kind="ExternalOutput").ap() if debug else None

    with ExitStack() as es:
        cx = Ctx(nc, es)
        S = cx.S
        (modT, k_modT), gate = emit_mod(cx, es, w_mod, b_mod, c2, need=[0, 1, 2, 3, 4, 5], gates=[2, 5])
        (Gmix, k_Gmix), sh_mix = emit_gcols(cx, es, modT, k_modT, norm_mix, 0, 1, "mix")
        (Gmlp, k_Gmlp), sh_mlp = emit_gcols(cx, es, modT, k_modT, norm_mlp, 3, 4, "mlp")
        if debug:
            S.dma("sp", modT_dbg[:, :], modT[:].rearrange("p c n -> p (c n)"), reads=[k_modT], writes=["modT_dbg"])
        identf, k_identf = cx.sb(es, "identf", [128, 128])
        ident, k_id = cx.sb(es, "ident", [128, 128], BF16)
        S.dma("sp", identf[:], ident_d[:, :], writes=[k_identf])
        S.op("dve", lambda E: E.tensor_copy(out=ident[:], in_=identf[:]), reads=[k_identf], writes=[k_id])
        cx.barrier()
        if upto < 1:
            return nc

        with ExitStack() as st:
            stage_ring = Ring([cx.sb(st, "wstage", [128, 2048]) for _ in range(2)])
            cast_ring = Ring(["act", "dve", "pool"])
            w_in_sb, k_w_in = cx.sb(st, "w_in_sb", [128, 8, 2208], BF16)
            wk_in = load_weight_bf16(cx, w_in_sb, k_w_in, w_in, 8, 2208, stage_ring, cast_ring)
            qnT, k_qnT = cx.sb(st, "qnT", [128, 3])
            kvnT, k_kvnT = cx.sb(st, "kvnT", [128, 2])
            S.dma("sp", qnT[:], q_norm.rearrange("(c p) -> p c", p=128), writes=[k_qnT], allow_slow_non_contiguous=True)
            S.dma("sp", kvnT[:], kv_norm.rearrange("(c p) -> p c", p=128), writes=[k_kvnT], allow_slow_non_contiguous=True)
            w_uq_sb, k_w_uq = cx.sb(st, "w_uq_sb", [128, 3, 768], BF16)
            wk_uq = load_weight_bf16(cx, w_uq_sb, k_w_uq, w_uq, 3, 768, stage_ring, cast_ring, row_scale=(qnT, k_qnT))
            w_ukv_sb, k_w_ukv = cx.sb(st, "w_ukv_sb", [128, 2, 1024], BF16)
            wk_ukv = load_weight_bf16(cx, w_ukv_sb, k_w_ukv, w_ukv, 2, 1024, stage_ring, cast_ring,
                                      row_scale=(kvnT, k_kvnT))
            perms = {}
            for nm, dap, m in (("da", perm_da_d, 128), ("m", perm_m_d, 96), ("kr", perm_kr_d, 32)):
                pf, k_pf = cx.sb(st, "permf_" + nm, [m, m])
                pb, k_pb = cx.sb(st, "perm_" + nm, [m, m], BF16)
                S.dma("sp", pf[:], dap[:, :], writes=[k_pf])
                S.op("dve", lambda E: E.tensor_copy(out=pb[:], in_=pf[:]), reads=[k_pf], writes=[k_pb])
                perms[nm] = (pb, k_pb)

            xt_r = Ring([cx.sb(st, "xt", [128, 4, D]) for _ in range(2)])
            csda_r = Ring([cx.sb(st, "csda", [128, 2, 512]) for _ in range(2)])
            csm_r = Ring([cx.sb(st, "csm", [96, 2, 512]) for _ in range(2)])
            cskr_r = Ring([cx.sb(st, "cskr", [32, 2, 512]) for _ in range(2)])
            uT_r = Ring([cx.sb(st, "uT", [128, 8, 512], BF16) for _ in range(2)])
            wk = {
                "ss": cx.sb(st, "ss", [128, 4]), "rs": cx.sb(st, "rs", [128, 4]),
                "junk": cx.sb(st, "junk", [128, D], BF16), "xn": cx.sb(st, "xn", [128, 4, D], BF16),
                "ptr": Ring([cx.ps(st, "ptr", [128, 8, 128], BF16) for _ in range(2)]),
                "tmpT": Ring([cx.sb(st, "tmpT", [128, 8, 128]) for _ in range(2)]),
                "kb": Ring([cx.sb(st, "kb", [128, 512], BF16) for _ in range(2)]),
                "t1": Ring([cx.sb(st, "t1", [128, 512]) for _ in range(2)]),
                "t2": Ring([cx.sb(st, "t2", [128, 512]) for _ in range(2)]),
            }
            pg = Ring([cx.ps(st, "pg", [128, 512]) for _ in range(4)])
            psw_r = Ring([cx.ps(st, "psw", [128, 512]) for _ in range(2)])
            ob_r = Ring([cx.sb(st, "ob", [128, 512], BF16) for _ in range(6)])
            vt_r = Ring([cx.sb(st, "vt", [128, 512], BF16) for _ in range(4)])
            vaug_r = Ring([cx.sb(st, "vaug", [128, 8, 65], BF16) for _ in range(4)])
            for _ in range(4):
                va_, k_va = vaug_r.next()
                S.op("pool", lambda E: E.memset(va_[:], 1.0), writes=[k_va])
            cq_r = Ring([cx.sb(st, "cq", [128, 384]) for _ in range(2)])
            cqn_r = Ring([cx.sb(st, "cqn", [128, 384], BF16) for _ in range(2)])
            cqnT, k_cqnT = cx.sb(st, "cqnT", [128, 3, 512], BF16)
            ckvnT, k_ckvnT = cx.sb(st, "ckvnT", [128, 2, 512], BF16)
            krb, k_krb = cx.sb(st, "krb", [32, 512], BF16)
            s2, k_s2 = cx.sb(st, "s2", [128, 2])
            ev = Ring(["act", "dve"])

            def evac(out_ap, in_ap, reads, writes):
                e = ev.next()
                if e == "act":
                    S.op("act", lambda E: E.copy(out=out_ap, in_=in_ap), reads=reads, writes=writes)
                else:
                    S.op("dve", lambda E: E.tensor_copy(out=out_ap, in_=in_ap), reads=reads, writes=writes)

            blocks = [(0, 256, 1, True)] + [(NCTX + i * 512, 512, 0, i < 4) for i in range(16)]

            def issue_loads(bi):
                t0, NT, n, isq = blocks[bi]
                nt = NT // 128
                xt, k_xt = xt_r.next()
                S.dma("sp", xt[:, 0:nt, :], xs[t0:t0 + NT, :].rearrange("(j p) d -> p j d", p=128), writes=[k_xt])
                a, k_a = csda_r.next()
                S.dma("sp", a[:, :, 0:NT], cs_da_d[:, :, t0:t0 + NT], writes=[k_a])
                b_, k_b = csm_r.next()
                S.dma("sp", b_[:, :, 0:NT], cs_m_d[:, :, t0:t0 + NT], writes=[k_b])
                c_, k_c = cskr_r.next()
                S.dma("sp", c_[:, :, 0:NT], cs_kr_d[:, :, t0:t0 + NT], writes=[k_c])
                return (xt, k_xt), (a, k_a), (b_, k_b), (c_, k_c)

            blocks = blocks[:nblk]
            loaded = {0: issue_loads(0)} if blocks else {}
            for bi, (t0, NT, n, isq) in enumerate(blocks):
                nt = NT // 128
                if bi + 1 < len(blocks):
                    loaded[bi + 1] = issue_loads(bi + 1)
                (xt, k_xt), (csda, k_csda), (csm, k_csm), (cskr, k_cskr) = loaded.pop(bi)
                uT, k_uT = uT_r.next()
                uk = emit_norm_T(cx, xt, k_xt, nt, Gmix, k_Gmix, sh_mix, k_modT, n, uT, k_uT, ident, k_id, wk)
                q0 = 0 if bi == 0 else NCTX + (bi - 1) * 512

                def fm(M, col0, wtile, wkeys, rhs, rkeys, kcs):
                    ps_, k_ps = pg.next()
                    fns = [lambda E, kc=kc: E.matmul(ps_[:M, :NT], lhsT=wtile[:, kc, col0:col0 + M], rhs=rhs[:, kc, :NT],
                                                      start=(kc == 0), stop=(kc == kcs - 1)) for kc in range(kcs)]
                    S.group("pe", fns, reads=wkeys + rkeys, writes=[k_ps])
                    return ps_, k_ps

                SKIP = os.environ.get('A1_SKIP', '')
                for h in range(4 if 'd' not in SKIP else 0):
                    ps_, k_ps = fm(128, 512 + h * 128, w_in_sb, wk_in, uT, uk, 8)
                    ob, k_ob = ob_r.next()
                    rope_emit(cx, ps_, k_ps, 128, NT, perms["da"][0], perms["da"][1], csda, k_csda, wk, ob[:, :NT], k_ob, psw_r)
                    if int(os.environ.get("A1_D", "9")) >= 5:
                        S.dma("sp", kt_da[h, :, t0:t0 + NT], ob[:, :NT], reads=[k_ob], writes=[], semkey="st_" + k_ob)
                    if isq:
                        ps_, k_ps = fm(128, h * 128, w_in_sb, wk_in, uT, uk, 8)
                        ob, k_ob = ob_r.next()
                        rope_emit(cx, ps_, k_ps, 128, NT, perms["da"][0], perms["da"][1], csda, k_csda, wk, ob[:, :NT], k_ob, psw_r)
                        if int(os.environ.get("A1_D", "9")) >= 5:
                            S.dma("sp", qt_da[h, :, q0:q0 + NT], ob[:, :NT], reads=[k_ob], writes=[], semkey="st_" + k_ob)
                if 'k' not in SKIP:
                  ps_, k_ps = fm(32, 2176, w_in_sb, wk_in, uT, uk, 8)
                  rope_emit(cx, ps_, k_ps, 32, NT, perms["kr"][0], perms["kr"][1], cskr, k_cskr, wk, krb[:, :NT], k_krb, psw_r)
                for j in range(nt if 'v' not in SKIP else 0):
                    ps_, k_ps = pg.next()
                    fns = [lambda E, kc=kc: E.matmul(ps_[:, :512], lhsT=uT[:, kc, j * 128:(j + 1) * 128],
                                                      rhs=w_in_sb[:, kc, 1024:1536], start=(kc == 0), stop=(kc == 7))
                           for kc in range(8)]
                    S.group("pe", fns, reads=wk_in + [uk[j]], writes=[k_ps])
                    vt, k_vt = vt_r.next()
                    evac(vt[:, :], ps_[:, :512], [k_ps], [k_vt])
                    S.dma("sp", v_da[t0 + j * 128:t0 + (j + 1) * 128, :], vt[:, :], reads=[k_vt], writes=[], semkey="st_" + k_vt)
                    for which in (("ckv", 1920, 256, ckvnT, k_ckvnT, 2),) + ((("cq", 1536, 384, cqnT, k_cqnT, 3),) if isq else ()):
                        nm, c0, W, dstT, k_dstT, nch = which
                        ps_, k_ps = pg.next()
                        fns = [lambda E, kc=kc: E.matmul(ps_[:, :W], lhsT=uT[:, kc, j * 128:(j + 1) * 128],
                                                          rhs=w_in_sb[:, kc, c0:c0 + W], start=(kc == 0), stop=(kc == 7))
                               for kc in range(8)]
                        S.group("pe", fns, reads=wk_in + [uk[j]], writes=[k_ps])
                        cq, k_cq = cq_r.next()
                        S.op("act", lambda E: E.activation(out=cq[:, :W], in_=ps_[:, :W], func=AF.Square,
                                                           accum_out=s2[:, 0:1]), reads=[k_ps], writes=[k_cq, k_s2])
                        S.op("act", lambda E: E.activation(out=s2[:, 1:2], in_=s2[:, 0:1], func=AF.Sqrt, scale=1.0 / W,
                                                           bias=EPS), reads=[k_s2], writes=[k_s2])
                        S.op("dve", lambda E: E.reciprocal(out=s2[:, 1:2], in_=s2[:, 1:2]), reads=[k_s2], writes=[k_s2])
                        cqn, k_cqn = cqn_r.next()
                        S.op("dve", lambda E: E.tensor_scalar(out=cqn[:, :W], in0=ps_[:, :W], scalar1=s2[:, 1:2],
                                                              scalar2=None, op0=ALU.mult),
                             reads=[k_ps, k_s2], writes=[k_cqn])
                        ptr, k_ptr = wk["ptr"].next()
                        fns = [lambda E, kc=kc: E.transpose(out=ptr[:, kc, :], in_=cqn[:, kc * 128:(kc + 1) * 128],
                                                             identity=ident[:]) for kc in range(nch)]
                        S.group("pe", fns, reads=[k_cqn, k_id], writes=[k_ptr])
                        evac(dstT[:, 0:nch, j * 128:(j + 1) * 128], ptr[:, 0:nch, :], [k_ptr], [k_dstT + "_%d" % j])
                ckv_keys = [k_ckvnT + "_%d" % j for j in range(nt)]
                cq_keys = [k_cqnT + "_%d" % j for j in range(nt)]
                for j in range(nt if 'm' not in SKIP else 0):
                    ps_, k_ps = pg.next()
                    fns = [lambda E, kc=kc: E.matmul(ps_[:, :512], lhsT=ckvnT[:, kc, j * 128:(j + 1) * 128],
                                                      rhs=w_ukv_sb[:, kc, :].rearrange("p (h t d) -> p h t d", h=8, t=2)[:, :, 1, :],
                                                      start=(kc == 0), stop=(kc == 1)) for kc in range(2)]
                    S.group("pe", fns, reads=wk_ukv + [ckv_keys[j]], writes=[k_ps])
                    va_, k_va = vaug_r.next()
                    evac(va_[:, :, 0:64], ps_[:, :512].rearrange("p (h d) -> p h d", h=8), [k_ps], [k_va])
                    S.dma("sp", v_m[t0 + j * 128:t0 + (j + 1) * 128, :], va_[:].rearrange("p h d -> p (h d)"),
                          reads=[k_va], writes=[], semkey="st_" + k_va)
                for h in range(8 if 'h' not in SKIP else 0):
                    ps_, k_ps = fm(64, h * 128, w_ukv_sb, wk_ukv, ckvnT, ckv_keys, 2)
                    ob, k_ob = ob_r.next()
                    evac(ob[0:64, :NT], ps_[0:64, :NT], [k_ps], [k_ob])
                    S.op("act", lambda E: E.copy(out=ob[64:96, :NT], in_=krb[0:32, :NT]), reads=[k_krb], writes=[k_ob])
                    S.dma("sp", kt_m[h, :, t0:t0 + NT], ob[0:96, :NT], reads=[k_ob], writes=[], semkey="st_" + k_ob)
                    if isq:
                        ps_, k_ps = fm(96, h * 96, w_uq_sb, wk_uq, cqnT, cq_keys, 3)
                        ob, k_ob = ob_r.next()
                        rope_emit(cx, ps_, k_ps, 96, NT, perms["m"][0], perms["m"][1], csm, k_csm, wk, ob[0:96, :NT], k_ob, psw_r)
                        S.dma("sp", qt_m[h, :, q0:q0 + NT], ob[0:96, :NT], reads=[k_ob], writes=[], semkey="st_" + k_ob)
            cx.barrier()
        if upto >= 2:
            build_A2(cx, locals())
    return nc


def build_A2(cx, L):
    nc, S = cx.nc, cx.S
    kt_da, v_da, qt_da, yt_da = L["kt_da"], L["v_da"], L["qt_da"], L["yt_da"]
    kt_m, v_m, qt_m, yt_m = L["kt_m"], L["v_m"], L["qt_m"], L["yt_m"]
    nqb = L.get("nqb", 5)
    qblocks = ([(0, 256, 2)] + [(NCTX + i * 512, 512, NKEY // 128) for i in range(4)])[:nqb]
    lam_init = 0.8 - 0.6 * math.exp(-0.3 * 0)

    with ExitStack() as st:
        onesb, k_onesb = cx.sb(st, "onesb", [128, 128], BF16)
        onesm, k_onesm = cx.sb(st, "onesm", [128, 128], BF16)
        S.op("pool", lambda E: E.memset(onesb[:], 1.0), writes=[k_onesb])
        S.op("pool", lambda E: E.memset(onesm[:], 1.0 / 128), writes=[k_onesm])
        lamb, k_lamb = cx.sb(st, "lamb", [128, 256])
        lw, k_lw = cx.sb(st, "lw", [128, 8])
        S.dma("sp", lamb[:], L["lam"].partition_broadcast(128), writes=[k_lamb])
        prod, k_prod = cx.sb(st, "lprod", [128, 2, 64])
        lam4 = lamb[:].rearrange("p (a d) -> p a d", a=4)
        S.op("dve", lambda E: E.tensor_tensor(out=prod[:, 0, :], in0=lam4[:, 0, :], in1=lam4[:, 1, :], op=ALU.mult),
             reads=[k_lamb], writes=[k_prod])
        S.op("dve", lambda E: E.tensor_tensor(out=prod[:, 1, :], in0=lam4[:, 2, :], in1=lam4[:, 3, :], op=ALU.mult),
             reads=[k_lamb], writes=[k_prod])
        S.op("dve", lambda E: E.reduce_sum(out=lw[:, 0:2], in_=prod[:], axis=AX.X), reads=[k_prod], writes=[k_lw])
        S.op("act", lambda E: E.activation(out=lw[:, 2:4], in_=lw[:, 0:2], func=AF.Exp), reads=[k_lw], writes=[k_lw])
        S.op("dve", lambda E: E.tensor_tensor(out=lw[:, 4:5], in0=lw[:, 3:4], in1=lw[:, 2:3], op=ALU.subtract),
             reads=[k_lw], writes=[k_lw])
        S.op("dve", lambda E: E.tensor_scalar_add(out=lw[:, 5:6], in0=lw[:, 4:5], scalar1=-lam_init),
             reads=[k_lw], writes=[k_lw])
        neglam = lw[:, 5:6]
        sn, k_sn = cx.sb(st, "sn", [128, 2])
        S.dma("sp", sn[:, 0:1], L["subnorm"].rearrange("(p o) -> p o", o=1), writes=[k_sn])
        S.op("dve", lambda E: E.tensor_scalar(out=sn[:, 1:2], in0=sn[:, 0:1], scalar1=1.0 - lam_init, scalar2=None,
                                              op0=ALU.mult), reads=[k_sn], writes=[k_sn])
        kt_r = Ring([cx.sb(st, "ktda", [128, NKEY], BF16) for _ in range(2)])
        v_r = Ring([cx.sb(st, "vda", [128, NKEY // 128, 128], BF16) for _ in range(2)])
        q_r = Ring([cx.sb(st, "qda", [128, NQ], BF16) for _ in range(2)])
        pt_r = Ring([cx.sb(st, "pt", [128, 512], BF16) for _ in range(6)])
        acc_o = [cx.ps(st, "acc_o", [128, 512]) for _ in range(2)]
        acc_s = [cx.ps(st, "acc_s", [128, 512]) for _ in range(2)]
        pss_r = Ring([cx.ps(st, "pss", [128, 512]) for _ in range(4)])
        rr = [cx.sb(st, "rr", [128, 512]) for _ in range(2)]
        tt = [cx.sb(st, "tt", [128, 512]) for _ in range(2)]
        dd, k_dd = cx.sb(st, "dd", [128, 512])
        d2, k_d2 = cx.sb(st, "d2", [128, 512], BF16)
        rstd, k_rstd = cx.sb(st, "rstd", [128, 512])
        ya, k_ya = cx.sb(st, "ya", [128, 512])
        yab_r = Ring([cx.sb(st, "yab", [128, 512], BF16) for _ in range(2)])

        def load_head(h):
            kt, k_kt = kt_r.next()
            S.dma("sp", kt[:], kt_da[h, :, :], writes=[k_kt])
            v, k_v = v_r.next()
            S.dma("sp", v[:], v_da[:, h * 128:(h + 1) * 128].rearrange("(t p) d -> p t d", p=128), writes=[k_v])
            q, k_q = q_r.next()
            S.dma("sp", q[:], qt_da[h, :, :], writes=[k_q])
            return (kt, k_kt), (v, k_v), (q, k_q)

        nh_da = L.get("nh_da", 4)
        nxt = load_head(0) if nh_da > 0 else None
        for h in range(nh_da):
            (kt, k_kt), (v, k_v), (q, k_q) = nxt
            if h + 1 < nh_da:
                nxt = load_head(h + 1)
            for (q0, Nq, nkt) in qblocks:
                pend = None
                for step in range(nkt + 1):
                    cur = None
                    if step < nkt:
                        cur = []
                        for m in (0, 1):
                            ps_, k_ps = pss_r.next()
                            S.op("pe", lambda E: E.matmul(ps_[:, :Nq], lhsT=kt[m * 64:(m + 1) * 64, step * 128:(step + 1) * 128],
                                                          rhs=q[m * 64:(m + 1) * 64, q0:q0 + Nq], start=True, stop=True),
                                 reads=[k_kt, k_q], writes=[k_ps])
                            pt, k_pt = pt_r.next()
                            S.op("act", lambda E: E.activation(out=pt[:, :Nq], in_=ps_[:, :Nq], func=AF.Exp, scale=DA_SCALE),
                                 reads=[k_ps], writes=[k_pt])
                            cur.append((pt, k_pt))
                    if pend is not None:
                        kk = step - 1
                        for m in (0, 1):
                            pt, k_pt = pend[m]
                            S.op("pe", lambda E: E.matmul(acc_o[m][0][:, :Nq], lhsT=v[:, kk, :], rhs=pt[:, :Nq],
                                                          start=(kk == 0), stop=(kk == nkt - 1)),
                                 reads=[k_v, k_pt], writes=[acc_o[m][1]])
                            S.op("pe", lambda E: E.matmul(acc_s[m][0][:, :Nq], lhsT=onesb[:, :], rhs=pt[:, :Nq],
                                                          start=(kk == 0), stop=(kk == nkt - 1)),
                                 reads=[k_onesb, k_pt], writes=[acc_s[m][1]])
                    pend = cur
                for m in (0, 1):
                    S.op("dve", lambda E: E.reciprocal(out=rr[m][0][:, :Nq], in_=acc_s[m][0][:, :Nq]),
                         reads=[acc_s[m][1]], writes=[rr[m][1]])
                    S.op("dve", lambda E: E.tensor_tensor(out=tt[m][0][:, :Nq], in0=acc_o[m][0][:, :Nq], in1=rr[m][0][:, :Nq],
                                                          op=ALU.mult), reads=[acc_o[m][1], rr[m][1]], writes=[tt[m][1]])
                S.op("dve", lambda E: E.scalar_tensor_tensor(out=dd[:, :Nq], in0=tt[1][0][:, :Nq], scalar=neglam,
                                                             in1=tt[0][0][:, :Nq], op0=ALU.mult, op1=ALU.add),
                     reads=[tt[0][1], tt[1][1], k_lw], writes=[k_dd])
                S.op("pool", lambda E: E.tensor_tensor(out=d2[:, :Nq], in0=dd[:, :Nq], in1=dd[:, :Nq], op=ALU.mult),
                     reads=[k_dd], writes=[k_d2])
                ps_, k_ps = pss_r.next()
                S.op("pe", lambda E: E.matmul(ps_[:, :Nq], lhsT=onesm[:, :], rhs=d2[:, :Nq], start=True, stop=True),
                     reads=[k_onesm, k_d2], writes=[k_ps])
                S.op("act", lambda E: E.activation(out=rstd[:, :Nq], in_=ps_[:, :Nq], func=AF.Sqrt, bias=EPS),
                     reads=[k_ps], writes=[k_rstd])
                S.op("dve", lambda E: E.reciprocal(out=rstd[:, :Nq], in_=rstd[:, :Nq]), reads=[k_rstd], writes=[k_rstd])
                S.op("dve", lambda E: E.tensor_tensor(out=ya[:, :Nq], in0=dd[:, :Nq], in1=rstd[:, :Nq], op=ALU.mult),
                     reads=[k_dd, k_rstd], writes=[k_ya])
                yab, k_yab = yab_r.next()
                S.op("act", lambda E: E.activation(out=yab[:, :Nq], in_=ya[:, :Nq], func=AF.Copy, scale=sn[:, 1:2]),
                     reads=[k_ya, k_sn], writes=[k_yab])
                S.dma("sp", yt_da[h, :, q0:q0 + Nq], yab[:, :Nq], reads=[k_yab], writes=[], semkey="st_" + k_yab)
        cx.barrier()

    with ExitStack() as st:
        sel, k_sel = cx.sb(st, "sel65", [65, 64])
        S.op("pool", lambda E: E.memset(sel[:], 0.0), writes=[k_sel])
        S.op("pool", lambda E: E.memset(sel[64:65, :], 1.0), writes=[k_sel])
        kt_r = Ring([cx.sb(st, "ktm", [96, NKEY], BF16) for _ in range(2)])
        v_r = Ring([cx.sb(st, "vm", [128, NKEY // 128, 65], BF16) for _ in range(2)])
        q_r = Ring([cx.sb(st, "qm", [96, NQ], BF16) for _ in range(2)])
        pt_r = Ring([cx.sb(st, "ptm", [128, 512], BF16) for _ in range(4)])
        acc_r = Ring([cx.ps(st, "accm", [128, 512]) for _ in range(2)])
        pss_r = Ring([cx.ps(st, "pssm", [128, 512]) for _ in range(4)])
        psb_r = Ring([cx.ps(st, "psb", [128, 512]) for _ in range(2)])
        accs_r = Ring([cx.sb(st, "accs", [65, 512]) for _ in range(2)])
        rrm, k_rrm = cx.sb(st, "rrm", [64, 512])
        ymb_r = Ring([cx.sb(st, "ymb", [64, 512], BF16) for _ in range(2)])

        def load_head_m(h):
            kt, k_kt = kt_r.next()
            S.dma("sp", kt[:], kt_m[h, :, :], writes=[k_kt])
            v, k_v = v_r.next()
            S.dma("sp", v[:], v_m[:, h * 65:(h + 1) * 65].rearrange("(t p) d -> p t d", p=128), writes=[k_v])
            q, k_q = q_r.next()
            S.dma("sp", q[:], qt_m[h, :, :], writes=[k_q])
            return (kt, k_kt), (v, k_v), (q, k_q)

        nh_m = L.get("nh_m", 8)
        nxt = load_head_m(0) if nh_m > 0 else None
        for h in range(nh_m):
            (kt, k_kt), (v, k_v), (q, k_q) = nxt
            if h + 1 < nh_m:
                nxt = load_head_m(h + 1)
            for (q0, Nq, nkt) in qblocks:
                acc, k_acc = acc_r.next()
                pend = None
                for step in range(nkt + 1):
                    cur = None
                    if step < nkt:
                        ps_, k_ps = pss_r.next()
                        S.op("pe", lambda E: E.matmul(ps_[:, :Nq], lhsT=kt[:, step * 128:(step + 1) * 128],
                                                      rhs=q[:, q0:q0 + Nq], start=True, stop=True),
                             reads=[k_kt, k_q], writes=[k_ps])
                        pt, k_pt = pt_r.next()
                        S.op("act", lambda E: E.activation(out=pt[:, :Nq], in_=ps_[:, :Nq], func=AF.Exp, scale=MLA_SCALE),
                             reads=[k_ps], writes=[k_pt])
                        cur = (pt, k_pt)
                    if pend is not None:
                        kk = step - 1
                        pt, k_pt = pend
                        S.op("pe", lambda E: E.matmul(acc[0:65, :Nq], lhsT=v[:, kk, :], rhs=pt[:, :Nq],
                                                      start=(kk == 0), stop=(kk == nkt - 1)),
                             reads=[k_v, k_pt], writes=[k_acc])
                    pend = cur
                accs, k_accs = accs_r.next()
                S.op("act", lambda E: E.copy(out=accs[0:65, :Nq], in_=acc[0:65, :Nq]), reads=[k_acc], writes=[k_accs])
                psb, k_psb = psb_r.next()
                S.op("pe", lambda E: E.matmul(psb[0:64, :Nq], lhsT=sel[0:65, :], rhs=accs[0:65, :Nq], start=True, stop=True),
                     reads=[k_sel, k_accs], writes=[k_psb])
                S.op("dve", lambda E: E.reciprocal(out=rrm[:, :Nq], in_=psb[0:64, :Nq]), reads=[k_psb], writes=[k_rrm])
                ymb, k_ymb = ymb_r.next()
                S.op("dve", lambda E: E.tensor_tensor(out=ymb[:, :Nq], in0=accs[0:64, :Nq], in1=rrm[:, :Nq], op=ALU.mult),
                     reads=[k_accs, k_rrm], writes=[k_ymb])
                S.dma("sp", yt_m[h, :, q0:q0 + Nq], ymb[:, :Nq], reads=[k_ymb], writes=[], semkey="st_" + k_ymb)
        cx.barrier()
    if L.get("upto", 99) >= 3:
        ysrc = [(yt_da, 4, 128, 0), (yt_m, 8, 64, 512)]
        emit_outproj_mlp(cx, ysrc, L["w_out"], L["xs"], L["gate"], L["Gmlp"], L["k_Gmlp"], L["sh_mlp"], L["k_modT"],
                         L["w1"], L["w2"], L["h_out"], L["ident"], L["k_id"], final_norm=None)


def emit_outproj_mlp(cx, ysrc, w_out, hsrc, gate, Gmlp, k_Gmlp, sh_mlp, k_shmlp, w1, w2, h_out, ident, k_id,
                     final_norm=None, gate_mix=2, gate_mlp=5, ytok=None):
    nc, S = cx.nc, cx.S
    NT_ALL = NQ // 128
    with ExitStack() as st:
        h1, k_h1 = cx.sb(st, "h1", [128, NT_ALL, D])
        uT, k_uT = cx.sb(st, "uTmlp", [128, 8, NQ], BF16)
        stage_ring = Ring([cx.sb(st, "wstage2", [128, 2048]) for _ in range(2)])
        cast_ring = Ring(["act", "pool", "dve"])
        tmp_r = Ring([cx.sb(st, "gtmp", [128, 512]) for _ in range(3)])
        with ExitStack() as st2:
            wo = []
            for (ydram, nch, prows, row0) in ysrc:
                wt, k_wt = cx.sb(st2, "wo", [prows, nch, D], BF16)
                keys = load_weight_bf16(cx, wt, k_wt, w_out[row0:row0 + nch * prows, :], nch, D, stage_ring, cast_ring,
                                        prows=prows)
                wo.append((wt, keys))
            yt_r = [Ring([cx.sb(st2, "ytile", [prows, nch, 128], BF16) for _ in range(2)]) for (_, nch, prows, _) in ysrc]
            if ytok is not None:
                Ydram, ssdg = ytok
                sgb, k_sgb = cx.sb(st2, "sgb", [128, 512])
                S.dma("sp", sgb[:], ssdg.partition_broadcast(128), writes=[k_sgb])
                yf_r = Ring([cx.sb(st2, "yf", [128, D]) for _ in range(2)])
                ynb_r = Ring([cx.sb(st2, "ynb", [128, D], BF16) for _ in range(2)])
                ysq, k_ysq = cx.sb(st2, "ysq", [128, 256], BF16)
                ys2, k_ys2 = cx.sb(st2, "ys2", [128, 4])
            xt_r = Ring([cx.sb(st2, "xres", [128, 1, D]) for _ in range(2)])
            pso_r = Ring([cx.ps(st2, "pso", [128, 512]) for _ in range(4)])
            wk = {
                "ss": cx.sb(st2, "ss", [128, 4]), "rs": cx.sb(st2, "rs", [128, 4]),
                "junk": cx.sb(st2, "junk", [128, D], BF16), "xn": cx.sb(st2, "xn", [128, 1, D], BF16),
                "ptr": Ring([cx.ps(st2, "ptr", [128, 8, 128], BF16) for _ in range(2)]),
                "tmpT": Ring([cx.sb(st2, "tmpT", [128, 8, 128]) for _ in range(2)]),
            }

            def loads(ti):
                ys = []
                if ytok is not None:
                    yf, k_yf = yf_r.next()
                    S.dma("sp", yf[:], Ydram[ti * 128:(ti + 1) * 128, :], writes=[k_yf])
                    ynb, k_ynb = ynb_r.next()
                    for gi in (0, 1):
                        S.op("act", lambda E: E.activation(out=ysq[:], in_=yf[:, 512 + gi * 256:768 + gi * 256], func=AF.Square,
                                                           accum_out=ys2[:, gi:gi + 1]), reads=[k_yf], writes=[k_ysq, k_ys2])
                    S.op("act", lambda E: E.activation(out=ys2[:, 2:4], in_=ys2[:, 0:2], func=AF.Sqrt, scale=1.0 / 256, bias=EPS),
                         reads=[k_ys2], writes=[k_ys2])
                    S.op("dve", lambda E: E.reciprocal(out=ys2[:, 2:4], in_=ys2[:, 2:4]), reads=[k_ys2], writes=[k_ys2])
                    S.op("pool", lambda E: E.tensor_copy(out=ynb[:, 0:512], in_=yf[:, 0:512]), reads=[k_yf], writes=[k_ynb + "a"])
                    for gi in (0, 1):
                        S.op("dve", lambda E: E.scalar_tensor_tensor(out=ynb[:, 512 + gi * 256:768 + gi * 256],
                                                                     in0=yf[:, 512 + gi * 256:768 + gi * 256], scalar=ys2[:, 2 + gi:3 + gi],
                                                                     in1=sgb[:, gi * 256:(gi + 1) * 256], op0=ALU.mult, op1=ALU.mult),
                             reads=[k_yf, k_ys2, k_sgb], writes=[k_ynb + "b%d" % gi])
                    ptr, k_ptr = wk["ptr"].next()
                    fns = [lambda E, kc=kc: E.transpose(out=ptr[:, kc, :], in_=ynb[:, kc * 128:(kc + 1) * 128], identity=ident[:])
                           for kc in range(8)]
                    S.group("pe", fns, reads=[k_ynb + "a", k_ynb + "b0", k_ynb + "b1", k_id], writes=[k_ptr])
                    yt, k_yt = yt_r[0].next()
                    S.op("act", lambda E: E.copy(out=yt[:], in_=ptr[:]), reads=[k_ptr], writes=[k_yt])
                    ys.append((yt, k_yt))
                for si, (ydram, nch, prows, row0) in enumerate(ysrc if ytok is None else []):
                    yt, k_yt = yt_r[si].next()
                    S.dma("sp", yt[:], ydram[:, :, ti * 128:(ti + 1) * 128].rearrange("c p t -> p c t"), writes=[k_yt])
                    ys.append((yt, k_yt))
                xt, k_xt = xt_r.next()
                S.dma("sp", xt[:, 0, :], hsrc[ti * 128:(ti + 1) * 128, :], writes=[k_xt])
                return ys, (xt, k_xt)

            nxt = loads(0)
            for ti in range(NT_ALL):
                n = 1 if ti < 2 else 0
                ys, (xt, k_xt) = nxt
                if ti + 1 < NT_ALL:
                    nxt = loads(ti + 1)
                for half in (0, 1):
                    ps_, k_ps = pso_r.next()
                    fns = []
                    rk = []
                    tot = sum(nch for (_, nch, _, _) in ysrc)
                    idx = 0
                    for si, (ydram, nch, prows, row0) in enumerate(ysrc):
                        yt, k_yt = ys[si]
                        wt, wkeys = wo[si]
                        rk += [k_yt] + wkeys
                        for c in range(nch):
                            fns.append(lambda E, yt=yt, wt=wt, c=c, prows=prows, idx=idx: E.matmul(
                                ps_[:, :], lhsT=yt[0:prows, c, :], rhs=wt[0:prows, c, half * 512:(half + 1) * 512],
                                start=(idx == 0), stop=(idx == tot - 1)))
                            idx += 1
                    S.group("pe", fns, reads=rk, writes=[k_ps])
                    g, k_g = gate[(gate_mix, n)]
                    tmp, k_tmp = tmp_r.next()
                    S.op("dve", lambda E: E.tensor_tensor(out=tmp[:], in0=ps_[:, :], in1=g[:, half * 512:(half + 1) * 512],
                                                          op=ALU.mult), reads=[k_ps, k_g], writes=[k_tmp])
                    S.op("pool", lambda E: E.tensor_tensor(out=h1[:, ti, half * 512:(half + 1) * 512], in0=tmp[:],
                                                           in1=xt[:, 0, half * 512:(half + 1) * 512], op=ALU.add),
                         reads=[k_tmp, k_xt], writes=["%s_%d_%d" % (k_h1, ti, half)])
                hk = ["%s_%d_%d" % (k_h1, ti, hh) for hh in (0, 1)]
                S.op("act", lambda E: E.activation(out=wk["junk"][0][:], in_=h1[:, ti, :], func=AF.Square,
                                                   accum_out=wk["ss"][0][:, 0:1]), reads=hk, writes=[wk["junk"][1], wk["ss"][1]])
                S.op("act", lambda E: E.activation(out=wk["rs"][0][:, 0:1], in_=wk["ss"][0][:, 0:1], func=AF.Sqrt,
                                                   scale=1.0 / D, bias=EPS), reads=[wk["ss"][1]], writes=[wk["rs"][1]])
                S.op("dve", lambda E: E.reciprocal(out=wk["rs"][0][:, 0:1], in_=wk["rs"][0][:, 0:1]),
                     reads=[wk["rs"][1]], writes=[wk["rs"][1]])
                xn, k_xn = wk["xn"]
                S.op("dve", lambda E: E.tensor_scalar(out=xn[:, 0, :], in0=h1[:, ti, :], scalar1=wk["rs"][0][:, 0:1],
                                                      scalar2=None, op0=ALU.mult), reads=hk + [wk["rs"][1]], writes=[k_xn])
                ptr, k_ptr = wk["ptr"].next()
                fns = [lambda E, kc=kc: E.transpose(out=ptr[:, kc, :], in_=xn[:, 0, kc * 128:(kc + 1) * 128], identity=ident[:])
                       for kc in range(8)]
                S.group("pe", fns, reads=[k_xn, k_id], writes=[k_ptr])
                tmpT, k_tmpT = wk["tmpT"].next()
                S.op("dve", lambda E: E.tensor_tensor(out=tmpT[:], in0=ptr[:], in1=Gmlp[:, :, n:n + 1].broadcast_to([128, 8, 128]),
                                                      op=ALU.mult), reads=[k_ptr, k_Gmlp], writes=[k_tmpT])
                S.op("pool", lambda E: E.tensor_tensor(out=uT[:, :, ti * 128:(ti + 1) * 128], in0=tmpT[:],
                                                       in1=sh_mlp[:, :, n:n + 1].broadcast_to([128, 8, 128]), op=ALU.add),
                     reads=[k_tmpT, k_shmlp], writes=["%s_%d" % (k_uT, ti)])
            cx.barrier()
        with ExitStack() as st2:
            w1_r = Ring([cx.sb(st2, "w1c", [128, 8, 512], BF16) for _ in range(2)])
            w2_r = Ring([cx.sb(st2, "w2c", [128, 4, D], BF16) for _ in range(2)])
            psu_r = Ring([cx.ps(st2, "psu", [128, 512]) for _ in range(4)])
            psd_r = Ring([cx.ps(st2, "psd", [128, 512]) for _ in range(4)])
            rl_r = Ring([cx.sb(st2, "rl", [128, 512]) for _ in range(2)])
            hT_r = Ring([cx.sb(st2, "hT", [128, 4, 512], BF16) for _ in range(2)])
            tblocks = [(0, 256)] + [(NCTX + i * 512, 512) for i in range(4)]
            w2v = w2.rearrange("(c p) n -> p c n", p=128)

            def load_w(hc):
                w1c, k_w1c = w1_r.next()
                k1 = load_weight_bf16(cx, w1c, k_w1c, w1, 8, 512, stage_ring, cast_ring, col0=hc * 512)
                w2c, k_w2c = w2_r.next()
                k2 = load_weight_bf16(cx, w2c, k_w2c, w2[hc * 512:(hc + 1) * 512, :], 4, D, stage_ring, cast_ring)
                return (w1c, k1), (w2c, k2)

            nxt = load_w(0)
            for hc in range(8):
                (w1c, k1), (w2c, k2) = nxt
                if hc + 1 < 8:
                    nxt = load_w(hc + 1)
                for (t0, NT) in tblocks:
                    n = 1 if t0 == 0 else 0
                    hT, k_hT = hT_r.next()
                    tiles = list(range(t0 // 128, (t0 + NT) // 128))
                    for sub in range(4):
                        ps_, k_ps = psu_r.next()
                        fns = [lambda E, kc=kc: E.matmul(ps_[:, :NT], lhsT=w1c[:, kc, sub * 128:(sub + 1) * 128],
                                                          rhs=uT[:, kc, t0:t0 + NT], start=(kc == 0), stop=(kc == 7))
                               for kc in range(8)]
                        S.group("pe", fns, reads=k1 + ["%s_%d" % (k_uT, ti) for ti in tiles], writes=[k_ps])
                        rl, k_rl = rl_r.next()
                        S.op("act", lambda E: E.activation(out=rl[:, :NT], in_=ps_[:, :NT], func=AF.Relu),
                             reads=[k_ps], writes=[k_rl])
                        S.op("pool", lambda E: E.tensor_tensor(out=hT[:, sub, :NT], in0=rl[:, :NT], in1=rl[:, :NT], op=ALU.mult),
                             reads=[k_rl], writes=["%s_%d" % (k_hT, sub)])
                    for j, ti in enumerate(tiles):
                        for half in (0, 1):
                            ps_, k_ps = psd_r.next()
                            fns = [lambda E, sub=sub: E.matmul(ps_[:, :], lhsT=hT[:, sub, j * 128:(j + 1) * 128],
                                                                rhs=w2c[:, sub, half * 512:(half + 1) * 512],
                                                                start=(sub == 0), stop=(sub == 3)) for sub in range(4)]
                            S.group("pe", fns, reads=k2 + ["%s_%d" % (k_hT, sub) for sub in range(4)], writes=[k_ps])
                            g, k_g = gate[(gate_mlp, n)]
                            tmp, k_tmp = tmp_r.next()
                            S.op("dve", lambda E: E.tensor_tensor(out=tmp[:], in0=ps_[:, :], in1=g[:, half * 512:(half + 1) * 512],
                                                                  op=ALU.mult), reads=[k_ps, k_g], writes=[k_tmp])
                            hk = "%s_%d_%d" % (k_h1, ti, half)
                            S.op("pool", lambda E: E.tensor_tensor(out=h1[:, ti, half * 512:(half + 1) * 512],
                                                                   in0=h1[:, ti, half * 512:(half + 1) * 512], in1=tmp[:], op=ALU.add),
                                 reads=[k_tmp, hk], writes=[hk])
            cx.barrier()
        if final_norm is not None:
            with ExitStack() as st2:
                fn_b, k_fnb = cx.sb(st2, "fnb", [128, D])
                S.dma("sp", fn_b[:], final_norm.partition_broadcast(128), writes=[k_fnb])
                ss, k_ss = cx.sb(st2, "fss", [128, 2])
                junk, k_junk = cx.sb(st2, "fjunk", [128, D], BF16)
                for ti in range(NT_ALL):
                    hk = ["%s_%d_%d" % (k_h1, ti, hh) for hh in (0, 1)]
                    S.op("act", lambda E: E.activation(out=junk[:], in_=h1[:, ti, :], func=AF.Square, accum_out=ss[:, 0:1]),
                         reads=hk, writes=[k_junk, k_ss])
                    S.op("act", lambda E: E.activation(out=ss[:, 1:2], in_=ss[:, 0:1], func=AF.Sqrt, scale=1.0 / D, bias=EPS),
                         reads=[k_ss], writes=[k_ss])
                    S.op("dve", lambda E: E.reciprocal(out=ss[:, 1:2], in_=ss[:, 1:2]), reads=[k_ss], writes=[k_ss])
                    S.op("dve", lambda E: E.scalar_tensor_tensor(out=h1[:, ti, :], in0=h1[:, ti, :], scalar=ss[:, 1:2],
                                                                 in1=fn_b[:], op0=ALU.mult, op1=ALU.mult),
                         reads=hk + [k_ss, k_fnb], writes=hk)
                cx.barrier()
        for ti in range(NT_ALL):
            hk = ["%s_%d_%d" % (k_h1, ti, hh) for hh in (0, 1)]
            S.dma("sp", h_out[ti * 128:(ti + 1) * 128, :], h1[:, ti, :], reads=hk, writes=[], semkey="st_h1_%d" % (ti % 4))
        cx.barrier()


def host_consts(qtr):
    pos = np.concatenate([-np.ones(NCTX, np.int64), (np.arange(NLAT) + qtr * NOWN) % NLAT])
    c64, s64 = rope_tables(pos, 64)
    c32, s32 = rope_tables(pos, 32)
    cs_da = np.stack([np.concatenate([c64, c64], 0), np.concatenate([s64, s64], 0)], axis=1)
    one = np.ones((64, NKEY), np.float32)
    zero = np.zeros((64, NKEY), np.float32)
    cs_m = np.stack([np.concatenate([one, c32], 0), np.concatenate([zero, s32], 0)], axis=1)
    cs_kr = np.stack([c32, s32], axis=1)
    return {
        "cs_da": np.ascontiguousarray(cs_da, np.float32), "cs_m": np.ascontiguousarray(cs_m, np.float32),
        "cs_kr": np.ascontiguousarray(cs_kr, np.float32),
        "perm_da": perm_matrix(2, 64), "perm_m": perm_matrix(1, 32, pad_rows=64), "perm_kr": perm_matrix(1, 32),
        "ident": np.eye(128, dtype=np.float32),
    }


def prep_A(inp, core):
    b, qtr = core // 4, core % 4
    f = lambda a: np.ascontiguousarray(a, dtype=np.float32)
    xs = np.concatenate([inp["ctx"][b], np.roll(inp["x"][b], -qtr * NOWN, axis=0)], axis=0)
    c2 = np.stack([inp["c"][b].reshape(8, 128).T, inp["c_ctx"].reshape(8, 128).T], axis=-1).reshape(128, 16)
    m = {
        "xs": f(xs), "c2": f(c2), "w_mod": f(inp["w_mod"][0]), "b_mod": f(inp["b_mod"][0]),
        "norm_mix": f(inp["norm_mix"][0]), "norm_mlp": f(inp["norm_mlp"][0]), "w_in": f(inp["att_w_in"][0]),
        "q_norm": f(inp["mla_q_norm"][0]), "w_uq": f(inp["mla_w_uq"][0]), "kv_norm": f(inp["mla_kv_norm"][0]),
        "w_ukv": f(inp["mla_w_ukv"][0]), "w_out": f(inp["att_w_out"][0]), "lam": f(inp["att_lambda"][0].reshape(256)),
        "subnorm": f(inp["att_subnorm"][0]), "w1": f(inp["w_mlp_in"][0]), "w2": f(inp["w_mlp_out"][0]),
    }
    m.update(host_consts(qtr))
    return m


NCOLB = 1156
C_Q, C_FF, C_FB, C_I, C_G, C_Z, C_X, C_B, C_C, C_DT = 0, 128, 256, 384, 512, 640, 768, 896, 1024, 1152


def build_B(nlat=NLAT, debug=False):
    nc = bass.Bass("TRN2", target_bir_lowering=False)
    NK = NCTX + nlat
    NPAD = NK + 8
    OFF_CTX, OFF_LAT = 2, NCTX + 6

    blob = Blob(nc, spec_B(nlat))

    def din(name, shape, dt=F32):
        return blob.get(name)

    def dscr(name, shape, dt=BF16):
        return nc.dram_tensor(name, list(shape), dt, kind="Internal").ap()

    hs = din("hs", [NK, D])
    c2 = din("c2", [128, 16])
    w_mod = din("w_mod", [D, 2048])
    b_mod = din("b_mod", [2048])
    norm_mix = din("norm_mix", [D])
    w_rec = din("w_rec", [D, NCOLB])
    bl = din("bl", [128, 4])
    out_norm = din("out_norm", [128])
    cw = din("cw", [128, 15])
    cb = din("cb", [128, 3])
    alog4 = din("alog4", [4])
    dtb4 = din("dtb4", [4])
    skipb = din("skipb", [128])
    masks = din("masks", [64, 4, 64])
    mask01 = din("mask01", [128, 512])
    ident_d = din("ident", [128, 128])
    yb = nc.dram_tensor("yb", [NK, 256], F32, kind="ExternalOutput").ap()

    xr = dscr("xr", [3, 128, NPAD], F32)
    xc = dscr("xc", [3, 128, NK], BF16)
    o_d = [dscr("o_d%d" % d, [NK, 128], F32) for d in (0, 1)]
    y_d = [dscr("y_d%d" % d, [NK, 128], F32) for d in (0, 1)]
    gz_d = dscr("gz_d", [NK, 256], BF16)

    nlb = nlat // 512
    blocks = [(0, 256, 1)] + [(NCTX + i * 512, 512, 0) for i in range(nlb)]

    with ExitStack() as es:
        cx = Ctx(nc, es)
        S = cx.S
        (modT, k_modT), _ = emit_mod(cx, es, w_mod, b_mod, c2, need=[0, 1], gates=[])
        (Gmix, k_Gmix), sh_mix = emit_gcols(cx, es, modT, k_modT, norm_mix, 0, 1, "mixb")
        identf, k_identf = cx.sb(es, "identf", [128, 128])
        ident, k_id = cx.sb(es, "ident", [128, 128], BF16)
        S.dma("sp", identf[:], ident_d[:, :], writes=[k_identf])
        S.op("dve", lambda E: E.tensor_copy(out=ident[:], in_=identf[:]), reads=[k_identf], writes=[k_id])
        stage_ring = Ring([cx.sb(es, "wstageb", [128, 2048]) for _ in range(2)])
        cast_ring = Ring(["act", "dve", "pool"])
        w_sb, k_w = cx.sb(es, "w_rec_sb", [128, 8, NCOLB], BF16)
        wk_rec = load_weight_bf16(cx, w_sb, k_w, w_rec, 8, NCOLB, stage_ring, cast_ring)
        wdt, k_wdt = cx.sb(es, "wdt", [128, 8, 4, 128], BF16)
        for kc in range(8):
            S.op("dve", lambda E: E.tensor_copy(out=wdt[:, kc], in_=w_sb[:, kc, C_DT:C_DT + 4].unsqueeze(2).broadcast_to([128, 4, 128])),
                 reads=wk_rec, writes=[k_wdt + "_%d" % kc])
        wk_dt = [k_wdt + "_%d" % kc for kc in range(8)]
        blt, k_bl = cx.sb(es, "blt", [128, 8])
        S.dma("sp", blt[:, 0:4], bl[:, :], writes=[k_bl])
        for d in (0, 1):
            S.op("dve", lambda E: E.tensor_tensor(out=blt[:, 4 + 2 * d:5 + 2 * d], in0=blt[:, 2 * d + 1:2 * d + 2],
                                                  in1=blt[:, 2 * d:2 * d + 1], op=ALU.subtract), reads=[k_bl], writes=[k_bl])
            S.op("act", lambda E: E.activation(out=blt[:, 4 + 2 * d:5 + 2 * d], in_=blt[:, 4 + 2 * d:5 + 2 * d], func=AF.Sigmoid),
                 reads=[k_bl], writes=[k_bl])
            S.op("dve", lambda E: E.tensor_scalar(out=blt[:, 5 + 2 * d:6 + 2 * d], in0=blt[:, 4 + 2 * d:5 + 2 * d], scalar1=-1.0,
                                                  scalar2=1.0, op0=ALU.mult, op1=ALU.add), reads=[k_bl], writes=[k_bl])
        onb, k_onb = cx.sb(es, "onb", [64, 128])
        S.dma("sp", onb[:], out_norm.partition_broadcast(64), writes=[k_onb])
        onb128 = cx.sb(es, "onb128", [128, 128])
        S.dma("sp", onb128[0][:], out_norm.partition_broadcast(128), writes=[onb128[1]])
        skb, k_skb = cx.sb(es, "skb", [64, 128])
        S.dma("sp", skb[:], skipb.partition_broadcast(64), writes=[k_skb])
        cwt, k_cw = cx.sb(es, "cwt", [128, 15])
        cbt, k_cb = cx.sb(es, "cbt", [128, 3])
        S.dma("sp", cwt[:], cw[:, :], writes=[k_cw])
        S.dma("sp", cbt[:], cb[:, :], writes=[k_cb])
        ad, k_ad = cx.sb(es, "ad", [128, 12])
        S.dma("sp", ad[:, 0:4], alog4.partition_broadcast(128), writes=[k_ad])
        S.dma("sp", ad[:, 4:8], dtb4.partition_broadcast(128), writes=[k_ad + "b"])
        S.op("act", lambda E: E.activation(out=ad[:, 8:12], in_=ad[:, 0:4], func=AF.Exp), reads=[k_ad], writes=[k_ad + "c"])
        S.op("dve", lambda E: E.tensor_scalar(out=ad[:, 8:12], in0=ad[:, 8:12], scalar1=-1.0, scalar2=None, op0=ALU.mult),
             reads=[k_ad + "c"], writes=[k_ad + "c"])
        k_aneg, k_dtb = k_ad + "c", k_ad + "b"
        mk, k_mk = cx.sb(es, "mk", [64, 4, 64])
        S.dma("sp", mk[:], masks[:, :, :], writes=[k_mk])
        m01, k_m01 = cx.sb(es, "m01", [128, 512])
        S.dma("sp", m01[:], mask01[:, :], writes=[k_m01])
        cx.barrier()

        wkn = {
            "ss": cx.sb(es, "ss", [128, 4]), "rs": cx.sb(es, "rs", [128, 4]),
            "junk": cx.sb(es, "junk", [128, D], BF16), "xn": cx.sb(es, "xn", [128, 4, D], BF16),
            "ptr": Ring([cx.ps(es, "ptr", [128, 8, 128], BF16) for _ in range(2)]),
            "tmpT": Ring([cx.sb(es, "tmpT", [128, 8, 128]) for _ in range(2)]),
        }
        xt_r = Ring([cx.sb(es, "xt", [128, 4, D]) for _ in range(2)])
        uT_r = Ring([cx.sb(es, "uT", [128, 8, 512], BF16) for _ in range(2)])

        def load_x(bi):
            t0, NT, n = blocks[bi]
            xt, k_xt = xt_r.next()
            S.dma("sp", xt[:, 0:NT // 128, :], hs[t0:t0 + NT, :].rearrange("(j p) d -> p j d", p=128), writes=[k_xt])
            return xt, k_xt

        with ExitStack() as st:
            pg = Ring([cx.ps(st, "pg0", [128, 512]) for _ in range(3)])
            xo_r = Ring([cx.sb(st, "xo", [128, 512]) for _ in range(3)])
            zt, k_zt = cx.sb(st, "zt", [128, 4])
            S.op("pool", lambda E: E.memset(zt[:], 0.0), writes=[k_zt])
            for grp in range(3):
                for off in (0, OFF_CTX + NCTX, OFF_LAT + nlat):
                    w_ = 2 if off != OFF_CTX + NCTX else 4
                    S.dma("sp", xr[grp, :, off:off + w_], zt[:, 0:w_], reads=[k_zt], writes=[], semkey="st_zt")
            nxt = load_x(0)
            for bi, (t0, NT, n) in enumerate(blocks):
                xt, k_xt = nxt
                if bi + 1 < len(blocks):
                    nxt = load_x(bi + 1)
                uT, k_uT = uT_r.next()
                uk = emit_norm_T(cx, xt, k_xt, NT // 128, Gmix, k_Gmix, sh_mix, k_modT, n, uT, k_uT, ident, k_id, wkn)
                po = (OFF_CTX + t0) if n == 1 else (OFF_LAT + t0 - NCTX)
                for grp, c0 in enumerate((C_X, C_B, C_C)):
                    ps_, k_ps = pg.next()
                    fns = [lambda E, kc=kc: E.matmul(ps_[:, :NT], lhsT=w_sb[:, kc, c0:c0 + 128], rhs=uT[:, kc, :NT],
                                                      start=(kc == 0), stop=(kc == 7)) for kc in range(8)]
                    S.group("pe", fns, reads=wk_rec + uk, writes=[k_ps])
                    xo, k_xo = xo_r.next()
                    S.op("act", lambda E: E.copy(out=xo[:, :NT], in_=ps_[:, :NT]), reads=[k_ps], writes=[k_xo])
                    S.dma("sp", xr[grp, :, po:po + NT], xo[:, :NT], reads=[k_xo], writes=[], semkey="st_" + k_xo)
            cx.barrier()
            win_r = Ring([cx.sb(st, "win", [128, 516]) for _ in range(3)])
            acc_r = Ring([cx.sb(st, "cacc", [128, 512]) for _ in range(2)])
            co_r = Ring([cx.sb(st, "cout", [128, 512], BF16) for _ in range(3)])
            for bi, (t0, NT, n) in enumerate(blocks):
                po = (OFF_CTX + t0) if n == 1 else (OFF_LAT + t0 - NCTX)
                for grp in range(3):
                    win, k_win = win_r.next()
                    S.dma("sp", win[:, 0:NT + 4], xr[grp, :, po - 2:po + NT + 2], writes=[k_win])
                    acc, k_acc = acc_r.next()
                    S.op("dve", lambda E: E.tensor_scalar(out=acc[:, :NT], in0=win[:, 0:NT], scalar1=cwt[:, grp * 5:grp * 5 + 1],
                                                          scalar2=cbt[:, grp:grp + 1], op0=ALU.mult, op1=ALU.add),
                         reads=[k_win, k_cw, k_cb], writes=[k_acc])
                    for j in range(1, 5):
                        S.op("dve", lambda E: E.scalar_tensor_tensor(out=acc[:, :NT], in0=win[:, j:j + NT],
                                                                     scalar=cwt[:, grp * 5 + j:grp * 5 + j + 1], in1=acc[:, :NT],
                                                                     op0=ALU.mult, op1=ALU.add),
                             reads=[k_win, k_cw, k_acc], writes=[k_acc])
                    co, k_co = co_r.next()
                    S.op("act", lambda E: E.activation(out=co[:, :NT], in_=acc[:, :NT], func=AF.Silu), reads=[k_acc], writes=[k_co])
                    S.dma("sp", xc[grp, :, t0:t0 + NT], co[:, :NT], reads=[k_co], writes=[], semkey="st_" + k_co)
            cx.barrier()
        build_B_sweeps(cx, locals())
    return nc


def build_B_sweeps(cx, L):
    nc, S = cx.nc, cx.S
    g = lambda n: L[n]
    blocks, hs, xc, o_d, y_d, gz_d, yb = g("blocks"), g("hs"), g("xc"), g("o_d"), g("y_d"), g("gz_d"), g("yb")
    w_sb, wk_rec, wdt, wk_dt = g("w_sb"), g("wk_rec"), g("wdt"), g("wk_dt")
    blt, k_bl, ad, k_aneg, k_dtb = g("blt"), g("k_bl"), g("ad"), g("k_aneg"), g("k_dtb")
    mk, k_mk, m01, k_m01 = g("mk"), g("k_mk"), g("m01"), g("k_m01")
    skb, k_skb, onb, k_onb = g("skb"), g("k_skb"), g("onb"), g("k_onb")
    ident, k_id, identf, k_identf = g("ident"), g("k_id"), g("identf"), g("k_identf")
    Gmix, k_Gmix, sh_mix, k_modT, wkn, uT_r, load_x = g("Gmix"), g("k_Gmix"), g("sh_mix"), g("k_modT"), g("wkn"), g("uT_r"), g("load_x")
    NK = g("NK")

    with ExitStack() as st:
        def T(name, shape, dt=F32, n=1):
            return Ring([cx.sb(st, name, shape, dt) for _ in range(n)])
        xT_r, BT_r, CT_r = T("xT", [128, 512], BF16, 2), T("BT", [128, 512], BF16, 2), T("CT", [128, 512], BF16, 2)
        f_t, lf_t, k_t, bc_t, A_t, dA_t, e_t, q_t = (T(nm, [128, 512]) for nm in ("f_t", "lf_t", "k_t", "bc_t", "A_t", "dA_t", "e_t", "q_t"))
        qtl_t, ktl_t, qh_t = T("qtl", [128, 512], BF16), T("ktl", [128, 512], BF16), T("qh", [128, 512], BF16)
        ksc_t, dtot_t = T("ksc", [128, 8]), T("dtot", [128, 8])
        dt_t = [T("dt%d" % h, [128, 512]) for h in (0, 1)]
        a_t = [T("a%d" % h, [128, 512]) for h in (0, 1)]
        As_t = [T("As%d" % h, [128, 512]) for h in (0, 1)]
        eA_t = [T("eA%d" % h, [128, 512]) for h in (0, 1)]
        wd_t = [T("wd%d" % h, [128, 512]) for h in (0, 1)]
        et_t = [T("et%d" % h, [128, 8]) for h in (0, 1)]
        psm = Ring([cx.ps(st, "psm", [128, 512]) for _ in range(3)])
        pbig = psm
        psb = Ring([cx.ps(st, "psb", [128, 512]) for _ in range(2)])
        ptq = cx.ps(st, "ptq", [128, 512])
        v_r, x_r, B_r, kk_r = (T(nm, [64, 128], BF16, 3) for nm in ("v_c", "x_c", "B_c", "k_c"))
        xs_r = T("xs_c", [64, 128], F32, 2)
        xbk_r = T("xbk", [64, 3, 128], BF16, 3)
        gz_r = T("gz_c", [64, 256], BF16, 2)
        cols_r = T("cols", [64, 8], F32, 3)
        attm_r = T("attm", [64, 64], BF16, 3)
        oc_r, yc_r = T("o_c", [64, 128], F32, 2), T("y_c", [64, 128], F32, 2)
        cb_r = T("cb_c", [64, 64], F32, 2)
        dm_r, dec_r = T("dm", [64, 64], F32, 2), T("dec", [64, 64], F32, 2)
        wt_r = T("wt", [64, 64], BF16, 6)
        ytmp_r = T("ytmp", [64, 64], F32, 2)
        xw_r = T("xw", [64, 64], BF16, 6)
        Sst, k_S = cx.sb(st, "Sst", [128, 128])
        Sbf, k_Sbf = cx.sb(st, "Sbf", [128, 128], BF16)
        tS_r = T("tS", [128, 128], F32, 2)
        hst = [cx.sb(st, "hst%d" % h, [128, 64]) for h in (0, 1)]
        hbf = [cx.sb(st, "hbf%d" % h, [128, 64], BF16) for h in (0, 1)]

        for d in (0, 1):
            S.op("pool", lambda E: E.memset(Sst[:], 0.0), writes=[k_S])
            S.op("pool", lambda E: E.memset(Sbf[:], 0.0), writes=[k_Sbf])
            for h in (0, 1):
                S.op("pool", lambda E: E.memset(hst[h][0][:], 0.0), writes=[hst[h][1]])
                S.op("pool", lambda E: E.memset(hbf[h][0][:], 0.0), writes=[hbf[h][1]])
            order = list(range(len(blocks))) if d == 0 else [0] + list(range(len(blocks) - 1, 0, -1))
            nxt = load_x(order[0])
            for oi, bi in enumerate(order):
                t0, NT, n = blocks[bi]
                nch = NT // 64
                xt, k_xt = nxt
                if oi + 1 < len(order):
                    nxt = load_x(order[oi + 1])
                uT, k_uT = uT_r.next()
                uk = emit_norm_T(cx, xt, k_xt, NT // 128, Gmix, k_Gmix, sh_mix, k_modT, n, uT, k_uT, ident, k_id, wkn)
                (xT, k_xT), (BT, k_BT), (CT, k_CT) = xT_r.next(), BT_r.next(), CT_r.next()
                S.dma("sp", xT[:, :NT], xc[0, :, t0:t0 + NT], writes=[k_xT])
                S.dma("sp", BT[:, :NT], xc[1, :, t0:t0 + NT], writes=[k_BT])
                S.dma("sp", CT[:, :NT], xc[2, :, t0:t0 + NT], writes=[k_CT])

                def fm(c0, wt_=None, wkeys=None):
                    ps_, k_ps = pbig.next()
                    if wt_ is None:
                        fns = [lambda E, kc=kc: E.matmul(ps_[:, :NT], lhsT=w_sb[:, kc, c0:c0 + 128], rhs=uT[:, kc, :NT],
                                                          start=(kc == 0), stop=(kc == 7)) for kc in range(8)]
                        S.group("pe", fns, reads=wk_rec + uk, writes=[k_ps])
                    else:
                        fns = [lambda E, kc=kc: E.matmul(ps_[:, :NT], lhsT=wdt[:, kc, c0, :], rhs=uT[:, kc, :NT],
                                                          start=(kc == 0), stop=(kc == 7)) for kc in range(8)]
                        S.group("pe", fns, reads=wk_dt + uk, writes=[k_ps])
                    return ps_, k_ps

                v3 = lambda ap: ap[:, :NT].rearrange("p (c t) -> p c t", t=64)
                (f_, k_f), (lf, k_lf), (k_, k_k), (bc, k_bc), (A_, k_A), (dA, k_dA), (e_, k_e), (q_, k_q) = (
                    r.next() for r in (f_t, lf_t, k_t, bc_t, A_t, dA_t, e_t, q_t))
                (qtl, k_qtl), (ktl, k_ktl), (qh, k_qh) = qtl_t.next(), ktl_t.next(), qh_t.next()
                (ksc, k_ksc), (dtot, k_dtot) = ksc_t.next(), dtot_t.next()
                ps_, k_ps = fm(C_FF if d == 0 else C_FB)
                S.op("act", lambda E: E.activation(out=f_[:, :NT], in_=ps_[:, :NT], func=AF.Sigmoid), reads=[k_ps], writes=[k_f])
                S.op("dve", lambda E: E.tensor_scalar(out=f_[:, :NT], in0=f_[:, :NT], scalar1=blt[:, 5 + 2 * d:6 + 2 * d],
                                                      scalar2=blt[:, 4 + 2 * d:5 + 2 * d], op0=ALU.mult, op1=ALU.add),
                     reads=[k_f, k_bl], writes=[k_f])
                S.op("act", lambda E: E.activation(out=lf[:, :NT], in_=f_[:, :NT], func=AF.Ln), reads=[k_f], writes=[k_lf])
                S.op("dve", lambda E: E.tensor_scalar(out=k_[:, :NT], in0=f_[:, :NT], scalar1=-1.0, scalar2=1.0, op0=ALU.mult,
                                                      op1=ALU.add), reads=[k_f], writes=[k_k])
                S.op("dve", lambda E: E.tensor_tensor_scan(out=bc[:, :NT], data0=m01[:, :NT], data1=lf[:, :NT], initial=0.0,
                                                           op0=ALU.mult, op1=ALU.add), reads=[k_m01, k_lf], writes=[k_bc])
                btot = v3(bc)[:, :, 63:64]
                if d == 0:
                    A3, k_Ax = v3(bc), k_bc
                    Af = bc
                else:
                    S.op("dve", lambda E: E.tensor_tensor(out=A_[:, :NT], in0=lf[:, :NT], in1=bc[:, :NT], op=ALU.subtract),
                         reads=[k_lf, k_bc], writes=[k_A])
                    S.op("dve", lambda E: E.tensor_tensor(out=v3(A_), in0=v3(A_), in1=btot.broadcast_to([128, nch, 64]), op=ALU.add),
                         reads=[k_A, k_bc], writes=[k_A])
                    A3, k_Ax = v3(A_), k_A
                    Af = A_
                S.op("dve", lambda E: E.tensor_tensor(out=v3(dA), in0=A3, in1=A3[:, :, 32:33].broadcast_to([128, nch, 64]),
                                                      op=ALU.subtract), reads=[k_Ax], writes=[k_dA])
                ps_, k_ps = fm(C_Q)
                S.op("act", lambda E: E.activation(out=q_[:, :NT], in_=ps_[:, :NT], func=AF.Silu), reads=[k_ps], writes=[k_q])
                S.op("act", lambda E: E.activation(out=e_[:, :NT], in_=dA[:, :NT], func=AF.Exp), reads=[k_dA], writes=[k_e])
                S.op("pool", lambda E: E.tensor_tensor(out=qtl[:, :NT], in0=q_[:, :NT], in1=e_[:, :NT], op=ALU.mult),
                     reads=[k_q, k_e], writes=[k_qtl])
                S.op("act", lambda E: E.activation(out=e_[:, :NT], in_=dA[:, :NT], func=AF.Exp, scale=-1.0), reads=[k_dA], writes=[k_e])
                S.op("pool", lambda E: E.tensor_tensor(out=ktl[:, :NT], in0=k_[:, :NT], in1=e_[:, :NT], op=ALU.mult),
                     reads=[k_k, k_e], writes=[k_ktl])
                S.op("act", lambda E: E.activation(out=e_[:, :NT], in_=Af[:, :NT], func=AF.Exp), reads=[k_Ax], writes=[k_e])
                S.op("pool", lambda E: E.tensor_tensor(out=qh[:, :NT], in0=q_[:, :NT], in1=e_[:, :NT], op=ALU.mult),
                     reads=[k_q, k_e], writes=[k_qh])
                S.op("dve", lambda E: E.tensor_tensor(out=ksc[:, 0:nch].unsqueeze(2), in0=btot, in1=A3[:, :, 32:33], op=ALU.subtract),
                     reads=[k_bc, k_Ax], writes=[k_ksc])
                S.op("act", lambda E: E.activation(out=ksc[:, 0:nch], in_=ksc[:, 0:nch], func=AF.Exp), reads=[k_ksc], writes=[k_ksc])
                S.op("act", lambda E: E.activation(out=dtot[:, 0:nch].unsqueeze(2), in_=btot, func=AF.Exp), reads=[k_bc], writes=[k_dtot])
                hq = []
                for h in (0, 1):
                    j = 2 * d + h
                    (dt, k_dt), (a_, k_a), (As, k_As), (eA, k_eA), (wd, k_wd), (et, k_et) = (
                        r.next() for r in (dt_t[h], a_t[h], As_t[h], eA_t[h], wd_t[h], et_t[h]))
                    ps_, k_ps = fm(j, wdt, wk_dt)
                    S.op("act", lambda E: E.activation(out=dt[:, :NT], in_=ps_[:, :NT], func=AF.Exp, bias=ad[:, 4 + j:5 + j]),
                         reads=[k_ps, k_dtb], writes=[k_dt])
                    S.op("act", lambda E: E.activation(out=dt[:, :NT], in_=dt[:, :NT], func=AF.Ln, bias=1.0), reads=[k_dt], writes=[k_dt])
                    S.op("dve", lambda E: E.tensor_scalar(out=a_[:, :NT], in0=dt[:, :NT], scalar1=ad[:, 8 + j:9 + j], scalar2=None,
                                                          op0=ALU.mult), reads=[k_dt, k_aneg], writes=[k_a])
                    S.op("dve", lambda E: E.tensor_tensor_scan(out=wd[:, :NT], data0=m01[:, :NT], data1=a_[:, :NT], initial=0.0,
                                                               op0=ALU.mult, op1=ALU.add), reads=[k_m01, k_a], writes=[k_wd])
                    atot = v3(wd)[:, :, 63:64]
                    if d == 0:
                        S.op("dve", lambda E: E.tensor_copy(out=As[:, :NT], in_=wd[:, :NT]), reads=[k_wd], writes=[k_As])
                    else:
                        S.op("dve", lambda E: E.tensor_tensor(out=As[:, :NT], in0=a_[:, :NT], in1=wd[:, :NT], op=ALU.subtract),
                             reads=[k_a, k_wd], writes=[k_As])
                        S.op("dve", lambda E: E.tensor_tensor(out=v3(As), in0=v3(As), in1=atot.broadcast_to([128, nch, 64]), op=ALU.add),
                             reads=[k_As, k_wd], writes=[k_As])
                    S.op("act", lambda E: E.activation(out=et[:, 0:nch].unsqueeze(2), in_=atot, func=AF.Exp), reads=[k_wd], writes=[k_et])
                    S.op("act", lambda E: E.activation(out=eA[:, :NT], in_=As[:, :NT], func=AF.Exp), reads=[k_As], writes=[k_eA])
                    S.op("dve", lambda E: E.tensor_tensor(out=v3(wd), in0=v3(As), in1=atot.broadcast_to([128, nch, 64]), op=ALU.subtract),
                         reads=[k_As, k_wd], writes=[k_wd])
                    S.op("act", lambda E: E.activation(out=wd[:, :NT], in_=wd[:, :NT], func=AF.Exp, scale=-1.0), reads=[k_wd], writes=[k_wd])
                    S.op("pool", lambda E: E.tensor_tensor(out=wd[:, :NT], in0=wd[:, :NT], in1=dt[:, :NT], op=ALU.mult),
                         reads=[k_wd, k_dt], writes=[k_wd])
                    hq.append(((dt, k_dt), (As, k_As), (eA, k_eA), (wd, k_wd), (et, k_et)))
                corder = list(range(nch)) if d == 0 else list(range(nch - 1, -1, -1))

                def front(c):
                    cs = slice(c * 64, (c + 1) * 64)
                    tok0 = t0 + c * 64
                    F = {"cs": cs, "tok0": tok0, "c": c}
                    (v_c, k_v), (xbk, k_xbk) = v_r.next(), xbk_r.next()
                    x_c, B_c, k_c = xbk[:, 0, :], xbk[:, 1, :], xbk[:, 2, :]
                    k_x = k_B = k_kc = k_xbk
                    F.update(v=(v_c, k_v), x=(x_c, k_x), B=(B_c, k_B), kc=(k_c, k_kc))
                    ps_, k_ps = psm.next()
                    NW = 384 if d == 0 else 128
                    fns = [lambda E, kc=kc: E.matmul(ps_[0:64, 0:NW], lhsT=uT[:, kc, cs], rhs=w_sb[:, kc, C_I:C_I + NW],
                                                      start=(kc == 0), stop=(kc == 7)) for kc in range(8)]
                    S.group("pe", fns, reads=wk_rec + uk, writes=[k_ps])
                    S.op("act", lambda E: E.copy(out=v_c[:], in_=ps_[0:64, 0:128]), reads=[k_ps], writes=[k_v])
                    if d == 0:
                        gz_c, k_gz = gz_r.next()
                        S.op("act", lambda E: E.copy(out=gz_c[:], in_=ps_[0:64, 128:384]), reads=[k_ps], writes=[k_gz])
                        S.dma("sp", gz_d[tok0:tok0 + 64, :], gz_c[:], reads=[k_gz], writes=[], semkey="st_" + k_gz)
                    ptr, k_ptr = wkn["ptr"].next()
                    fns = [lambda E, src=src, i=i: E.transpose(out=ptr[0:64, i, :], in_=src[:, cs], identity=ident[:])
                           for i, src in enumerate((xT, BT, ktl))]
                    S.group("pe", fns, reads=[k_xT, k_BT, k_ktl, k_id], writes=[k_ptr])
                    S.op("dve", lambda E: E.tensor_copy(out=xbk[:], in_=ptr[0:64, 0:3, :]), reads=[k_ptr], writes=[k_xbk])
                    cols, k_cols = cols_r.next()
                    F["cols"] = (cols, k_cols)
                    fns = []
                    rk = []
                    for h in (0, 1):
                        (dt, k_dt), (As, k_As), (eA, k_eA), (wd, k_wd), (et, k_et) = hq[h]
                        for qi, (src, k_src) in enumerate(((As, k_As), (dt, k_dt), (wd, k_wd), (eA, k_eA))):
                            fns.append(lambda E, src=src, qi=qi, h=h: E.transpose(out=ptq[0][0:64, (h * 4 + qi) * 64:(h * 4 + qi) * 64 + 64],
                                                                                   in_=src[0:64, cs], identity=identf[0:64, 0:64]))
                            rk.append(k_src)
                    S.group("pe", fns, reads=rk + [k_identf], writes=[ptq[1]])
                    S.op("dve", lambda E: E.tensor_copy(out=cols[:].unsqueeze(2),
                                                        in_=ptq[0][0:64, :].rearrange("p (q t) -> p q t", t=64)[:, :, 0:1]),
                         reads=[ptq[1]], writes=[k_cols])
                    ps_a, k_psa = psm.next()
                    S.op("pe", lambda E: E.matmul(ps_a[0:64, 0:64], lhsT=ktl[:, cs], rhs=qtl[:, cs], start=True, stop=True),
                         reads=[k_ktl, k_qtl], writes=[k_psa])
                    attm, k_attm = attm_r.next()
                    F["attm"] = (attm, k_attm)
                    S.op("dve", lambda E: E.tensor_tensor(out=attm[:], in0=ps_a[0:64, 0:64], in1=mk[:, d, :], op=ALU.mult),
                         reads=[k_psa, k_mk], writes=[k_attm])
                    ps_cb, k_pscb = psm.next()
                    S.op("pe", lambda E: E.matmul(ps_cb[0:64, 0:64], lhsT=BT[:, cs], rhs=CT[:, cs], start=True, stop=True),
                         reads=[k_BT, k_CT], writes=[k_pscb])
                    cb_c, k_cbc = cb_r.next()
                    S.op("act", lambda E: E.copy(out=cb_c[:], in_=ps_cb[0:64, 0:64]), reads=[k_pscb], writes=[k_cbc])
                    F["wt"], F["xw"] = [], []
                    for h in (0, 1):
                        (dt, k_dt), (As, k_As), (eA, k_eA), (wd, k_wd), (et, k_et) = hq[h]
                        cA, cdt, cwd, ceA = (cols[:, h * 4 + i:h * 4 + i + 1] for i in range(4))
                        dm, k_dm = dm_r.next()
                        S.op("dve", lambda E: E.scalar_tensor_tensor(out=dm[:], in0=As[0:64, cs], scalar=cA, in1=mk[:, 2 + d, :],
                                                                     op0=ALU.subtract, op1=ALU.add),
                             reads=[k_As, k_cols, k_mk], writes=[k_dm])
                        dec, k_dec = dec_r.next()
                        S.op("act", lambda E: E.activation(out=dec[:], in_=dm[:], func=AF.Exp), reads=[k_dm], writes=[k_dec])
                        wt, k_wt = wt_r.next()
                        S.op("dve", lambda E: E.scalar_tensor_tensor(out=wt[:], in0=dec[:], scalar=cdt, in1=cb_c[:],
                                                                     op0=ALU.mult, op1=ALU.mult),
                             reads=[k_dec, k_cols, k_cbc], writes=[k_wt])
                        xw, k_xw = xw_r.next()
                        S.op("pool", lambda E: E.tensor_scalar(out=xw[:], in0=x_c[:, h * 64:(h + 1) * 64], scalar1=cwd, scalar2=None,
                                                               op0=ALU.mult), reads=[k_x, k_cols], writes=[k_xw])
                        F["wt"].append((wt, k_wt))
                        F["xw"].append((xw, k_xw))
                    return F

                def back(F):
                    cs, tok0, c = F["cs"], F["tok0"], F["c"]
                    (v_c, k_v), (x_c, k_x), (B_c, k_B), (k_c, k_kc) = F["v"], F["x"], F["B"], F["kc"]
                    cols, k_cols = F["cols"]
                    attm, k_attm = F["attm"]
                    ps_o, k_pso = psb.next()
                    S.group("pe", [lambda E: E.matmul(ps_o[0:64, 0:128], lhsT=attm[:], rhs=v_c[:], start=True, stop=False),
                                   lambda E: E.matmul(ps_o[0:64, 0:128], lhsT=qh[:, cs], rhs=Sbf[:], start=False, stop=True)],
                            reads=[k_attm, k_v, k_qh, k_Sbf], writes=[k_pso])
                    o_c, k_oc = oc_r.next()
                    S.op("act", lambda E: E.copy(out=o_c[:], in_=ps_o[0:64, 0:128]), reads=[k_pso], writes=[k_oc])
                    S.dma("sp", o_d[d][tok0:tok0 + 64, :], o_c[:], reads=[k_oc], writes=[], semkey="st_" + k_oc)
                    ps_s, k_pss = psb.next()
                    S.op("pe", lambda E: E.matmul(ps_s[:, 0:128], lhsT=k_c, rhs=v_c[:], start=True, stop=True),
                         reads=[k_kc, k_v], writes=[k_pss])
                    tS, k_tS = tS_r.next()
                    S.op("dve", lambda E: E.tensor_scalar(out=tS[:], in0=ps_s[:, 0:128], scalar1=ksc[:, c:c + 1], scalar2=None,
                                                          op0=ALU.mult), reads=[k_pss, k_ksc], writes=[k_tS])
                    S.op("dve", lambda E: E.scalar_tensor_tensor(out=Sst[:], in0=Sst[:], scalar=dtot[:, c:c + 1], in1=tS[:],
                                                                 op0=ALU.mult, op1=ALU.add), reads=[k_S, k_dtot, k_tS], writes=[k_S])
                    S.op("act", lambda E: E.copy(out=Sbf[:], in_=Sst[:]), reads=[k_S], writes=[k_Sbf])
                    y_c, k_yc = yc_r.next()
                    for h in (0, 1):
                        (dt, k_dt), (As, k_As), (eA, k_eA), (wd, k_wd), (et, k_et) = hq[h]
                        ceA = cols[:, h * 4 + 3:h * 4 + 4]
                        wt, k_wt = F["wt"][h]
                        xw, k_xw = F["xw"][h]
                        ps_y, k_psy = psb.next()
                        S.op("pe", lambda E: E.matmul(ps_y[0:64, 0:64], lhsT=wt[:], rhs=x_c[:, h * 64:(h + 1) * 64], start=True, stop=True),
                             reads=[k_wt, k_x], writes=[k_psy])
                        ytmp, k_ytmp = ytmp_r.next()
                        S.op("act", lambda E: E.copy(out=ytmp[:], in_=ps_y[0:64, 0:64]), reads=[k_psy], writes=[k_ytmp])
                        ps_yo, k_psyo = psb.next()
                        S.op("pe", lambda E: E.matmul(ps_yo[0:64, 0:64], lhsT=CT[:, cs], rhs=hbf[h][0][:], start=True, stop=True),
                             reads=[k_CT, hbf[h][1]], writes=[k_psyo])
                        S.op("dve", lambda E: E.scalar_tensor_tensor(out=y_c[:, h * 64:(h + 1) * 64], in0=ps_yo[0:64, 0:64], scalar=ceA,
                                                                     in1=ytmp[:], op0=ALU.mult, op1=ALU.add),
                             reads=[k_psyo, k_cols, k_ytmp], writes=[k_yc + "_%d" % h])
                        ps_st, k_psst = psb.next()
                        S.op("pe", lambda E: E.matmul(ps_st[:, 0:64], lhsT=B_c, rhs=xw[:], start=True, stop=True),
                             reads=[k_B, k_xw], writes=[k_psst])
                        S.op("dve", lambda E: E.scalar_tensor_tensor(out=hst[h][0][:], in0=hst[h][0][:], scalar=et[:, c:c + 1],
                                                                     in1=ps_st[:, 0:64], op0=ALU.mult, op1=ALU.add),
                             reads=[hst[h][1], k_et, k_psst], writes=[hst[h][1]])
                        S.op("act", lambda E: E.copy(out=hbf[h][0][:], in_=hst[h][0][:]), reads=[hst[h][1]], writes=[hbf[h][1]])
                    yks = [k_yc + "_0", k_yc + "_1"]
                    if d == 0:
                        xs_, k_xs = xs_r.next()
                        S.op("pool", lambda E: E.tensor_tensor(out=xs_[:], in0=x_c, in1=skb[:], op=ALU.mult),
                             reads=[k_x, k_skb], writes=[k_xs])
                        S.op("pool", lambda E: E.tensor_tensor(out=y_c[:], in0=y_c[:], in1=xs_[:], op=ALU.add),
                             reads=yks + [k_xs], writes=yks)
                    S.dma("sp", y_d[d][tok0:tok0 + 64, :], y_c[:], reads=yks, writes=[], semkey="st_" + k_yc)

                Fq = front(corder[0])
                for ci, c in enumerate(corder):
                    Fn = front(corder[ci + 1]) if ci + 1 < len(corder) else None
                    back(Fq)
                    Fq = Fn
            cx.barrier()

        ld = [T("m_of", [128, 128]), T("m_ob", [128, 128]), T("m_yf", [128, 128]), T("m_yb", [128, 128])]
        gzl = T("m_gz", [128, 256], BF16)
        osum, ysum = T("m_os", [128, 128]), T("m_ys", [128, 128])
        sq, st2 = T("m_sq", [128, 128]), T("m_st", [128, 2])
        outt = T("m_out", [128, 256], F32, 2)
        for ti in range(NK // 128):
            rows = slice(ti * 128, (ti + 1) * 128)
            tl = [r.next() for r in ld]
            for (t_, k_t), src in zip(tl, (o_d[0], o_d[1], y_d[0], y_d[1])):
                S.dma("sp", t_[:], src[rows, :], writes=[k_t])
            gzt, k_gzt = gzl.next()
            S.dma("sp", gzt[:], gz_d[rows, :], writes=[k_gzt])
            S.op("act", lambda E: E.activation(out=gzt[:], in_=gzt[:], func=AF.Silu), reads=[k_gzt], writes=[k_gzt])
            (os_, k_os), (ys_, k_ys), (sq_, k_sq), (s2, k_s2) = osum.next(), ysum.next(), sq.next(), st2.next()
            ot, k_ot = outt.next()
            S.op("dve", lambda E: E.tensor_tensor(out=os_[:], in0=tl[0][0][:], in1=tl[1][0][:], op=ALU.add),
                 reads=[tl[0][1], tl[1][1]], writes=[k_os])
            S.op("act", lambda E: E.activation(out=sq_[:], in_=os_[:], func=AF.Square, accum_out=s2[:, 0:1]), reads=[k_os], writes=[k_sq, k_s2])
            S.op("act", lambda E: E.activation(out=s2[:, 1:2], in_=s2[:, 0:1], func=AF.Sqrt, scale=1.0 / 128, bias=EPS), reads=[k_s2], writes=[k_s2])
            S.op("dve", lambda E: E.reciprocal(out=s2[:, 1:2], in_=s2[:, 1:2]), reads=[k_s2], writes=[k_s2])
            S.op("dve", lambda E: E.scalar_tensor_tensor(out=os_[:], in0=os_[:], scalar=s2[:, 1:2], in1=gzt[:, 0:128], op0=ALU.mult,
                                                         op1=ALU.mult), reads=[k_os, k_s2, k_gzt], writes=[k_os])
            S.op("pool", lambda E: E.tensor_tensor(out=ot[:, 0:128], in0=os_[:], in1=L["onb128"][0][:], op=ALU.mult),
                 reads=[k_os, L["onb128"][1]], writes=[k_ot + "a"])
            S.op("dve", lambda E: E.tensor_tensor(out=ys_[:], in0=tl[2][0][:], in1=tl[3][0][:], op=ALU.add),
                 reads=[tl[2][1], tl[3][1]], writes=[k_ys])
            S.op("pool", lambda E: E.tensor_tensor(out=ot[:, 128:256], in0=ys_[:], in1=gzt[:, 128:256], op=ALU.mult),
                 reads=[k_ys, k_gzt], writes=[k_ot + "b"])
            S.dma("sp", yb[rows, :], ot[:], reads=[k_ot + "a", k_ot + "b"], writes=[], semkey="st_" + k_ot)
        cx.barrier()


def host_consts_B():
    s = np.arange(64)[:, None]
    t = np.arange(64)[None, :]
    mul_f = (s <= t).astype(np.float32)
    mul_b = (s >= t).astype(np.float32)
    masks = np.stack([mul_f, mul_b, (mul_f - 1.0) * 30000.0, (mul_b - 1.0) * 30000.0], axis=1)
    m01 = np.ones((128, 512), np.float32)
    m01[:, ::64] = 0.0
    return {"masks": np.ascontiguousarray(masks, np.float32), "mask01": m01, "ident": np.eye(128, dtype=np.float32)}


def prep_B(inp, core, h_lat, h_ctx):
    b, g = core // 4, core % 4
    f = lambda a: np.ascontiguousarray(a, dtype=np.float32)
    W = inp["rec_w_in"][0]
    s0, s1, gg = 2 * g, 2 * g + 1, g // 2
    cols = np.concatenate([
        np.arange(0 + g * 128, 0 + (g + 1) * 128), np.arange(512 + g * 128, 512 + (g + 1) * 128),
        np.arange(1024 + g * 128, 1024 + (g + 1) * 128), np.arange(1536 + g * 128, 1536 + (g + 1) * 128),
        np.arange(2048 + g * 128, 2048 + (g + 1) * 128), np.arange(2560 + s0 * 64, 2560 + s0 * 64 + 128),
        np.arange(3072 + s0 * 64, 3072 + s0 * 64 + 128), np.arange(3584 + gg * 128, 3584 + (gg + 1) * 128),
        np.arange(3840 + gg * 128, 3840 + (gg + 1) * 128), np.array([4096 + s0, 4096 + s1, 4096 + 8 + s0, 4096 + 8 + s1])])
    bl_all = inp["hgrn_bound_logits"]
    bl = np.stack([bl_all[0, g * 128:(g + 1) * 128], bl_all[1, g * 128:(g + 1) * 128],
                   bl_all[0, 512 + g * 128:512 + (g + 1) * 128], bl_all[1, 512 + g * 128:512 + (g + 1) * 128]], axis=1)
    xch = [np.arange(s0 * 64, s0 * 64 + 128), 512 + np.arange(gg * 128, (gg + 1) * 128), 768 + np.arange(gg * 128, (gg + 1) * 128)]
    cwm = inp["ssd_conv_w"][0]
    cw = np.concatenate([cwm[:, ch].T for ch in xch], axis=1)
    cb = np.stack([inp["ssd_conv_b"][0][ch] for ch in xch], axis=1)
    al, db = inp["ssd_a_log"][0], inp["ssd_dt_bias"][0]
    m = {
        "hs": f(np.concatenate([h_ctx, h_lat], 0)),
        "c2": f(np.stack([inp["c"][b].reshape(8, 128).T, inp["c_ctx"].reshape(8, 128).T], axis=-1).reshape(128, 16)),
        "w_mod": f(inp["w_mod"][1][:, 0:2048]), "b_mod": f(inp["b_mod"][1][0:2048]), "norm_mix": f(inp["norm_mix"][1]),
        "w_rec": f(W[:, cols]), "bl": f(bl), "out_norm": f(inp["hgrn_out_norm"][0][g * 128:(g + 1) * 128]),
        "cw": f(cw), "cb": f(cb),
        "alog4": f(np.array([al[0, s0], al[0, s1], al[1, s0], al[1, s1]])),
        "dtb4": f(np.array([db[0, s0], db[0, s1], db[1, s0], db[1, s1]])),
        "skipb": f(np.repeat(inp["ssd_skip"][0][[s0, s1]], 64)),
    }
    m.update(host_consts_B())
    return m


SPEC_C = [("hs", (NQ, D)), ("Y", (NQ, D)), ("c2", (128, 16)), ("w_mod", (D, 4096)), ("b_mod", (4096,)),
          ("norm_mlp", (D,)), ("w_out", (D, D)), ("ssd_g", (512,)), ("w1", (D, 4 * D)), ("w2", (4 * D, D)),
          ("final_norm", (D,)), ("ident", (128, 128))]


def build_C():
    nc = bass.Bass("TRN2", target_bir_lowering=False)
    blob = Blob(nc, SPEC_C)
    out = nc.dram_tensor("out", [NQ, D], F32, kind="ExternalOutput").ap()
    with ExitStack() as es:
        cx = Ctx(nc, es)
        S = cx.S
        (modT, k_modT), gate = emit_mod(cx, es, blob.get("w_mod"), blob.get("b_mod"), blob.get("c2"),
                                        need=[2, 3, 4, 5], gates=[2, 5], i0=2)
        (Gmlp, k_Gmlp), sh_mlp = emit_gcols(cx, es, modT, k_modT, blob.get("norm_mlp"), 3, 4, "mlpc")
        identf, k_identf = cx.sb(es, "identf", [128, 128])
        ident, k_id = cx.sb(es, "ident", [128, 128], BF16)
        S.dma("sp", identf[:], blob.get("ident"), writes=[k_identf])
        S.op("dve", lambda E: E.tensor_copy(out=ident[:], in_=identf[:]), reads=[k_identf], writes=[k_id])
        cx.barrier()
        emit_outproj_mlp(cx, [(None, 8, 128, 0)], blob.get("w_out"), blob.get("hs"), gate, Gmlp, k_Gmlp, sh_mlp, k_modT,
                         blob.get("w1"), blob.get("w2"), out, ident, k_id, final_norm=blob.get("final_norm"),
                         ytok=(blob.get("Y"), blob.get("ssd_g")))
    return nc


def prep_C(inp, core, h_own, h_ctx, Yrows):
    b = core // 4
    f = lambda a: np.ascontiguousarray(a, dtype=np.float32)
    return {
        "hs": f(np.concatenate([h_ctx, h_own], 0)), "Y": f(Yrows),
        "c2": f(np.stack([inp["c"][b].reshape(8, 128).T, inp["c_ctx"].reshape(8, 128).T], axis=-1).reshape(128, 16)),
        "w_mod": f(inp["w_mod"][1][:, 2048:]), "b_mod": f(inp["b_mod"][1][2048:]), "norm_mlp": f(inp["norm_mlp"][1]),
        "w_out": f(inp["rec_w_out"][0]), "ssd_g": f(inp["ssd_norm"][0]), "w1": f(inp["w_mlp_in"][1]),
        "w2": f(inp["w_mlp_out"][1]), "final_norm": f(inp["final_norm"]), "ident": np.eye(128, dtype=np.float32),
    }


_NC_CACHE = {}


def _get_nc(name):
    if name not in _NC_CACHE:
        _NC_CACHE[name] = {"A": build_A, "B": build_B, "C": build_C}[name]()
    return _NC_CACHE[name]


def kernel(**inp):
    inp = {k: np.asarray(v) for k, v in inp.items()}
    cores = list(range(8))
    mapsA = [{"blob": Blob.pack(SPEC_A, prep_A(inp, c))} for c in cores]
    resA = run_bass_kernel_spmd(_get_nc("A"), mapsA, core_ids=cores).results
    hA = [np.asarray(r["h_out"], dtype=np.float32) for r in resA]
    del mapsA
    h_lat = [np.concatenate([hA[b * 4 + q][NCTX:] for q in range(4)], 0) for b in range(2)]
    h_ctx = [hA[b * 4][:NCTX] for b in range(2)]
    sB = spec_B()
    mapsB = [{"blob": Blob.pack(sB, prep_B(inp, c, h_lat[c // 4], h_ctx[c // 4]))} for c in cores]
    resB = run_bass_kernel_spmd(_get_nc("B"), mapsB, core_ids=cores).results
    del mapsB
    Y = []
    for b in range(2):
        Yb = np.empty((NKEY, D), np.float32)
        for g in range(4):
            yb = np.asarray(resB[b * 4 + g]["yb"], dtype=np.float32)
            Yb[:, g * 128:(g + 1) * 128] = yb[:, 0:128]
            Yb[:, 512 + g * 128:512 + (g + 1) * 128] = yb[:, 128:256]
        Y.append(Yb)
    mapsC = []
    for c in cores:
        b, q = c // 4, c % 4
        rows = np.concatenate([Y[b][:NCTX], Y[b][NCTX + q * NOWN:NCTX + (q + 1) * NOWN]], 0)
        mapsC.append({"blob": Blob.pack(SPEC_C, prep_C(inp, c, hA[c][NCTX:], h_ctx[b], rows))})
    resC = run_bass_kernel_spmd(_get_nc("C"), mapsC, core_ids=cores).results
    out = np.empty((2, NLAT, D), np.float32)
    for c in cores:
        b, q = c // 4, c % 4
        out[b, q * NOWN:(q + 1) * NOWN] = np.asarray(resC[c]["out"], dtype=np.float32)[NCTX:]
    return out
```
